# Optimizing a Trainium2 kernel written in Bass

```python
import jax, jax.numpy as jnp
from jax import lax
import numpy as np

D_MODEL = 1024
BATCH = 2
SEQ = 8192
DEPTH = 2
DEC_BATCH = 32
DEC_SEQ = 4
PAST_LEN = 8192
PAGE_SIZE = 128

HEAD_DIM = 64
N_HEADS = D_MODEL // HEAD_DIM
N_KV = 4
HPG = N_HEADS // N_KV
Q_W = N_HEADS * HEAD_DIM
KV_W = N_KV * HEAD_DIM
NSA_SPLITS = tuple(Q_W + i * KV_W for i in range(7))
D_FF = ((8 * D_MODEL // 3 + 127) // 128) * 128
PLE_DIM = 256
CONV_W = 3
CMP_BLK = 64
SEL_BLK = 64
TOP_N = 16
WINDOW = 512
CMP_HID = 2 * HEAD_DIM
QBLK = 128
ROPE_THETA = 10000.0
RMS_EPS = 1e-6
FORCE_BONUS = 1e4
N_CONV = (DEPTH + 1) // 2
N_NSA = DEPTH // 2

kernel_name = 'hybrid_conv_nsa_macaron_step'


def rms_norm(x, g):
    x32 = x.astype(jnp.float32)
    y = x32 * lax.rsqrt(jnp.mean(x32 * x32, axis=-1, keepdims=True) + RMS_EPS)
    return (y * g.astype(jnp.float32)).astype(x.dtype)


def swiglu(h, w_gu, w_down):
    g, u = jnp.split(h @ w_gu, 2, axis=-1)
    return (jax.nn.silu(g) * u) @ w_down


def rope(x, pos):
    half = HEAD_DIM // 2
    inv = ROPE_THETA ** (-jnp.arange(half, dtype=jnp.float32) / half)
    ang = pos.astype(jnp.float32)[:, None] * inv
    ang = ang.reshape((pos.shape[0],) + (1,) * (x.ndim - 3) + (half,))
    cos, sin = jnp.cos(ang), jnp.sin(ang)
    x1 = x[..., :half].astype(jnp.float32)
    x2 = x[..., half:].astype(jnp.float32)
    return jnp.concatenate([x1 * cos - x2 * sin, x2 * cos + x1 * sin], axis=-1).astype(x.dtype)


def masked_softmax(s, mask):
    s = jnp.where(mask, s.astype(jnp.float32), -jnp.inf)
    m = jnp.max(s, axis=-1, keepdims=True)
    m = jnp.where(jnp.isfinite(m), m, 0.0)
    e = jnp.exp(s - m)
    return e / jnp.maximum(jnp.sum(e, axis=-1, keepdims=True), 1e-30)


def conv_mixer(h, prev, w_in, w_conv, w_out):
    t = h.shape[1]
    b_gate, c_gate, v = jnp.split(h @ w_in, 3, axis=-1)
    u = jnp.concatenate([prev.astype(h.dtype), c_gate * v], axis=1)
    y = sum(w_conv[j] * u[:, j:j + t] for j in range(CONV_W))
    return (b_gate * y) @ w_out, u[:, -(CONV_W - 1):]


def nsa_project(h, pos, w_in, qk_g):
    b, t, _ = h.shape
    q, kc, vc, ks, vs, kw, vw, gt = jnp.split(h @ w_in, NSA_SPLITS, axis=-1)
    heads = lambda a: a.reshape(b, t, N_KV, HEAD_DIM)
    q = rms_norm(q.reshape(b, t, N_KV, HPG, HEAD_DIM), qk_g[0])
    k_sel = rope(rms_norm(heads(ks), qk_g[2]), pos)
    k_win = rope(rms_norm(heads(kw), qk_g[3]), pos)
    gates = jax.nn.sigmoid(gt.reshape(b, t, N_KV, HPG, 3))
    return q, rope(q, pos), heads(kc), heads(vc), k_sel, heads(vs), k_win, heads(vw), gates


def compress(rows, pos_emb, w1, w2):
    b, l = rows.shape[:2]
    nb = l // CMP_BLK
    blk = rows[:, :nb * CMP_BLK].reshape(b, nb, CMP_BLK, N_KV, HEAD_DIM) + pos_emb[:, None, :]
    blk = blk.transpose(0, 1, 3, 2, 4).reshape(b, nb, N_KV, CMP_BLK * HEAD_DIM)
    return jax.nn.gelu(blk @ w1) @ w2


def to_blocks(rows):
    b, l = rows.shape[:2]
    nsb = -(-l // SEL_BLK)
    rows = jnp.pad(rows, ((0, 0), (0, nsb * SEL_BLK - l), (0, 0), (0, 0)))
    return rows.reshape(b, nsb, SEL_BLK, N_KV, HEAD_DIM).transpose(0, 3, 1, 2, 4)


def nsa_branches(q_nope, q_rope, gates, q_pos, kc, vc, ks_blk, vs_blk, kw, vw, kw_pos):
    scale = HEAD_DIM ** -0.5
    nb = kc.shape[1]
    nsb = ks_blk.shape[2]
    n_sel = min(TOP_N, nsb)
    s = jnp.einsum('bqghd,bngd->bghqn', q_nope, kc) * scale
    cmp_end = jnp.arange(nb) * CMP_BLK + (CMP_BLK - 1)
    p_cmp = masked_softmax(s, cmp_end[None, :] <= q_pos[:, None])
    o_cmp = jnp.einsum('bghqn,bngd->bqghd', p_cmp.astype(vc.dtype), vc)
    imp = jnp.pad(p_cmp.sum(axis=2), ((0, 0), (0, 0), (0, 0), (0, nsb - nb)))
    blk = jnp.arange(nsb)[None, :]
    cur = (q_pos // SEL_BLK)[:, None]
    forced = (blk == 0) | (blk == cur) | (blk == cur - 1)
    score = jnp.where(blk > cur, -jnp.inf, imp + jnp.where(forced, FORCE_BONUS, 0.0))
    top_s, idx = lax.top_k(score, n_sel)
    gather = jax.vmap(jax.vmap(lambda blocks, ix: blocks[ix]))
    lead = idx.shape[:3]
    k_g = gather(ks_blk, idx).reshape(lead + (n_sel * SEL_BLK, HEAD_DIM))
    v_g = gather(vs_blk, idx).reshape(lead + (n_sel * SEL_BLK, HEAD_DIM))
    k_pos = (idx[..., None] * SEL_BLK + jnp.arange(SEL_BLK)).reshape(lead + (n_sel * SEL_BLK,))
    sel_mask = (k_pos <= q_pos[:, None]) & jnp.repeat(jnp.isfinite(top_s), SEL_BLK, axis=-1)
    s = jnp.einsum('bqghd,bgqkd->bghqk', q_rope, k_g) * scale
    p_sel = masked_softmax(s, sel_mask[:, :, None])
    o_sel = jnp.einsum('bghqk,bgqkd->bqghd', p_sel.astype(v_g.dtype), v_g)
    diff = q_pos[:, None] - kw_pos[None, :]
    win_mask = (diff >= 0) & (diff < WINDOW) & (kw_pos >= 0)[None, :]
    s = jnp.einsum('bqghd,bkgd->bghqk', q_rope, kw) * scale
    p_win = masked_softmax(s, win_mask)
    o_win = jnp.einsum('bghqk,bkgd->bqghd', p_win.astype(vw.dtype), vw)
    return gates[..., 0:1] * o_cmp + gates[..., 1:2] * o_sel + gates[..., 2:3] * o_win


def nsa_prompt(h, w_in, qk_g, cmp_pos, cmp_w1, cmp_w2, w_out):
    b, t, _ = h.shape
    pos = jnp.arange(t)
    q_nope, q_rope, kc_r, vc_r, ks, vs, kw, vw, gates = nsa_project(h, pos, w_in, qk_g)
    kc = rms_norm(compress(kc_r, cmp_pos[0], cmp_w1[0], cmp_w2[0]), qk_g[1])
    vc = compress(vc_r, cmp_pos[1], cmp_w1[1], cmp_w2[1])
    ks_blk, vs_blk = to_blocks(ks), to_blocks(vs)
    pad = ((0, 0), (WINDOW, 0), (0, 0), (0, 0))
    kw_pad, vw_pad = jnp.pad(kw, pad), jnp.pad(vw, pad)

    def query_block(i):
        start = i * QBLK
        sl = lambda a, n: lax.dynamic_slice_in_dim(a, start, n, axis=1)
        q_pos = start + jnp.arange(QBLK)
        kw_pos = start - WINDOW + jnp.arange(QBLK + WINDOW)
        return nsa_branches(sl(q_nope, QBLK), sl(q_rope, QBLK), sl(gates, QBLK), q_pos, kc, vc,
                            ks_blk, vs_blk, sl(kw_pad, QBLK + WINDOW), sl(vw_pad, QBLK + WINDOW), kw_pos)

    o = lax.map(query_block, jnp.arange(t // QBLK))
    o = o.transpose(1, 0, 2, 3, 4, 5).reshape(b, t, Q_W)
    keep = min(WINDOW, t)
    return o @ w_out, (kc_r, vc_r, ks, vs, kw[:, -keep:], vw[:, -keep:])


def nsa_sample(h, k_cmp_pool, v_cmp_pool, k_sel_pool, v_sel_pool, k_win_buf, v_win_buf, page_table,
               w_in, qk_g, cmp_pos, cmp_w1, cmp_w2, w_out):
    b, t, _ = h.shape
    pos = PAST_LEN + jnp.arange(t)
    q_nope, q_rope, kc_r, vc_r, ks, vs, kw, vw, gates = nsa_project(h, pos, w_in, qk_g)
    gather_past = lambda pool: pool[page_table].reshape(b, -1, N_KV, HEAD_DIM)
    cat = lambda pool, new: jnp.concatenate([gather_past(pool).astype(new.dtype), new], axis=1)
    kc = rms_norm(compress(cat(k_cmp_pool, kc_r), cmp_pos[0], cmp_w1[0], cmp_w2[0]), qk_g[1])
    vc = compress(cat(v_cmp_pool, vc_r), cmp_pos[1], cmp_w1[1], cmp_w2[1])
    ks_blk, vs_blk = to_blocks(cat(k_sel_pool, ks)), to_blocks(cat(v_sel_pool, vs))
    w_buf = k_win_buf.shape[1]
    kw_all = jnp.concatenate([k_win_buf.astype(kw.dtype), kw], axis=1)
    vw_all = jnp.concatenate([v_win_buf.astype(vw.dtype), vw], axis=1)
    kw_pos = PAST_LEN - w_buf + jnp.arange(w_buf + t)
    o = nsa_branches(q_nope, q_rope, gates, pos, kc, vc, ks_blk, vs_blk, kw_all, vw_all, kw_pos)
    return o.reshape(b, t, Q_W) @ w_out, (kc_r, vc_r, ks, vs, kw_all[:, -w_buf:], vw_all[:, -w_buf:])


def macaron_layer(x, p, norm_g, w_gu, w_down, w_ple_proj, w_ple_gate, mixer):
    x = x + 0.5 * swiglu(rms_norm(x, norm_g[0]), w_gu[0], w_down[0])
    m, state = mixer(rms_norm(x, norm_g[1]))
    x = x + m
    x = x + 0.5 * swiglu(rms_norm(x, norm_g[2]), w_gu[1], w_down[1])
    x = x + jax.nn.sigmoid(rms_norm(x, norm_g[3]) @ w_ple_gate) * (p @ w_ple_proj)
    return x, state


def setup_inputs(seed: int = 0) -> dict:
    key = jax.random.key(seed)
    k = jax.random.split(key, 26)
    f32 = jnp.float32

    def nrm(kk, shape, scale=1.0):
        return scale * jax.random.normal(kk, shape, f32)

    n_pages = PAST_LEN // PAGE_SIZE
    n_used = DEC_BATCH * n_pages
    n_pool = n_used + max(1, n_used // 4)
    w_buf = min(WINDOW, PAST_LEN)
    pool = (N_NSA, n_pool, PAGE_SIZE, N_KV, HEAD_DIM)
    buf = (N_NSA, DEC_BATCH, w_buf, N_KV, HEAD_DIM)
    page_table = jax.random.permutation(k[9], n_pool)[:n_used].reshape(DEC_BATCH, n_pages).astype(jnp.int32)
    return {
        'x_prompt': nrm(k[0], (BATCH, SEQ, D_MODEL)),
        'x_sample': nrm(k[1], (DEC_BATCH, DEC_SEQ, D_MODEL)),
        'state_conv': nrm(k[2], (N_CONV, DEC_BATCH, CONV_W - 1, D_MODEL)),
        'cache_k_cmp': nrm(k[3], pool),
        'cache_v_cmp': nrm(k[4], pool),
        'cache_k_sel': nrm(k[5], pool),
        'cache_v_sel': nrm(k[6], pool),
        'cache_k_win': nrm(k[7], buf),
        'cache_v_win': nrm(k[8], buf),
        'page_table': page_table,
        'p_prompt': nrm(k[10], (DEPTH, BATCH, SEQ, PLE_DIM)),
        'p_sample': nrm(k[11], (DEPTH, DEC_BATCH, DEC_SEQ, PLE_DIM)),
        'norm_g': 1.0 + nrm(k[12], (DEPTH, 4, D_MODEL), 0.05),
        'ffn_w_gu': nrm(k[13], (DEPTH, 2, D_MODEL, 2 * D_FF), D_MODEL ** -0.5),
        'ffn_w_down': nrm(k[14], (DEPTH, 2, D_FF, D_MODEL), D_FF ** -0.5),
        'ple_w_proj': nrm(k[15], (DEPTH, PLE_DIM, D_MODEL), PLE_DIM ** -0.5),
        'ple_w_gate': nrm(k[16], (DEPTH, D_MODEL, D_MODEL), D_MODEL ** -0.5),
        'conv_w_in': nrm(k[17], (N_CONV, D_MODEL, 3 * D_MODEL), D_MODEL ** -0.5),
        'conv_w': nrm(k[18], (N_CONV, CONV_W, D_MODEL), CONV_W ** -0.5),
        'conv_w_out': nrm(k[19], (N_CONV, D_MODEL, D_MODEL), D_MODEL ** -0.5),
        'nsa_w_in': nrm(k[20], (N_NSA, D_MODEL, Q_W + 6 * KV_W + 3 * N_HEADS), D_MODEL ** -0.5),
        'nsa_qk_g': 1.0 + nrm(k[21], (N_NSA, 4, HEAD_DIM), 0.05),
        'nsa_cmp_pos': nrm(k[22], (N_NSA, 2, CMP_BLK, HEAD_DIM), 0.1),
        'nsa_cmp_w1': nrm(k[23], (N_NSA, 2, CMP_BLK * HEAD_DIM, CMP_HID), (CMP_BLK * HEAD_DIM) ** -0.5),
        'nsa_cmp_w2': nrm(k[24], (N_NSA, 2, CMP_HID, HEAD_DIM), CMP_HID ** -0.5),
        'nsa_w_out': nrm(k[25], (N_NSA, Q_W, D_MODEL), Q_W ** -0.5),
    }


def reference(x_prompt, x_sample, state_conv, cache_k_cmp, cache_v_cmp, cache_k_sel, cache_v_sel,
              cache_k_win, cache_v_win, page_table, p_prompt, p_sample, norm_g, ffn_w_gu, ffn_w_down,
              ple_w_proj, ple_w_gate, conv_w_in, conv_w, conv_w_out, nsa_w_in, nsa_qk_g, nsa_cmp_pos,
              nsa_cmp_w1, nsa_cmp_w2, nsa_w_out):
    conv_p_list, conv_s_list, nsa_p_list, nsa_s_list = [], [], [], []
    for i in range(DEPTH):
        j = i // 2
        lw = (norm_g[i], ffn_w_gu[i], ffn_w_down[i], ple_w_proj[i], ple_w_gate[i])
        if i % 2 == 0:
            cw = (conv_w_in[j], conv_w[j], conv_w_out[j])
            zero_prev = jnp.zeros((x_prompt.shape[0], CONV_W - 1, D_MODEL), x_prompt.dtype)
            x_prompt, st_p = macaron_layer(x_prompt, p_prompt[i], *lw,
                                           lambda h: conv_mixer(h, zero_prev, *cw))
            x_sample, st_s = macaron_layer(x_sample, p_sample[i], *lw,
                                           lambda h: conv_mixer(h, state_conv[j], *cw))
            conv_p_list.append(st_p)
            conv_s_list.append(st_s)
        else:
            nw = (nsa_w_in[j], nsa_qk_g[j], nsa_cmp_pos[j], nsa_cmp_w1[j], nsa_cmp_w2[j], nsa_w_out[j])
            x_prompt, st_p = macaron_layer(x_prompt, p_prompt[i], *lw, lambda h: nsa_prompt(h, *nw))
            x_sample, st_s = macaron_layer(
                x_sample, p_sample[i], *lw,
                lambda h: nsa_sample(h, cache_k_cmp[j], cache_v_cmp[j], cache_k_sel[j], cache_v_sel[j],
                                     cache_k_win[j], cache_v_win[j], page_table, *nw))
            nsa_p_list.append(st_p)
            nsa_s_list.append(st_s)
    conv_p = jnp.stack(conv_p_list)
    conv_s = jnp.stack(conv_s_list)
    k_cmp_p, v_cmp_p, k_sel_p, v_sel_p, k_win_p, v_win_p = [jnp.stack(a) for a in zip(*nsa_p_list)]
    k_cmp_s, v_cmp_s, k_sel_s, v_sel_s, k_win_s, v_win_s = [jnp.stack(a) for a in zip(*nsa_s_list)]
    return (x_prompt, x_sample, conv_p, conv_s, k_cmp_p, v_cmp_p, k_sel_p, v_sel_p, k_win_p, v_win_p,
            k_cmp_s, v_cmp_s, k_sel_s, v_sel_s, k_win_s, v_win_s)
```

```python
import numpy as np
import ml_dtypes
import concourse.bass as bass
import concourse.mybir as mybir
from concourse.bass_utils import run_bass_kernel_spmd

F32 = mybir.dt.float32
BF16 = mybir.dt.bfloat16
I32 = mybir.dt.int32
ALU = mybir.AluOpType
AF = mybir.ActivationFunctionType
AX = mybir.AxisListType

D = 1024
KC = 8
DFF = 2816
NCORE = 8
NT = 16
TW = 130
NSEQ = 4
SW = 6
PCOL = NT * TW
NCOL = PCOL + NSEQ * SW
CTS = [(0, 390), (390, 780), (780, 1170), (1170, 1560), (1560, 1950), (1950, NCOL)]
HGROUPS = [(0, 6), (6, 12), (12, 17), (17, 22)]
EPS = 1e-6


def bc(ap, pos, count):
    l = [list(x) for x in ap.ap]
    l.insert(pos, [0, count])
    return bass.AP(ap.tensor, ap.offset, l)


class TK:
    NDS = 24

    def __init__(self, nc, stack):
        self.nc = nc
        self.stack = stack
        self.eng = {'pe': nc.tensor, 'act': nc.scalar, 'dve': nc.vector, 'pool': nc.gpsimd, 'sp': nc.sync}
        self.sem = {}
        self.cnt = {}
        self.nsem = 0
        for e in self.eng:
            self._newsem(e)
        self.dsem = [stack.enter_context(nc.semaphore(f"dq{i}")) for i in range(self.NDS)]
        self.dcnt = [0] * self.NDS
        self.dnext = 0
        self.waited = {e: {} for e in self.eng}
        self.recs = {}
        self.sid = {}

    def _newsem(self, e):
        self.nsem += 1
        self.sem[e] = self.stack.enter_context(self.nc.semaphore(f"e{e}{self.nsem}"))
        self.cnt[e] = 0

    def _overlap(self, k):
        out = []
        g = self.recs.get(k[0])
        if g:
            n = len(k)
            for kk, rec in g.items():
                m = min(n, len(kk))
                if kk[:m] == k[:m]:
                    out.append((kk, rec))
        return out

    def _wait(self, e, deps):
        w = self.waited[e]
        best = {}
        for (sem, val) in deps:
            i = id(sem)
            if val > w.get(i, 0) and val > best.get(i, (None, 0))[1]:
                best[i] = (sem, val)
        for i, (sem, val) in best.items():
            self.eng[e].wait_ge(sem, val)
            w[i] = val

    def op(self, e, fn, reads=(), writes=(), dma=False, pg=(0, 128)):
        deps = []
        for k in reads:
            for kk, rec in self._overlap(k):
                if rec['w'] is not None:
                    deps.append(rec['w'])
        for k in writes:
            for kk, rec in self._overlap(k):
                if rec['w'] is not None:
                    deps.append(rec['w'])
                deps.extend(rec['r'].values())
        if e == 'pe':
            deps = [d for d in deps if d[0] is not self.sem['pe']]
        di = None
        if dma:
            di = self.dnext
            self.dnext = (di + 1) % self.NDS
            if self.dcnt[di] > 0:
                deps.append((self.dsem[di], 16 * self.dcnt[di]))
        if e == 'pe':
            if getattr(self, 'last_pg', pg) != pg and self.cnt['pe'] > 0:
                deps.append((self.sem['pe'], self.cnt['pe']))
            self.last_pg = pg
        self._wait(e, deps)
        ins = fn(self.eng[e])
        if dma:
            self.dcnt[di] += 1
            ins.then_inc(self.dsem[di], 16)
            tok = (self.dsem[di], 16 * self.dcnt[di])
        else:
            if self.cnt[e] >= 30000:
                self._newsem(e)
            self.cnt[e] += 1
            ins.then_inc(self.sem[e], 1)
            tok = (self.sem[e], self.cnt[e])
        for k in reads:
            g = self.recs.setdefault(k[0], {})
            rec = g.get(k)
            if rec is None:
                rec = {'w': None, 'r': {}}
                g[k] = rec
            rec['r'][id(tok[0])] = tok
        for k in writes:
            g = self.recs.setdefault(k[0], {})
            n = len(k)
            for kk in [kk for kk in g if len(kk) > n and kk[:n] == k]:
                del g[kk]
            g[k] = {'w': tok, 'r': {}}
        return tok

    def barrier(self):
        deps = []
        for g in self.recs.values():
            for rec in g.values():
                if rec['w'] is not None:
                    deps.append(rec['w'])
                deps.extend(rec['r'].values())
        for e in self.eng:
            self._wait(e, deps)
        self.recs = {}

    def wait_all(self, e):
        deps = []
        for g in self.recs.values():
            for rec in g.values():
                if rec['w'] is not None:
                    deps.append(rec['w'])
                deps.extend(rec['r'].values())
        self._wait(e, deps)


class Prog:
    def __init__(self, dbg=None):
        self.dbg = dbg
        self.nc = bass.Bass("TRN2", target_bir_lowering=False)
        self.ins = {}
        self.outs = {}

    def din(self, name, shape, dt=F32):
        t = self.nc.dram_tensor(name, list(shape), dt, kind="ExternalInput")
        self.ins[name] = t
        return t

    def dout(self, name, shape, dt=F32):
        t = self.nc.dram_tensor(name, list(shape), dt, kind="ExternalOutput")
        self.outs[name] = t
        return t


class Builder:
    def __init__(self, P, stack):
        self.P = P
        nc = self.nc = P.nc
        self.stack = stack
        self.tk = TK(nc, stack)
        sb = lambda name, shape, dt: stack.enter_context(nc.sbuf_tensor("s_" + name, list(shape), dt))
        self.sb = sb
        self.resid = sb("resid", [128, KC, NCOL], F32)
        self.xn = sb("xn", [128, KC, NCOL], BF16)
        self.onesb = sb("onesb", [128, 128], BF16)
        self.ngs = sb("ngs", [128, 2 * 4 * KC], F32)
        self.epsb = sb("epsb", [128, 1], F32)
        AR = 90112
        self.arena = sb("arena", [128, AR // 4], F32)

        def carve(off, shape, dt):
            n = 1
            for d in shape[1:]:
                n *= d
            esz = 4 if dt == F32 else 2
            assert off % 4 == 0 and off + n * esz <= AR, (off, shape)
            ap = self.arena[:, off // 4:(off + n * esz + 3) // 4]
            if dt != F32:
                ap = ap.bitcast(dt)
                ap = ap[:, 0:n]
            if len(shape) == 3:
                ap = ap.rearrange("p (a b) -> p a b", b=shape[2])
            elif len(shape) == 4:
                ap = ap.rearrange("p (a b c) -> p a b c", b=shape[2], c=shape[3])
            return ap
        self.carve = carve
        self.hid = carve(0, [128, KC, NCOL], BF16)
        self.wst = carve(33664, [128, 2, 2048], F32)
        self.wbf = carve(50048, [128, 3, 2048], BF16)
        self.sq = carve(62336, [128, KC, 390], BF16)
        self.tmp = carve(68576, [128, 2, 390], F32)
        self.rstd = carve(71696, [128, 390], F32)
        self.wdg = carve(73256, [128, 6, 1024], BF16)
        self.ub = carve(73256, [128, NCOL], F32)
        self.yb = carve(81672, [128, NCOL], F32)
        self.ps = [stack.enter_context(nc.psum_tensor(f"ps{i}", [128, 512], F32)) for i in range(8)]
        self.bank = 0
        self.wslot = 0
        self.sslot = 0
        self.tmpi = 0

    def nb(self):
        b = self.bank
        self.bank = (b + 1) % 8
        return b

    def ns(self):
        q = self.sslot
        self.sslot = (q + 1) % 2
        return q

    def nt(self):
        t = self.tmpi
        self.tmpi = (t + 1) % 2
        return t

    def load_w(self, src, n, g=None, m=None, parts=None):
        tk = self.tk
        s = self.wslot
        self.wslot = (s + 1) % 3
        q = self.ns()
        wst, wbf = self.wst, self.wbf
        tk.op('sp', lambda e: e.dma_start(out=wst[:, q, 0:n], in_=src), writes=[('wst', q)], dma=True)
        if parts is None:
            parts = [(0, n, g, m)]
        for (a, b, gg, mm) in parts:
            if gg is None:
                tk.op('dve', lambda e: e.tensor_copy(wbf[:, s, a:b], wst[:, q, a:b]),
                      reads=[('wst', q)], writes=[('wbf', s, a)])
            else:
                tk.op('dve', lambda e: e.tensor_tensor(
                    wbf[:, s, a:b].rearrange("p (c m) -> p c m", m=mm),
                    wst[:, q, a:b].rearrange("p (c m) -> p c m", m=mm),
                    bc(gg, 2, mm), ALU.mult),
                    reads=[('wst', q), ('ngs',)], writes=[('wbf', s, a)])
        return s

    def stream(self, blocks, body):
        pend = [self.load_w(*blocks[0])]
        for i in range(len(blocks)):
            if i + 1 < len(blocks):
                pend.append(self.load_w(*blocks[i + 1]))
            body(i, pend.pop(0))

    def gain(self, l, i):
        o = (l * 4 + i) * KC
        return self.ngs[:, o:o + KC]

    def norm(self):
        tk = self.tk
        resid, xn, sq, rstd, ps = self.resid, self.xn, self.sq, self.rstd, self.ps
        for ci, (a, b) in enumerate(CTS):
            n = b - a
            tk.op('act', lambda e: e.activation(sq[:, :, 0:n], resid[:, :, a:b], AF.Square),
                  reads=[('resid', ci)], writes=[('sq',)])
            bk = self.nb()
            for c in range(KC):
                tk.op('pe', lambda e: e.matmul(ps[bk][:, 0:n], self.onesb[:, :], sq[:, c, 0:n],
                                               start=(c == 0), stop=(c == KC - 1)),
                      reads=[('sq',), ('onesb',)], writes=[('ps', bk)])
            tk.op('act', lambda e: e.activation(rstd[:, 0:n], ps[bk][:, 0:n], AF.Ln, bias=self.epsb[:, :]),
                  reads=[('ps', bk), ('epsb',)], writes=[('rstd',)])
            tk.op('act', lambda e: e.activation(rstd[:, 0:n], rstd[:, 0:n], AF.Exp, scale=-0.5),
                  reads=[('rstd',)], writes=[('rstd',)])
            tk.op('pool', lambda e: e.tensor_tensor(xn[:, :, a:b], resid[:, :, a:b], bc(rstd[:, 0:n], 1, KC), ALU.mult),
                  reads=[('resid', ci), ('rstd',)], writes=[('xn', ci)])

    def ffn(self, l, f):
        tk = self.tk
        P = self.P
        resid, xn, hid, wbf, wdg, wst, tmp, ps = self.resid, self.xn, self.hid, self.wbf, self.wdg, self.wst, self.tmp, self.ps
        self.norm()
        g = self.gain(l, 2 * f)
        wgu = P.ins['wgu']
        wdn = P.ins['wdn']
        for (h0, h1) in HGROUPS:
            blocks = [(wgu[l, f, j], KC * 256, g, 256) for j in range(h0, h1)]

            def body(i, s, h0=h0):
                j = h0 + i
                for ci, (a, b) in enumerate(CTS):
                    n = b - a
                    bg, bu = self.nb(), self.nb()
                    for c in range(KC):
                        tk.op('pe', lambda e: e.matmul(ps[bg][:, 0:n], wbf[:, s, c * 256:c * 256 + 128], xn[:, c, a:b],
                                                       start=(c == 0), stop=(c == KC - 1)),
                              reads=[('wbf', s), ('xn', ci)], writes=[('ps', bg)])
                    for c in range(KC):
                        tk.op('pe', lambda e: e.matmul(ps[bu][:, 0:n], wbf[:, s, c * 256 + 128:c * 256 + 256], xn[:, c, a:b],
                                                       start=(c == 0), stop=(c == KC - 1)),
                              reads=[('wbf', s), ('xn', ci)], writes=[('ps', bu)])
                    t = self.nt()
                    tk.op('act', lambda e: e.activation(tmp[:, t, 0:n], ps[bg][:, 0:n], AF.Silu),
                          reads=[('ps', bg)], writes=[('tmp', t)])
                    tk.op('dve', lambda e: e.tensor_tensor(hid[:, i, a:b], tmp[:, t, 0:n], ps[bu][:, 0:n], ALU.mult),
                          reads=[('tmp', t), ('ps', bu)], writes=[('hid', i, ci)])
                s2 = self.ns()
                tk.op('sp', lambda e: e.dma_start(out=wst[:, s2, 0:1024], in_=wdn[l, f, j]), writes=[('wst', s2)], dma=True)
                tk.op('dve', lambda e: e.tensor_copy(wdg[:, i, :], wst[:, s2, 0:1024]),
                      reads=[('wst', s2)], writes=[('wdg', i)])

            self.stream(blocks, body)
            ng_ = h1 - h0
            for m in range(KC):
                for ci, (a, b) in enumerate(CTS):
                    n = b - a
                    bk = self.nb()
                    for i in range(ng_):
                        tk.op('pe', lambda e: e.matmul(ps[bk][:, 0:n], wdg[:, i, m * 128:(m + 1) * 128], hid[:, i, a:b],
                                                       start=(i == 0), stop=(i == ng_ - 1)),
                              reads=[('wdg', i), ('hid', i, ci)], writes=[('ps', bk)])
                    tk.op('dve', lambda e: e.scalar_tensor_tensor(resid[:, m, a:b], ps[bk][:, 0:n], 0.5, resid[:, m, a:b],
                                                                  ALU.mult, ALU.add),
                          reads=[('ps', bk), ('resid', ci, m)], writes=[('resid', ci, m)])

    def ple(self, l):
        tk = self.tk
        P = self.P
        resid, xn, hid, wbf, wst, tmp, ps = self.resid, self.xn, self.hid, self.wbf, self.wst, self.tmp, self.ps
        self.norm()
        g = self.gain(l, 3)
        pT = P.ins['pT']
        for ci, (a, b) in enumerate(CTS):
            n = b - a
            s = self.ns()
            tk.op('sp', lambda e: e.dma_start(out=wst[:, s, 0:2 * n].rearrange("p (c n) -> p c n", c=2), in_=pT[l, :, :, a:b]),
                  writes=[('wst', s)], dma=True)
            tk.op('dve', lambda e: e.tensor_copy(hid[:, 0:2, a:b], wst[:, s, 0:2 * n].rearrange("p (c n) -> p c n", c=2)),
                  reads=[('wst', s)], writes=[('hid', 0, ci), ('hid', 1, ci)])
        wpg = P.ins['wpg']
        blocks = [(wpg[l, m], 1280, None, None, [(0, 1024, g, 128), (1024, 1280, None, None)]) for m in range(KC)]

        def body(m, s):
            for ci, (a, b) in enumerate(CTS):
                n = b - a
                bg, bp = self.nb(), self.nb()
                for c in range(KC):
                    tk.op('pe', lambda e: e.matmul(ps[bg][:, 0:n], wbf[:, s, c * 128:(c + 1) * 128], xn[:, c, a:b],
                                                   start=(c == 0), stop=(c == KC - 1)),
                          reads=[('wbf', s), ('xn', ci)], writes=[('ps', bg)])
                for c in range(2):
                    tk.op('pe', lambda e: e.matmul(ps[bp][:, 0:n], wbf[:, s, 1024 + c * 128:1024 + (c + 1) * 128], hid[:, c, a:b],
                                                   start=(c == 0), stop=(c == 1)),
                          reads=[('wbf', s), ('hid', c, ci)], writes=[('ps', bp)])
                t = self.nt()
                tk.op('act', lambda e: e.activation(tmp[:, t, 0:n], ps[bg][:, 0:n], AF.Sigmoid),
                      reads=[('ps', bg)], writes=[('tmp', t)])
                tk.op('dve', lambda e: e.tensor_tensor(tmp[:, t, 0:n], tmp[:, t, 0:n], ps[bp][:, 0:n], ALU.mult),
                      reads=[('tmp', t), ('ps', bp)], writes=[('tmp', t)])
                tk.op('pool', lambda e: e.tensor_tensor(resid[:, m, a:b], resid[:, m, a:b], tmp[:, t, 0:n], ALU.add),
                      reads=[('tmp', t), ('resid', ci, m)], writes=[('resid', ci, m)])

        self.stream(blocks, body)

    def linear_resid(self, wtiles, g):
        tk = self.tk
        resid, hid, wbf, ps = self.resid, self.hid, self.wbf, self.ps
        blocks = [(wtiles[m], KC * 128, g, 128) for m in range(KC)]

        def body(m, s):
            for ci, (a, b) in enumerate(CTS):
                n = b - a
                bk = self.nb()
                for c in range(KC):
                    tk.op('pe', lambda e: e.matmul(ps[bk][:, 0:n], wbf[:, s, c * 128:(c + 1) * 128], hid[:, c, a:b],
                                                   start=(c == 0), stop=(c == KC - 1)),
                          reads=[('wbf', s), ('hid', c, ci)], writes=[('ps', bk)])
                tk.op('dve', lambda e: e.tensor_tensor(resid[:, m, a:b], resid[:, m, a:b], ps[bk][:, 0:n], ALU.add),
                      reads=[('ps', bk), ('resid', ci, m)], writes=[('resid', ci, m)])

        self.stream(blocks, body)

    def conv(self):
        tk = self.tk
        P = self.P
        resid, xn, hid, wbf, tmp, ps = self.resid, self.xn, self.hid, self.wbf, self.tmp, self.ps
        ub, yb, cws, hms, sts, cvo = self.ub, self.yb, self.cws, self.hms, self.sts, self.cvo
        self.norm()
        g = self.gain(0, 1)
        cwin = P.ins['cwin']
        blocks = [(cwin[i], KC * 128, g, 128) for i in range(24)]

        def mm(s, ci, a, b):
            n = b - a
            bk = self.nb()
            for c in range(KC):
                tk.op('pe', lambda e: e.matmul(ps[bk][:, 0:n], wbf[:, s, c * 128:(c + 1) * 128], xn[:, c, a:b],
                                               start=(c == 0), stop=(c == KC - 1)),
                      reads=[('wbf', s), ('xn', ci)], writes=[('ps', bk)])
            return bk

        def body(i, s):
            f, kind = i // 3, i % 3
            if kind == 0:
                for ci, (a, b) in enumerate(CTS):
                    bk = mm(s, ci, a, b)
                    tk.op('act', lambda e: e.activation(ub[:, a:b], ps[bk][:, 0:b - a], AF.Copy),
                          reads=[('ps', bk)], writes=[('ub', ci)])
            elif kind == 1:
                for ci, (a, b) in enumerate(CTS):
                    bk = mm(s, ci, a, b)
                    tk.op('dve', lambda e: e.tensor_tensor(ub[:, a:b], ub[:, a:b], ps[bk][:, 0:b - a], ALU.mult),
                          reads=[('ps', bk), ('ub', ci)], writes=[('ub', ci)])
                uh = ub[:, 0:PCOL].rearrange("p (k w) -> p k w", w=TW)[:, :, 0:2]
                tk.op('dve', lambda e: e.tensor_tensor(uh, uh, bc(hms[:, :], 2, 2), ALU.mult),
                      reads=[('ub',), ('hms',)], writes=[('ub',)])
                us = ub[:, PCOL:NCOL].rearrange("p (s w) -> p s w", w=SW)[:, :, 0:2]
                tk.op('dve', lambda e: e.tensor_copy(us, sts[:, f, :, :]),
                      reads=[('sts',)], writes=[('ub',)])
                tk.op('dve', lambda e: e.tensor_copy(cvo[:, f, 0:2], ub[:, PCOL - 2:PCOL]),
                      reads=[('ub',)], writes=[('cvo', f)])
                usn = ub[:, PCOL:NCOL].rearrange("p (s w) -> p s w", w=SW)[:, :, 4:6]
                tk.op('dve', lambda e: e.tensor_copy(cvo[:, f, 2:2 + 2 * NSEQ].rearrange("p (s w) -> p s w", w=2), usn),
                      reads=[('ub',)], writes=[('cvo', f)])
                n2 = NCOL - 2
                tk.op('dve', lambda e: e.tensor_scalar(yb[:, 2:NCOL], ub[:, 2:NCOL], cws[:, 2 * KC + f:2 * KC + f + 1], None, ALU.mult),
                      reads=[('ub',), ('cws',)], writes=[('yb',)])
                tk.op('dve', lambda e: e.scalar_tensor_tensor(yb[:, 2:NCOL], ub[:, 1:NCOL - 1], cws[:, KC + f:KC + f + 1], yb[:, 2:NCOL],
                                                               ALU.mult, ALU.add),
                      reads=[('ub',), ('cws',), ('yb',)], writes=[('yb',)])
                tk.op('dve', lambda e: e.scalar_tensor_tensor(yb[:, 2:NCOL], ub[:, 0:n2], cws[:, f:f + 1], yb[:, 2:NCOL],
                                                               ALU.mult, ALU.add),
                      reads=[('ub',), ('cws',), ('yb',)], writes=[('yb',)])
            else:
                for ci, (a, b) in enumerate(CTS):
                    bk = mm(s, ci, a, b)
                    tk.op('dve', lambda e: e.tensor_tensor(hid[:, f, a:b], yb[:, a:b], ps[bk][:, 0:b - a], ALU.mult),
                          reads=[('ps', bk), ('yb',)], writes=[('hid', f, ci)])

        self.stream(blocks, body)
        cwout = P.ins['cwout']
        self.linear_resid([cwout[m] for m in range(KC)], None)


    def nsa_proj(self):
        tk = self.tk
        P = self.P
        xn, hid, wst, tmp, ps, rstd = self.xn, self.hid, self.wst, self.tmp, self.ps, self.rstd
        self.norm()
        g = self.gain(1, 1)
        nwkv = P.ins['nwkv']
        NKV = 1584
        hf = hid[:, :, :].rearrange("p c n -> p (c n)")
        for c in range(KC):
            q = self.ns()
            tk.op('sp', lambda e: e.dma_start(out=wst[:, q, 0:NKV], in_=nwkv[:, c, :]), writes=[('wst', q)], dma=True)
            tk.op('dve', lambda e: e.tensor_scalar(hf[:, c * NKV:(c + 1) * NKV], wst[:, q, 0:NKV], g[:, c:c + 1], None, ALU.mult),
                  reads=[('wst', q), ('ngs',)], writes=[('hid',)])
        kvo = [self.ub, self.yb]
        kvk = [('ub',), ('yb',)]
        s1 = rstd[:, 0:128]
        s2 = rstd[:, 128:256]
        st4 = self.st4
        o_kvp, o_kvs = P.outs['o_kvp'], P.outs['o_kvs']
        for ti in range(NT + 1):
            if ti < NT:
                c0, R = ti * TW + 2, 128
            else:
                c0, R = PCOL, NSEQ * SW
            ko = kvo[ti % 2]
            kk = kvk[ti % 2]
            banks = []
            for (a, b) in [(0, 512), (512, 1024), (1024, 1536), (1536, NKV)]:
                bk = self.nb()
                banks.append(bk)
                for c in range(KC):
                    tk.op('pe', lambda e: e.matmul(ps[bk][0:R, 0:b - a], xn[:, c, c0:c0 + R], hf[:, c * NKV + a:c * NKV + b],
                                                   start=(c == 0), stop=(c == KC - 1)),
                          reads=[('hid',), ('xn',)], writes=[('ps', bk)])
            bA, bB, bC, bD = banks
            tk.op('act', lambda e: e.activation(ko[0:R, 0:512], ps[bA][0:R, 0:512], AF.Copy), reads=[('ps', bA)], writes=[kk + (0,)])
            tk.op('act', lambda e: e.activation(ko[0:R, 768:1024], ps[bB][0:R, 256:512], AF.Copy), reads=[('ps', bB)], writes=[kk + (3,)])
            tk.op('act', lambda e: e.activation(ko[0:R, 1280:1536], ps[bC][0:R, 256:512], AF.Copy), reads=[('ps', bC)], writes=[kk + (5,)])
            tk.op('act', lambda e: e.activation(self.gat[0:R, ti, :], ps[bD][0:R, 0:48], AF.Sigmoid), reads=[('ps', bD)], writes=[('gat', ti)])
            for (bk, gi, oc, part) in [(bB, 2, 512, 2), (bC, 3, 1024, 4)]:
                x = ps[bk][0:R, 0:256]
                x3 = x.rearrange("p (g d) -> p g d", d=64)
                tk.op('act', lambda e: e.activation(tmp[0:R, 0, 0:256], x, AF.Square), reads=[('ps', bk)], writes=[('tmp', 0)])
                tk.op('dve', lambda e: e.tensor_reduce(st4[0:R, 0:4], tmp[0:R, 0, 0:256].rearrange("p (g d) -> p g d", d=64), AX.X, ALU.add),
                      reads=[('tmp', 0)], writes=[('st4',)])
                tk.op('act', lambda e: e.activation(st4[0:R, 4:8], st4[0:R, 0:4], AF.Ln, bias=self.epsb[0:R, :], scale=1.0 / 64),
                      reads=[('st4',), ('epsb',)], writes=[('st4',)])
                tk.op('act', lambda e: e.activation(st4[0:R, 4:8], st4[0:R, 4:8], AF.Exp, scale=-0.5),
                      reads=[('st4',)], writes=[('st4',)])
                t1 = tmp[0:R, 1, 0:256].rearrange("p (g d) -> p g d", d=64)
                tk.op('dve', lambda e: e.tensor_tensor(t1, x3, bc(st4[0:R, 4:8], 2, 64), ALU.mult),
                      reads=[('ps', bk), ('st4',)], writes=[('tmp', 1)])
                tk.op('dve', lambda e: e.tensor_tensor(t1, t1, bc(self.qkg[0:R, gi, :], 1, 4), ALU.mult),
                      reads=[('tmp', 1), ('qkg',)], writes=[('tmp', 1)])
                x1, x2 = t1[:, :, 0:32], t1[:, :, 32:64]
                cosb = bc(self.cs[0:R, ti, 0:32], 1, 4)
                sinb = bc(self.cs[0:R, ti, 32:64], 1, 4)
                o3 = ko[0:R, oc:oc + 256].rearrange("p (g d) -> p g d", d=64)
                a1 = s1[0:R, :].rearrange("p (g d) -> p g d", d=32)
                a2 = s2[0:R, :].rearrange("p (g d) -> p g d", d=32)
                rd = [('tmp', 1), ('cs',)]
                tk.op('pool', lambda e: e.tensor_tensor(a1, x1, cosb, ALU.mult), reads=rd, writes=[('rstd', 0)])
                tk.op('pool', lambda e: e.tensor_tensor(a2, x2, sinb, ALU.mult), reads=rd, writes=[('rstd', 1)])
                tk.op('pool', lambda e: e.tensor_tensor(o3[:, :, 0:32], a1, a2, ALU.subtract), reads=[('rstd',)], writes=[kk + (part,)])
                tk.op('pool', lambda e: e.tensor_tensor(a1, x2, cosb, ALU.mult), reads=rd, writes=[('rstd', 0)])
                tk.op('pool', lambda e: e.tensor_tensor(a2, x1, sinb, ALU.mult), reads=rd, writes=[('rstd', 1)])
                tk.op('pool', lambda e: e.tensor_tensor(o3[:, :, 32:64], a1, a2, ALU.add), reads=[('rstd',)], writes=[kk + (part + 100,)])
            if ti < NT:
                tk.op('pool', lambda e: e.dma_start(out=o_kvp[ti, :, :], in_=ko[0:R, 0:1536]), reads=[kk], writes=[('o_kvp', ti)], dma=True)
            else:
                tk.op('pool', lambda e: e.dma_start(out=o_kvs[:, :], in_=ko[0:R, 0:1536]), reads=[kk], writes=[('o_kvs',)], dma=True)
                for sq_ in range(NSEQ):
                    r0 = sq_ * SW + 2
                    tk.op('pool', lambda e: e.dma_start(out=P.outs['o_kws'][sq_, 508:512, :], in_=ko[r0:r0 + 4, 1024:1280]),
                          reads=[kk], writes=[('o_kws', sq_)], dma=True)
                    tk.op('pool', lambda e: e.dma_start(out=P.outs['o_vws'][sq_, 508:512, :], in_=ko[r0:r0 + 4, 1280:1536]),
                          reads=[kk], writes=[('o_vws', sq_)], dma=True)


    def stage_cast(self, src, dst, n, eng, stg, in1=None):
        tk = self.tk
        q = self.ns()
        tk.op('sp', lambda e: e.dma_start(out=stg[:, q, 0:n], in_=src), writes=[('stg', q)], dma=True)
        return q

    def compress(self, cstop=9, slot=0, seq=None):
        tk = self.tk
        P = self.P
        ps = self.ps
        cv = self.carve
        w1b = cv(0, [128, 64, 128], BF16)
        kcb = cv(16384, [128, 64, 256], BF16)
        stg = cv(49152, [128, 2, 2048], F32)
        gel = cv(65536, [128, 512], BF16)
        wk = cv(66560, [128, 3, 512], F32)
        cmt = cv(72704, [128, 256], F32)
        w2b = cv(73728, [128, 2, 64], BF16)
        st4 = self.st4
        w1r, kcr, w2s = P.ins['w1r'], P.ins['kcr'], P.ins['w2s']
        q = self.ns()
        tk.op('sp', lambda e: e.dma_start(out=stg[:, q, 0:128], in_=w2s[:, :]), writes=[('stg', q)], dma=True)
        tk.op('dve', lambda e: e.tensor_copy(w2b[:, :, :].rearrange("p a b -> p (a b)"), stg[:, q, 0:128]), reads=[('stg', q)], writes=[('w2b',)])
        w1f = w1b[:, :, :].rearrange("p a b -> p (a b)")
        for kv in range(2):
            for pc in range(4):
                q = self.ns()
                tk.op('sp', lambda e: e.dma_start(out=stg[:, q, :], in_=w1r[kv, :, pc * 2048:(pc + 1) * 2048]), writes=[('stg', q)], dma=True)
                tk.op('dve', lambda e: e.tensor_copy(w1f[:, pc * 2048:(pc + 1) * 2048], stg[:, q, :]), reads=[('stg', q)], writes=[('w1b', pc)])
            if seq is None:
                for pc in range(8):
                    q = self.ns()
                    tk.op('sp', lambda e: e.dma_start(out=stg[:, q, :].rearrange("p (t c) -> p t c", c=256), in_=kcr[kv, :, pc * 8:(pc + 1) * 8, :]),
                          writes=[('stg', q)], dma=True)
                    tk.op('pool', lambda e: e.tensor_tensor(
                        kcb[:, pc * 8:(pc + 1) * 8, :].rearrange("p t (g d) -> p t g d", d=64),
                        stg[:, q, :].rearrange("p (t g d) -> p t g d", g=4, d=64),
                        bc(bc(self.posr[:, kv, :], 1, 4), 1, 8), ALU.add),
                        reads=[('stg', q), ('posr',)], writes=[('kcb', pc)])
            else:
                pool_d = P.ins['pool_kc' if kv == 0 else 'pool_vc']
                sg = stg[:, :, :].rearrange("p a (b c) -> p (a b) c", c=256)
                for page in range(64):
                    q = page % 16
                    col = seq * 64 + page
                    tk.op('pool', lambda e: e.indirect_dma_start(out=sg[:, q, :], out_offset=None, in_=pool_d[:, :],
                                                                 in_offset=bass.IndirectOffsetOnAxis(ap=self.idx[:, col:col + 1], axis=0)),
                          reads=[('idx',)], writes=[('stg', q // 8, q % 8)], dma=True)
                    tk.op('dve', lambda e: e.tensor_tensor(
                        kcb[:, page, :].rearrange("p (g d) -> p g d", d=64),
                        sg[:, q, :].rearrange("p (g d) -> p g d", d=64),
                        bc(self.posr[:, kv, :], 1, 4), ALU.add),
                        reads=[('stg', q // 8, q % 8), ('posr',)], writes=[('kcb', page)])
            if cstop <= 1:
                continue
            for e_ in range(2):
                for g in range(4):
                    po = ps[1][:, g * 128:(g + 1) * 128].rearrange("p (t e) -> p t e", e=2)[:, :, e_]
                    for d in range(64):
                        tk.op('pe', lambda e: e.matmul(po, w1b[64 * e_:64 * e_ + 64, d, :],
                                                       kcb[64 * e_:64 * e_ + 64, :, g * 64 + d], start=(d == 0), stop=(d == 63)),
                              reads=[('w1b',), ('kcb',)], writes=[('ps', 1)], pg=(64 * e_, 64))
            if cstop <= 2:
                continue
            x = ps[1][:, 0:512]
            tk.op('act', lambda e: e.activation(wk[:, 0, :], x, AF.Square), reads=[('ps', 1)], writes=[('wk', 0)])
            tk.op('dve', lambda e: e.tensor_scalar(wk[:, 0, :], wk[:, 0, :], 0.044715, 1.0, ALU.mult, ALU.add), reads=[('wk', 0)], writes=[('wk', 0)])
            tk.op('dve', lambda e: e.tensor_tensor(wk[:, 1, :], wk[:, 0, :], x, ALU.mult), reads=[('wk', 0), ('ps', 1)], writes=[('wk', 1)])
            tk.op('act', lambda e: e.activation(wk[:, 2, :], wk[:, 1, :], AF.Sigmoid, scale=1.5957691216057308), reads=[('wk', 1)], writes=[('wk', 2)])
            tk.op('dve', lambda e: e.tensor_tensor(gel[:, :], wk[:, 2, :], x, ALU.mult), reads=[('wk', 2), ('ps', 1)], writes=[('gel',)])
            if cstop <= 3:
                continue
            for g in range(4):
                tk.op('pe', lambda e: e.matmul(ps[2][:, g * 64:(g + 1) * 64], gel[:, g * 128:(g + 1) * 128], w2b[:, kv, :],
                                               start=True, stop=True),
                      reads=[('gel',), ('w2b',)], writes=[('ps', 2)])
            if cstop <= 4:
                continue
            y = ps[2][:, 0:256]
            if kv == 0:
                tk.op('act', lambda e: e.activation(wk[:, 0, 0:256], y, AF.Square), reads=[('ps', 2)], writes=[('wk', 0)])
                tk.op('dve', lambda e: e.tensor_reduce(st4[:, 0:4], wk[:, 0, 0:256].rearrange("p (g d) -> p g d", d=64), AX.X, ALU.add),
                      reads=[('wk', 0)], writes=[('st4',)])
                tk.op('act', lambda e: e.activation(st4[:, 4:8], st4[:, 0:4], AF.Ln, bias=self.epsb[:, :], scale=1.0 / 64),
                      reads=[('st4',), ('epsb',)], writes=[('st4',)])
                tk.op('act', lambda e: e.activation(st4[:, 4:8], st4[:, 4:8], AF.Exp, scale=-0.5), reads=[('st4',)], writes=[('st4',)])
                c3 = cmt[:, :].rearrange("p (g d) -> p g d", d=64)
                tk.op('dve', lambda e: e.tensor_tensor(c3, y.rearrange("p (g d) -> p g d", d=64), bc(st4[:, 4:8], 2, 64), ALU.mult),
                      reads=[('ps', 2), ('st4',)], writes=[('cmt',)])
                tk.op('dve', lambda e: e.tensor_tensor(c3, c3, bc(self.qkg[:, 1, :], 1, 4), ALU.mult), reads=[('cmt',), ('qkg',)], writes=[('cmt',)])
                for gp in range(2):
                    tk.op('pe', lambda e: e.transpose(ps[3][:, gp * 128:(gp + 1) * 128], cmt[:, gp * 128:(gp + 1) * 128], self.ident[:, :]),
                          reads=[('cmt',), ('ident',)], writes=[('ps', 3)])
                tk.op('act', lambda e: e.activation(self.kcT[:, :, :].rearrange("p a b -> p (a b)"), ps[3][:, 0:256], AF.Copy),
                      reads=[('ps', 3)], writes=[('kcT',)])
            else:
                tk.op('act', lambda e: e.activation(self.vcb[:, :], y, AF.Copy), reads=[('ps', 2)], writes=[('vcb',)])
        if cstop >= 9:
            tk.op('sp', lambda e: e.dma_start(out=self.d_kc[slot, :, :], in_=self.kcT[:, :, :].rearrange("p a b -> p (a b)")),
                  reads=[('kcT',)], writes=[('d_kc', slot)], dma=True)
            tk.op('sp', lambda e: e.dma_start(out=self.d_vc[slot, :, :], in_=self.vcb[:, :]), reads=[('vcb',)], writes=[('d_vc', slot)], dma=True)

    def qk_norm_rope(self, src, R, H, gi, ti, dst0, dst1, scr_sq, sm, a1, a2, scale, k0, k1, ka):
        tk = self.tk
        x3 = src.rearrange("p (g d) -> p g d", d=64)
        tk.op('act', lambda e: e.activation(scr_sq, src, AF.Square), reads=[ka], writes=[k1])
        tk.op('dve', lambda e: e.tensor_reduce(sm[0:R, 0:H], scr_sq.rearrange("p (g d) -> p g d", d=64), AX.X, ALU.add), reads=[k1], writes=[('sm', 0)])
        tk.op('act', lambda e: e.activation(sm[0:R, H:2 * H], sm[0:R, 0:H], AF.Ln, bias=self.epsb[0:R, :], scale=1.0 / 64),
              reads=[('sm', 0), ('epsb',)], writes=[('sm', 1)])
        tk.op('act', lambda e: e.activation(sm[0:R, H:2 * H], sm[0:R, H:2 * H], AF.Exp, scale=-0.5), reads=[('sm', 1)], writes=[('sm', 1)])
        d0 = dst0.rearrange("p (g d) -> p g d", d=64)
        d1 = dst1.rearrange("p (g d) -> p g d", d=64)
        tk.op('dve', lambda e: e.tensor_tensor(d0, x3, bc(sm[0:R, H:2 * H], 2, 64), ALU.mult), reads=[ka, ('sm', 1)], writes=[k0])
        tk.op('dve', lambda e: e.scalar_tensor_tensor(d0, d0, scale, bc(self.qkg[0:R, gi, :], 1, H), ALU.mult, ALU.mult),
              reads=[k0, ('qkg',)], writes=[k0])
        x1, x2 = d0[:, :, 0:32], d0[:, :, 32:64]
        cosb = bc(self.cs[0:R, ti, 0:32], 1, H)
        sinb = bc(self.cs[0:R, ti, 32:64], 1, H)
        b1 = a1.rearrange("p (g d) -> p g d", d=32)
        b2 = a2.rearrange("p (g d) -> p g d", d=32)
        rd = [k0, ('cs',)]
        tk.op('pool', lambda e: e.tensor_tensor(b1, x1, cosb, ALU.mult), reads=rd, writes=[('ow', 0)])
        tk.op('pool', lambda e: e.tensor_tensor(b2, x2, sinb, ALU.mult), reads=rd, writes=[('ow', 1)])
        tk.op('pool', lambda e: e.tensor_tensor(d1[:, :, 0:32], b1, b2, ALU.subtract), reads=[('ow',)], writes=[k1 + (0,)])
        tk.op('pool', lambda e: e.tensor_tensor(b1, x2, cosb, ALU.mult), reads=rd, writes=[('ow', 0)])
        tk.op('pool', lambda e: e.tensor_tensor(b2, x1, sinb, ALU.mult), reads=rd, writes=[('ow', 1)])
        tk.op('pool', lambda e: e.tensor_tensor(d1[:, :, 32:64], b1, b2, ALU.add), reads=[('ow',)], writes=[k1 + (1,)])

    def attention(self, gp, nk=NT):
        tk = self.tk
        P = self.P
        ps, xn, resid = self.ps, self.xn, self.resid
        cv = self.carve
        Wq = cv(0, [128, 8, 512], BF16)
        Wo = cv(8192, [128, 4, 1024], BF16)
        stg = cv(16384, [128, 2, 1024], F32)
        selc2 = cv(24576, [128, 16, 128], BF16)
        KT = cv(32768, [128, 8192], BF16)
        V = cv(49152, [128, 64, 2, 65], BF16)
        kwT = cv(65792, [128, 8, 128], BF16)
        vw = cv(67840, [128, 8, 2, 65], BF16)
        qf = cv(69920, [128, 2, 512], F32)
        qT = cv(74016, [128, 2, 4, 128], BF16)
        cE = cv(76064, [128, 4, 128], F32)
        mk = cv(78112, [128, 4, 128], F32)
        mbT = cv(80160, [128, 4, 128], BF16)
        E = cv(81184, [128, 2, 512], BF16)
        PT = cv(83232, [128, 4, 128], BF16)
        oacc = cv(84256, [128, 512], F32)
        ow = cv(86304, [128, 2, 256], F32)
        oT = cv(88352, [128, 4, 128], BF16)
        sm = cv(89376, [128, 32], F32)
        osw = cv(28672, [128, 2, 260], F32)
        g11 = self.gain(1, 1)
        nwq, nwo, kts, vss, ktw, vws = [P.ins[k] for k in ('nwq', 'nwo', 'kts', 'vss', 'ktw', 'vws')]
        arow, tabs, dm, wm, selc, gat, kcT, vcb = self.arow, self.tabs, self.dm, self.wm, self.selc, self.gat, self.kcT, self.vcb
        tk.op('sp', lambda e: e.dma_start(out=selc2[:, :, :].rearrange("p a b -> p (a b)"), in_=P.ins['selc2'][:, :]), writes=[('selc2',)], dma=True)
        for h in range(4):
            q = self.ns()
            tk.op('sp', lambda e: e.dma_start(out=stg[:, q, :].rearrange("p (c m) -> p c m", m=512), in_=nwq[gp, :, 2 * h:2 * h + 2, :]),
                  writes=[('stg', q)], dma=True)
            tk.op('dve', lambda e: e.tensor_tensor(Wq[:, 2 * h:2 * h + 2, :], stg[:, q, :].rearrange("p (c m) -> p c m", m=512),
                                                   bc(g11[:, 2 * h:2 * h + 2], 2, 512), ALU.mult),
                  reads=[('stg', q), ('ngs',)], writes=[('Wq', h)])
            q = self.ns()
            tk.op('sp', lambda e: e.dma_start(out=stg[:, q, :], in_=nwo[gp, :, h, :]), writes=[('stg', q)], dma=True)
            tk.op('pool', lambda e: e.tensor_copy(Wo[:, h, :], stg[:, q, :]), reads=[('stg', q)], writes=[('Wo', h)])
        for c8 in range(8):
            q = self.ns()
            tk.op('sp', lambda e: e.dma_start(out=stg[:, q, :], in_=kts[gp, :, c8 * 1024:(c8 + 1) * 1024]), writes=[('stg', q)], dma=True)
            tk.op('dve', lambda e: e.tensor_copy(KT[:, c8 * 1024:(c8 + 1) * 1024], stg[:, q, :]), reads=[('stg', q)], writes=[('KT', c8)])
            q = self.ns()
            tk.op('sp', lambda e: e.dma_start(out=stg[:, q, :].rearrange("p (t c) -> p t c", c=128), in_=vss[:, c8 * 8:(c8 + 1) * 8, gp * 128:(gp + 1) * 128]),
                  writes=[('stg', q)], dma=True)
            tk.op('pool', lambda e: e.tensor_copy(V[:, c8 * 8:(c8 + 1) * 8, :, 0:64], stg[:, q, :].rearrange("p (t g d) -> p t g d", g=2, d=64)),
                  reads=[('stg', q)], writes=[('V', c8)])
        tk.op('sp', lambda e: e.dma_start(out=kcT[:, :, :].rearrange("p a b -> p (a b)"), in_=self.d_kc[0, :, :]), reads=[('d_kc', 0)], writes=[('kcT',)], dma=True)
        tk.op('sp', lambda e: e.dma_start(out=vcb[:, :], in_=self.d_vc[0, :, :]), reads=[('d_vc', 0)], writes=[('vcb',)], dma=True)
        tk.op('pool', lambda e: e.memset(V[:, :, :, 64:65], 1.0), writes=[('V1',)])
        tk.op('pool', lambda e: e.memset(vw[:, :, :, 64:65], 1.0), writes=[('vw1',)])
        for k in range(nk):
            c0 = k * TW + 2
            ci = k // 3
            m0 = 0 if k > 0 else 4
            j0 = 4 * k - 4
            nt_ = 8 - m0
            q = self.ns()
            tk.op('sp', lambda e: e.dma_start(out=stg[:, q, 0:nt_ * 128], in_=ktw[gp, :, (j0 + m0) * 128:(j0 + 8) * 128]), writes=[('stg', q)], dma=True)
            tk.op('dve', lambda e: e.tensor_copy(kwT[:, m0:8, :].rearrange("p a b -> p (a b)"), stg[:, q, 0:nt_ * 128]), reads=[('stg', q)], writes=[('kwT',)])
            q = self.ns()
            tk.op('sp', lambda e: e.dma_start(out=stg[:, q, 0:nt_ * 128].rearrange("p (t c) -> p t c", c=128),
                                              in_=vws[:, j0 + m0:j0 + 8, gp * 128:(gp + 1) * 128]), writes=[('stg', q)], dma=True)
            tk.op('pool', lambda e: e.tensor_copy(vw[:, m0:8, :, 0:64], stg[:, q, 0:nt_ * 128].rearrange("p (t g d) -> p t g d", g=2, d=64)),
                  reads=[('stg', q)], writes=[('vw',)])
            for c in range(KC):
                tk.op('pe', lambda e: e.matmul(ps[0][:, 0:512], xn[:, c, c0:c0 + 128], Wq[:, c, :], start=(c == 0), stop=(c == KC - 1)),
                      reads=[('xn', ci), ('Wq',)], writes=[('ps', 0)])
            self.qk_norm_rope(ps[0][:, 0:512], 128, 8, 0, k, qf[:, 0, :], qf[:, 1, :], qf[:, 1, :], sm, ow[:, 0, :], ow[:, 1, :], 0.125,
                              ('qf', 0), ('qf', 1), ('ps', 0))
            for v in range(2):
                bkv = 0 if v == 0 else 2
                for t in range(4):
                    tk.op('pe', lambda e: e.transpose(ps[bkv][:, t * 128:(t + 1) * 128], qf[:, v, t * 128:(t + 1) * 128], self.ident[:, :]),
                          reads=[('qf', v), ('ident',)], writes=[('ps', bkv)])
                tk.op('act', lambda e: e.activation(qT[:, v, :, :].rearrange("p a b -> p (a b)"), ps[bkv][:, 0:512], AF.Copy),
                      reads=[('ps', bkv)], writes=[('qT', v)])
            for gg in range(2):
                g = 2 * gp + gg
                h0, h1 = 64 * gg, 64 * gg + 64
                for j in range(4):
                    tk.op('pe', lambda e: e.matmul(ps[1][:, j * 128:(j + 1) * 128], qT[h0:h1, 0, j, :], kcT[h0:h1, gp, :], start=True, stop=True),
                          reads=[('qT', 0), ('kcT',)], writes=[('ps', 1)], pg=(h0, 64))
                cEf = cE[:, :, :].rearrange("p a b -> p (a b)")
                tk.op('act', lambda e: e.activation(cEf, ps[1][:, 0:512], AF.Exp), reads=[('ps', 1)], writes=[('cE',)])
                tk.op('dve', lambda e: e.tensor_scalar(mk[:, 0, :], arow[:, 0, :], tabs[:, 0, k:k + 1], None, ALU.is_le),
                      reads=[('arow',), ('tabs',)], writes=[('mk', 0)])
                tk.op('dve', lambda e: e.tensor_tensor(cE[:, :, :], cE[:, :, :], bc(mk[:, 0, :], 1, 4), ALU.mult), reads=[('cE',), ('mk', 0)], writes=[('cE',)])
                tk.op('dve', lambda e: e.tensor_reduce(sm[:, 16:20], cE[:, :, :], AX.X, ALU.add), reads=[('cE',)], writes=[('sm', 2)])
                tk.op('dve', lambda e: e.tensor_scalar(sm[:, 16:20], sm[:, 16:20], 1e-30, None, ALU.max), reads=[('sm', 2)], writes=[('sm', 2)])
                tk.op('dve', lambda e: e.reciprocal(sm[:, 20:24], sm[:, 16:20]), reads=[('sm', 2)], writes=[('sm', 3)])
                tk.op('dve', lambda e: e.tensor_tensor(cE[:, :, :], cE[:, :, :], bc(sm[:, 20:24], 2, 128), ALU.mult), reads=[('cE',), ('sm', 3)], writes=[('cE',)])
                tk.op('dve', lambda e: e.tensor_reduce(mk[:, 3, :], cE[:, :, :].rearrange("p j n -> p n j"), AX.X, ALU.add), reads=[('cE',)], writes=[('mk', 3)])
                for j in range(4):
                    tk.op('pe', lambda e: e.transpose(ps[0][:, j * 128:(j + 1) * 128], cE[:, j, :], self.ident[:, :]),
                          reads=[('cE',), ('ident',)], writes=[('ps', 0)])
                tk.op('act', lambda e: e.activation(PT[:, :, :].rearrange("p a b -> p (a b)"), ps[0][:, 0:512], AF.Copy), reads=[('ps', 0)], writes=[('PT',)])
                for j in range(4):
                    tk.op('pe', lambda e: e.matmul(ps[3][:, j * 64:(j + 1) * 64], PT[:, j, :], vcb[:, g * 64:(g + 1) * 64], start=True, stop=True),
                          reads=[('PT',), ('vcb',)], writes=[('ps', 3)])
                tk.op('dve', lambda e: e.tensor_scalar(mk[:, 0, :], arow[:, 1, :], tabs[:, 1, k:k + 1], None, ALU.is_le), reads=[('arow',), ('tabs',)], writes=[('mk', 0)])
                tk.op('dve', lambda e: e.tensor_scalar(mk[:, 1, :], arow[:, 1, :], tabs[:, 2, k:k + 1], None, ALU.is_ge), reads=[('arow',), ('tabs',)], writes=[('mk', 1)])
                tk.op('dve', lambda e: e.tensor_tensor(mk[:, 1, :], mk[:, 1, :], arow[:, 2, :], ALU.max), reads=[('mk', 1), ('arow',)], writes=[('mk', 1)])
                tk.op('dve', lambda e: e.scalar_tensor_tensor(mk[:, 1, :], mk[:, 1, :], 1e4, mk[:, 3, :], ALU.mult, ALU.add), reads=[('mk', 1), ('mk', 3)], writes=[('mk', 1)])
                tk.op('dve', lambda e: e.scalar_tensor_tensor(mk[:, 1, :], mk[:, 1, :], 1.0, mk[:, 0, :], ALU.add, ALU.mult), reads=[('mk', 1), ('mk', 0)], writes=[('mk', 1)])
                tk.op('dve', lambda e: e.max(sm[:, 0:8], mk[:, 1, :]), reads=[('mk', 1)], writes=[('sm', 0)])
                tk.op('dve', lambda e: e.match_replace(mk[:, 2, :], sm[:, 0:8], mk[:, 1, :], -1.0), reads=[('mk', 1), ('sm', 0)], writes=[('mk', 2)])
                tk.op('dve', lambda e: e.max(sm[:, 8:16], mk[:, 2, :]), reads=[('mk', 2)], writes=[('sm', 1)])
                tk.op('dve', lambda e: e.scalar_tensor_tensor(mk[:, 2, :], mk[:, 1, :], sm[:, 15:16], mk[:, 0, :], ALU.is_ge, ALU.mult),
                      reads=[('mk', 1), ('sm', 1), ('mk', 0)], writes=[('mk', 2)])
                tk.op('dve', lambda e: e.tensor_scalar(mk[:, 2, :], mk[:, 2, :], -1.0, 30000.0, ALU.add, ALU.mult), reads=[('mk', 2)], writes=[('mk', 2)])
                tk.op('pe', lambda e: e.transpose(ps[2][:, 0:128], mk[:, 2, :], self.ident[:, :]), reads=[('mk', 2), ('ident',)], writes=[('ps', 2)])
                tk.op('act', lambda e: e.activation(mbT[:, :, :], bc(ps[2][:, 0:128], 1, 4), AF.Copy), reads=[('ps', 2)], writes=[('mbT',)])
                qr = qT[h0:h1, 1, :, :].rearrange("p a b -> p (a b)")
                mbf = mbT[:, :, :].rearrange("p a b -> p (a b)")
                ntile = 4 * k + 4
                for j in range(ntile):
                    b = 6 + (j % 2)
                    a_, kk = j // 16, j % 16
                    tk.op('pe', lambda e: e.matmul(ps[b][:, 0:512], KT[h0:h1, j * 128:(j + 1) * 128], qr, start=True, stop=False),
                          reads=[('KT',), ('qT', 1)], writes=[('ps', b)], pg=(h0, 64))
                    if a_ < 3:
                        tk.op('pe', lambda e: e.matmul(ps[b][:, 0:512], selc[32 * a_:32 * a_ + 32, kk, :], mbf[32 * a_:32 * a_ + 32, :], start=False, stop=True),
                              reads=[('selc',), ('mbT',)], writes=[('ps', b)], pg=(32 * a_, 32))
                    else:
                        tk.op('pe', lambda e: e.matmul(ps[b][:, 0:512], selc2[64:128, kk, :], mbf[64:128, :], start=False, stop=True),
                              reads=[('selc2',), ('mbT',)], writes=[('ps', b)], pg=(64, 64))
                    tk.op('act', lambda e: e.activation(E[:, j % 2, :], ps[b][:, 0:512], AF.Exp), reads=[('ps', b)], writes=[('E', j % 2)])
                    if j >= 4 * k:
                        tk.op('pool', lambda e: e.tensor_tensor(E[:, j % 2, :].rearrange("p (a b) -> p a b", b=128), E[:, j % 2, :].rearrange("p (a b) -> p a b", b=128),
                                                                bc(dm[:, j - 4 * k, :], 1, 4), ALU.mult),
                              reads=[('E', j % 2), ('dm',)], writes=[('E', j % 2)])
                    bpv = 4 + (j % 2)
                    for jj in range(4):
                        tk.op('pe', lambda e: e.matmul(ps[bpv][:, jj * 65:(jj + 1) * 65], E[:, j % 2, jj * 128:(jj + 1) * 128], V[:, j, gg, :],
                                                       start=True, stop=True),
                              reads=[('E', j % 2), ('V',), ('V1',)], writes=[('ps', bpv)])
                    if j == 0:
                        tk.op('dve', lambda e: e.tensor_copy(osw[:, 0, :], ps[bpv][:, 0:260]), reads=[('ps', bpv)], writes=[('osw', 0)])
                    else:
                        tk.op('dve', lambda e: e.tensor_tensor(osw[:, 0, :], osw[:, 0, :], ps[bpv][:, 0:260], ALU.add), reads=[('ps', bpv), ('osw', 0)], writes=[('osw', 0)])
                for m in range(m0, 8):
                    b = 6 + (m % 2)
                    tk.op('pe', lambda e: e.matmul(ps[b][:, 0:512], kwT[h0:h1, m, :], qr, start=True, stop=True),
                          reads=[('kwT',), ('qT', 1)], writes=[('ps', b)], pg=(h0, 64))
                    tk.op('act', lambda e: e.activation(E[:, m % 2, :], ps[b][:, 0:512], AF.Exp), reads=[('ps', b)], writes=[('E', m % 2)])
                    tk.op('pool', lambda e: e.tensor_tensor(E[:, m % 2, :].rearrange("p (a b) -> p a b", b=128), E[:, m % 2, :].rearrange("p (a b) -> p a b", b=128),
                                                            bc(wm[:, m, :], 1, 4), ALU.mult),
                          reads=[('E', m % 2), ('wm',)], writes=[('E', m % 2)])
                    bpv = 4 + (m % 2)
                    for jj in range(4):
                        tk.op('pe', lambda e: e.matmul(ps[bpv][:, jj * 65:(jj + 1) * 65], E[:, m % 2, jj * 128:(jj + 1) * 128], vw[:, m, gg, :],
                                                       start=True, stop=True),
                              reads=[('E', m % 2), ('vw',), ('vw1',)], writes=[('ps', bpv)])
                    if m == m0:
                        tk.op('dve', lambda e: e.tensor_copy(osw[:, 1, :], ps[bpv][:, 0:260]), reads=[('ps', bpv)], writes=[('osw', 1)])
                    else:
                        tk.op('dve', lambda e: e.tensor_tensor(osw[:, 1, :], osw[:, 1, :], ps[bpv][:, 0:260], ALU.add), reads=[('ps', bpv), ('osw', 1)], writes=[('osw', 1)])
                if 'd_o' in P.outs and k == 0 and gg == 0 and gp == 0:
                    dbgt = cv(0, [128, 1024], F32)
                    tk.op('act', lambda e: e.activation(dbgt[:, 0:260], ps[3][:, 0:260], AF.Copy), reads=[('ps', 3)], writes=[('dbgt', 0)])
                    for bi in (1, 2):
                        tk.op('act', lambda e: e.activation(dbgt[:, bi * 260:(bi + 1) * 260], osw[:, bi - 1, :], AF.Copy), reads=[('osw', bi - 1)], writes=[('dbgt', bi)])
                    tk.op('act', lambda e: e.activation(dbgt[:, 780:908], mk[:, 2, :], AF.Copy), reads=[('mk', 2)], writes=[('dbgt', 3)])
                    tk.op('act', lambda e: e.activation(dbgt[:, 908:1024], mk[:, 3, 0:116], AF.Copy), reads=[('mk', 3)], writes=[('dbgt', 4)])
                    tk.op('pool', lambda e: e.dma_start(out=P.outs['d_o'][:, :], in_=dbgt[:, :]), reads=[('dbgt',)], writes=[('o_do',)], dma=True)
                gv = gat[:, k, g * 12:(g + 1) * 12].rearrange("p (j b) -> p j b", b=3)
                for (bk_, o_) in [(4, 24), (5, 28)]:
                    p3 = osw[:, bk_ - 4, :].rearrange("p (j d) -> p j d", d=65)
                    tk.op('dve', lambda e: e.tensor_scalar(sm[:, o_:o_ + 4], p3[:, :, 64], 1e-30, None, ALU.max), reads=[('osw', bk_ - 4)], writes=[('sm', o_)])
                    tk.op('dve', lambda e: e.reciprocal(sm[:, o_:o_ + 4], sm[:, o_:o_ + 4]), reads=[('sm', o_)], writes=[('sm', o_)])
                    tk.op('dve', lambda e: e.tensor_tensor(sm[:, o_:o_ + 4], sm[:, o_:o_ + 4], gv[:, :, 1 if bk_ == 4 else 2], ALU.mult),
                          reads=[('sm', o_), ('gat',)], writes=[('sm', o_)])
                oa = oacc[:, gg * 256:(gg + 1) * 256].rearrange("p (j d) -> p j d", d=64)
                tk.op('dve', lambda e: e.tensor_tensor(oa, ps[3][:, 0:256].rearrange("p (j d) -> p j d", d=64), bc(gv[:, :, 0], 2, 64), ALU.mult),
                      reads=[('ps', 3), ('gat',)], writes=[('oacc', gg)])
                for (bk_, o_, wi) in [(4, 24, 0), (5, 28, 1)]:
                    p3 = osw[:, wi, :].rearrange("p (j d) -> p j d", d=65)
                    w3 = ow[:, wi, :].rearrange("p (j d) -> p j d", d=64)
                    tk.op('dve', lambda e: e.tensor_tensor(w3, p3[:, :, 0:64], bc(sm[:, o_:o_ + 4], 2, 64), ALU.mult),
                          reads=[('osw', wi), ('sm', o_)], writes=[('ow', wi)])
                    tk.op('dve', lambda e: e.tensor_tensor(oa, oa, w3, ALU.add), reads=[('oacc', gg), ('ow', wi)], writes=[('oacc', gg)])
            for t in range(4):
                tk.op('pe', lambda e: e.transpose(ps[0][:, t * 128:(t + 1) * 128], oacc[:, t * 128:(t + 1) * 128], self.ident[:, :]),
                      reads=[('oacc',), ('ident',)], writes=[('ps', 0)])
            tk.op('act', lambda e: e.activation(oT[:, :, :].rearrange("p a b -> p (a b)"), ps[0][:, 0:512], AF.Copy), reads=[('ps', 0)], writes=[('oT',)])
            for half in range(2):
                bko = 1 if half == 0 else 2
                for mm in range(4):
                    m = half * 4 + mm
                    for t in range(4):
                        tk.op('pe', lambda e: e.matmul(ps[bko][:, mm * 128:(mm + 1) * 128], Wo[:, t, m * 128:(m + 1) * 128], oT[:, t, :],
                                                       start=(t == 0), stop=(t == 3)),
                              reads=[('Wo',), ('oT',)], writes=[('ps', bko)])
                tk.op('dve', lambda e: e.tensor_tensor(resid[:, 4 * half:4 * half + 4, c0:c0 + 128], resid[:, 4 * half:4 * half + 4, c0:c0 + 128],
                                                       ps[bko][:, 0:512].rearrange("p (a b) -> p a b", b=128), ALU.add),
                      reads=[('ps', bko), ('resid', ci)], writes=[('resid', ci)])


        if not self.do_sample:
            return
        tk.barrier()
        R = NSEQ * SW
        c0 = PCOL
        sg = stg[:, :, :].rearrange("p a (b c) -> p (a b) c", c=256)
        sgn = [0]

        def nsg():
            q = sgn[0]
            sgn[0] = (q + 1) % 8
            return q
        qs = cv(30752, [128, 2, 16], BF16)
        Es = cv(30816, [128, 2, 16], BF16)
        PTs = cv(30880, [128, 16], BF16)
        mbTs = cv(30912, [128, 16], BF16)
        oTs = cv(30944, [128, 4, 16], BF16)
        vnb = cv(31072, [128, 2, 2, 65], BF16)
        ktnb = cv(31592, [128, 2, 24], BF16)
        gts = cv(31688, [128, 4, 48], F32)
        nm, wm0, idx = self.nm, self.wm0, self.idx
        ktn, vn, ckwT, cvwt, gats = [P.ins[k_] for k_ in ('ktn', 'vn', 'ckwT', 'cvwt', 'gats')]
        pool_ks, pool_vs = P.ins['pool_ks'], P.ins['pool_vs']
        tk.op('sp', lambda e: e.dma_start(out=gts[0:4, :, :], in_=gats[:, :, :]), writes=[('gts',)], dma=True)
        tk.op('pool', lambda e: e.memset(vnb[:, :, :, 64:65], 1.0), writes=[('vnb1',)])
        for kind in range(2):
            q = nsg()
            tk.op('sp', lambda e: e.dma_start(out=sg[:, q, 0:24], in_=ktn[kind, gp, :, :]), writes=[('sg', q)], dma=True)
            tk.op('dve', lambda e: e.tensor_copy(ktnb[:, kind, :], sg[:, q, 0:24]), reads=[('sg', q)], writes=[('ktnb', kind)])
        for c in range(KC):
            tk.op('pe', lambda e: e.matmul(ps[0][0:R, 0:512], xn[:, c, c0:c0 + R], Wq[:, c, :], start=(c == 0), stop=(c == KC - 1)),
                  reads=[('xn',), ('Wq',)], writes=[('ps', 0)])
        self.qk_norm_rope(ps[0][0:R, 0:512], R, 8, 0, NT, qf[0:R, 0, :], qf[0:R, 1, :], qf[0:R, 1, :], sm, ow[0:R, 0, :], ow[0:R, 1, :], 0.125,
                          ('qf', 0), ('qf', 1), ('ps', 0))
        for v in range(2):
            bkv = 0 if v == 0 else 2
            for t in range(4):
                tk.op('pe', lambda e: e.transpose(ps[bkv][:, t * 32:t * 32 + R], qf[0:R, v, t * 128:(t + 1) * 128], self.ident[0:R, 0:R]),
                      reads=[('qf', v), ('ident',)], writes=[('ps', bkv)])
            tk.op('act', lambda e: e.activation(qT[:, v, :, 0:R], ps[bkv][:, 0:128].rearrange("p (a b) -> p a b", b=32)[:, :, 0:R], AF.Copy),
                  reads=[('ps', bkv)], writes=[('qT', v)])
        for sq_ in range(NSEQ):
            r0 = sq_ * SW + 2
            tk.op('sp', lambda e: e.dma_start(out=kcT[:, :, :].rearrange("p a b -> p (a b)"), in_=self.d_kc[1 + sq_, :, :]),
                  reads=[('d_kc', 1 + sq_)], writes=[('kcT',)], dma=True)
            tk.op('sp', lambda e: e.dma_start(out=vcb[:, :], in_=self.d_vc[1 + sq_, :, :]), reads=[('d_vc', 1 + sq_)], writes=[('vcb',)], dma=True)
            q = nsg()
            tk.op('sp', lambda e: e.dma_start(out=sg[0:4, q, :].rearrange("p (k c) -> p k c", c=128), in_=vn[:, sq_, :, gp * 128:(gp + 1) * 128]),
                  writes=[('sg', q)], dma=True)
            tk.op('dve', lambda e: e.tensor_copy(vnb[0:4, :, :, 0:64], sg[0:4, q, :].rearrange("p (k g d) -> p k g d", g=2, d=64)),
                  reads=[('sg', q)], writes=[('vnb',)])
            for page in range(64):
                col = sq_ * 64 + page
                q = nsg()
                tk.op('pool', lambda e: e.indirect_dma_start(out=sg[:, q, :], out_offset=None, in_=pool_ks[:, :],
                                                             in_offset=bass.IndirectOffsetOnAxis(ap=idx[:, col:col + 1], axis=0)),
                      reads=[('idx',)], writes=[('sg', q)], dma=True)
                b = 6 + (page % 2)
                tk.op('pe', lambda e: e.transpose(ps[b][:, 0:128], sg[:, q, gp * 128:(gp + 1) * 128], self.ident[:, :]),
                      reads=[('sg', q), ('ident',)], writes=[('ps', b)])
                tk.op('act', lambda e: e.activation(KT[:, page * 128:(page + 1) * 128], ps[b][:, 0:128], AF.Copy), reads=[('ps', b)], writes=[('KT', page)])
                q = nsg()
                tk.op('pool', lambda e: e.indirect_dma_start(out=sg[:, q, :], out_offset=None, in_=pool_vs[:, :],
                                                             in_offset=bass.IndirectOffsetOnAxis(ap=idx[:, col:col + 1], axis=0)),
                      reads=[('idx',)], writes=[('sg', q)], dma=True)
                tk.op('dve', lambda e: e.tensor_copy(V[:, page, :, 0:64], sg[:, q, gp * 128:(gp + 1) * 128].rearrange("p (g d) -> p g d", d=64)),
                      reads=[('sg', q)], writes=[('V', page)])
            for h in range(2):
                q = nsg()
                tk.op('sp', lambda e: e.dma_start(out=sg[:, q, :], in_=ckwT[gp, :, sq_ * 512 + h * 256:sq_ * 512 + (h + 1) * 256]), writes=[('sg', q)], dma=True)
                tk.op('dve', lambda e: e.tensor_copy(kwT[:, 2 * h:2 * h + 2, :].rearrange("p a b -> p (a b)"), sg[:, q, :]), reads=[('sg', q)], writes=[('kwT', h)])
                q = nsg()
                tk.op('sp', lambda e: e.dma_start(out=sg[:, q, :].rearrange("p (t c) -> p t c", c=128),
                                                  in_=cvwt[:, sq_ * 4 + 2 * h:sq_ * 4 + 2 * h + 2, gp * 128:(gp + 1) * 128]), writes=[('sg', q)], dma=True)
                tk.op('dve', lambda e: e.tensor_copy(vw[:, 2 * h:2 * h + 2, :, 0:64], sg[:, q, :].rearrange("p (t g d) -> p t g d", g=2, d=64)),
                      reads=[('sg', q)], writes=[('vw', h)])
            for v in range(2):
                tk.op('act', lambda e: e.activation(qs[:, v, :].rearrange("p (a b) -> p a b", b=4), qT[:, v, :, r0:r0 + 4], AF.Copy),
                      reads=[('qT', v)], writes=[('qs', v)])
            for gg in range(2):
                g = 2 * gp + gg
                h0, h1 = 64 * gg, 64 * gg + 64
                for j in range(4):
                    tk.op('pe', lambda e: e.matmul(ps[1][0:4, j * 128:(j + 1) * 128], qs[h0:h1, 0, j * 4:(j + 1) * 4], kcT[h0:h1, gp, :], start=True, stop=True),
                          reads=[('qs', 0), ('kcT',)], writes=[('ps', 1)], pg=(h0, 64))
                tk.op('act', lambda e: e.activation(cE[0:4, :, :].rearrange("p a b -> p (a b)"), ps[1][0:4, 0:512], AF.Exp), reads=[('ps', 1)], writes=[('cE',)])
                tk.op('dve', lambda e: e.tensor_reduce(sm[0:4, 16:20], cE[0:4, :, :], AX.X, ALU.add), reads=[('cE',)], writes=[('sm', 2)])
                tk.op('dve', lambda e: e.tensor_scalar(sm[0:4, 16:20], sm[0:4, 16:20], 1e-30, None, ALU.max), reads=[('sm', 2)], writes=[('sm', 2)])
                tk.op('dve', lambda e: e.reciprocal(sm[0:4, 20:24], sm[0:4, 16:20]), reads=[('sm', 2)], writes=[('sm', 3)])
                tk.op('dve', lambda e: e.tensor_tensor(cE[0:4, :, :], cE[0:4, :, :], bc(sm[0:4, 20:24], 2, 128), ALU.mult), reads=[('cE',), ('sm', 3)], writes=[('cE',)])
                tk.op('dve', lambda e: e.tensor_reduce(mk[0:4, 3, :], cE[0:4, :, :].rearrange("p j n -> p n j"), AX.X, ALU.add), reads=[('cE',)], writes=[('mk', 3)])
                for j in range(4):
                    tk.op('pe', lambda e: e.transpose(ps[0][:, j * 4:(j + 1) * 4], cE[0:4, j, :], self.ident[0:4, 0:4]),
                          reads=[('cE',), ('ident',)], writes=[('ps', 0)])
                tk.op('act', lambda e: e.activation(PTs[:, :], ps[0][:, 0:16], AF.Copy), reads=[('ps', 0)], writes=[('PTs',)])
                for j in range(4):
                    tk.op('pe', lambda e: e.matmul(ps[3][0:4, j * 64:(j + 1) * 64], PTs[:, j * 4:(j + 1) * 4], vcb[:, g * 64:(g + 1) * 64], start=True, stop=True),
                          reads=[('PTs',), ('vcb',)], writes=[('ps', 3)])
                kq = NT
                tk.op('dve', lambda e: e.tensor_scalar(mk[0:4, 1, :], arow[0:4, 1, :], tabs[0:4, 2, kq:kq + 1], None, ALU.is_ge), reads=[('arow',), ('tabs',)], writes=[('mk', 1)])
                tk.op('dve', lambda e: e.tensor_tensor(mk[0:4, 1, :], mk[0:4, 1, :], arow[0:4, 2, :], ALU.max), reads=[('mk', 1), ('arow',)], writes=[('mk', 1)])
                tk.op('dve', lambda e: e.scalar_tensor_tensor(mk[0:4, 1, :], mk[0:4, 1, :], 1e4, mk[0:4, 3, :], ALU.mult, ALU.add), reads=[('mk', 1), ('mk', 3)], writes=[('mk', 1)])
                tk.op('dve', lambda e: e.max(sm[0:4, 0:8], mk[0:4, 1, :]), reads=[('mk', 1)], writes=[('sm', 0)])
                tk.op('dve', lambda e: e.match_replace(mk[0:4, 2, :], sm[0:4, 0:8], mk[0:4, 1, :], -1.0), reads=[('mk', 1), ('sm', 0)], writes=[('mk', 2)])
                tk.op('dve', lambda e: e.max(sm[0:4, 8:16], mk[0:4, 2, :]), reads=[('mk', 2)], writes=[('sm', 1)])
                tk.op('dve', lambda e: e.tensor_scalar(mk[0:4, 2, :], mk[0:4, 1, :], sm[0:4, 14:15], None, ALU.is_ge), reads=[('mk', 1), ('sm', 1)], writes=[('mk', 2)])
                tk.op('dve', lambda e: e.tensor_scalar(mk[0:4, 2, :], mk[0:4, 2, :], -1.0, 30000.0, ALU.add, ALU.mult), reads=[('mk', 2)], writes=[('mk', 2)])
                tk.op('pe', lambda e: e.transpose(ps[2][:, 0:4], mk[0:4, 2, :], self.ident[0:4, 0:4]), reads=[('mk', 2), ('ident',)], writes=[('ps', 2)])
                tk.op('act', lambda e: e.activation(mbTs[:, :].rearrange("p (a b) -> p a b", b=4), bc(ps[2][:, 0:4], 1, 4), AF.Copy), reads=[('ps', 2)], writes=[('mbTs',)])
                qr = qs[h0:h1, 1, :]
                for page in range(64):
                    b = 6 + (page % 2)
                    a_, kk = page // 16, page % 16
                    tk.op('pe', lambda e: e.matmul(ps[b][:, 0:16], KT[h0:h1, page * 128:(page + 1) * 128], qr, start=True, stop=False),
                          reads=[('KT',), ('qs', 1)], writes=[('ps', b)], pg=(h0, 64))
                    if a_ < 3:
                        tk.op('pe', lambda e: e.matmul(ps[b][:, 0:16], selc[32 * a_:32 * a_ + 32, kk, :], mbTs[32 * a_:32 * a_ + 32, :], start=False, stop=True),
                              reads=[('selc',), ('mbTs',)], writes=[('ps', b)], pg=(32 * a_, 32))
                    else:
                        tk.op('pe', lambda e: e.matmul(ps[b][:, 0:16], selc2[64:128, kk, :], mbTs[64:128, :], start=False, stop=True),
                              reads=[('selc2',), ('mbTs',)], writes=[('ps', b)], pg=(64, 64))
                    tk.op('act', lambda e: e.activation(Es[:, page % 2, :], ps[b][:, 0:16], AF.Exp), reads=[('ps', b)], writes=[('Es', page % 2)])
                    bpv = 4 + (page % 2)
                    for jj in range(4):
                        tk.op('pe', lambda e: e.matmul(ps[bpv][0:4, jj * 65:(jj + 1) * 65], Es[:, page % 2, jj * 4:(jj + 1) * 4], V[:, page, gg, :], start=True, stop=True),
                              reads=[('Es', page % 2), ('V',), ('V1',)], writes=[('ps', bpv)])
                    if page == 0:
                        tk.op('dve', lambda e: e.tensor_copy(osw[0:4, 0, :], ps[bpv][0:4, 0:260]), reads=[('ps', bpv)], writes=[('osw', 0)])
                    else:
                        tk.op('dve', lambda e: e.tensor_tensor(osw[0:4, 0, :], osw[0:4, 0, :], ps[bpv][0:4, 0:260], ALU.add), reads=[('ps', bpv), ('osw', 0)], writes=[('osw', 0)])
                for m in range(4):
                    b = 6 + (m % 2)
                    tk.op('pe', lambda e: e.matmul(ps[b][:, 0:16], kwT[h0:h1, m, :], qr, start=True, stop=True),
                          reads=[('kwT',), ('qs', 1)], writes=[('ps', b)], pg=(h0, 64))
                    tk.op('act', lambda e: e.activation(Es[:, m % 2, :], ps[b][:, 0:16], AF.Exp), reads=[('ps', b)], writes=[('Es', m % 2)])
                    if m == 0:
                        tk.op('pool', lambda e: e.tensor_tensor(Es[:, 0, :], Es[:, 0, :], wm0[:, :], ALU.mult), reads=[('Es', 0), ('wm0',)], writes=[('Es', 0)])
                    bpv = 4 + (m % 2)
                    for jj in range(4):
                        tk.op('pe', lambda e: e.matmul(ps[bpv][0:4, jj * 65:(jj + 1) * 65], Es[:, m % 2, jj * 4:(jj + 1) * 4], vw[:, m, gg, :], start=True, stop=True),
                              reads=[('Es', m % 2), ('vw',), ('vw1',)], writes=[('ps', bpv)])
                    if m == 0:
                        tk.op('dve', lambda e: e.tensor_copy(osw[0:4, 1, :], ps[bpv][0:4, 0:260]), reads=[('ps', bpv)], writes=[('osw', 1)])
                    else:
                        tk.op('dve', lambda e: e.tensor_tensor(osw[0:4, 1, :], osw[0:4, 1, :], ps[bpv][0:4, 0:260], ALU.add), reads=[('ps', bpv), ('osw', 1)], writes=[('osw', 1)])
                for kind in range(2):
                    b = 6 + kind
                    tk.op('pe', lambda e: e.matmul(ps[b][0:4, 0:16], ktnb[h0:h1, kind, r0:r0 + 4], qr, start=True, stop=True),
                          reads=[('ktnb',), ('qs', 1)], writes=[('ps', b)], pg=(h0, 64))
                    tk.op('act', lambda e: e.activation(Es[0:4, kind, :], ps[b][0:4, 0:16], AF.Exp), reads=[('ps', b)], writes=[('Es', kind)])
                    tk.op('pool', lambda e: e.tensor_tensor(Es[0:4, kind, :], Es[0:4, kind, :], nm[0:4, :], ALU.mult), reads=[('Es', kind), ('nm',)], writes=[('Es', kind)])
                    bpv = 4 + kind
                    for jj in range(4):
                        tk.op('pe', lambda e: e.matmul(ps[bpv][0:4, jj * 65:(jj + 1) * 65], Es[0:4, kind, jj * 4:(jj + 1) * 4], vnb[0:4, kind, gg, :], start=True, stop=True),
                              reads=[('Es', kind), ('vnb',), ('vnb1',)], writes=[('ps', bpv)], pg=(0, 4))
                    tk.op('dve', lambda e: e.tensor_tensor(osw[0:4, kind, :], osw[0:4, kind, :], ps[bpv][0:4, 0:260], ALU.add), reads=[('ps', bpv), ('osw', kind)], writes=[('osw', kind)])
                gv = gts[0:4, sq_, g * 12:(g + 1) * 12].rearrange("p (j b) -> p j b", b=3)
                for (wi, o_) in [(0, 24), (1, 28)]:
                    p3 = osw[0:4, wi, :].rearrange("p (j d) -> p j d", d=65)
                    tk.op('dve', lambda e: e.tensor_scalar(sm[0:4, o_:o_ + 4], p3[:, :, 64], 1e-30, None, ALU.max), reads=[('osw', wi)], writes=[('sm', o_)])
                    tk.op('dve', lambda e: e.reciprocal(sm[0:4, o_:o_ + 4], sm[0:4, o_:o_ + 4]), reads=[('sm', o_)], writes=[('sm', o_)])
                    tk.op('dve', lambda e: e.tensor_tensor(sm[0:4, o_:o_ + 4], sm[0:4, o_:o_ + 4], gv[:, :, 1 + wi], ALU.mult), reads=[('sm', o_), ('gts',)], writes=[('sm', o_)])
                oa = oacc[0:4, gg * 256:(gg + 1) * 256].rearrange("p (j d) -> p j d", d=64)
                tk.op('dve', lambda e: e.tensor_tensor(oa, ps[3][0:4, 0:256].rearrange("p (j d) -> p j d", d=64), bc(gv[:, :, 0], 2, 64), ALU.mult),
                      reads=[('ps', 3), ('gts',)], writes=[('oacc', gg)])
                for (wi, o_) in [(0, 24), (1, 28)]:
                    p3 = osw[0:4, wi, :].rearrange("p (j d) -> p j d", d=65)
                    w3 = ow[0:4, wi, :].rearrange("p (j d) -> p j d", d=64)
                    tk.op('dve', lambda e: e.tensor_tensor(w3, p3[:, :, 0:64], bc(sm[0:4, o_:o_ + 4], 2, 64), ALU.mult), reads=[('osw', wi), ('sm', o_)], writes=[('ow', wi)])
                    tk.op('dve', lambda e: e.tensor_tensor(oa, oa, w3, ALU.add), reads=[('oacc', gg), ('ow', wi)], writes=[('oacc', gg)])
            for t in range(4):
                tk.op('pe', lambda e: e.transpose(ps[0][:, 32 + t * 4:32 + (t + 1) * 4], oacc[0:4, t * 128:(t + 1) * 128], self.ident[0:4, 0:4]),
                      reads=[('oacc',), ('ident',)], writes=[('ps', 0)])
            tk.op('act', lambda e: e.activation(oTs[:, :, sq_ * 4:(sq_ + 1) * 4], ps[0][:, 32:48].rearrange("p (a b) -> p a b", b=4), AF.Copy),
                  reads=[('ps', 0)], writes=[('oTs', sq_)])
        for half in range(2):
            bko = 1 if half == 0 else 2
            for mm in range(4):
                m = half * 4 + mm
                for t in range(4):
                    tk.op('pe', lambda e: e.matmul(ps[bko][:, mm * 16:(mm + 1) * 16], Wo[:, t, m * 128:(m + 1) * 128], oTs[:, t, :], start=(t == 0), stop=(t == 3)),
                          reads=[('Wo',), ('oTs',)], writes=[('ps', bko)])
            rv = resid[:, 4 * half:4 * half + 4, PCOL:NCOL].rearrange("p m (s w) -> p m s w", w=SW)[:, :, :, 2:6]
            tk.op('dve', lambda e: e.tensor_tensor(rv, rv, ps[bko][:, 0:64].rearrange("p (m s q) -> p m s q", s=4, q=4), ALU.add),
                  reads=[('ps', bko), ('resid', 5)], writes=[('resid', 5)])


def build(stage=99):
    from contextlib import ExitStack
    P = Prog()
    nc = P.nc
    xT = P.din('xT', [128, KC, NCOL])
    P.din('pT', [2, 128, 2, NCOL])
    ng = P.din('ng', [128, 2 * 4 * KC])
    P.din('wgu', [2, 2, 22, 128, KC * 256])
    P.din('wdn', [2, 2, 22, 128, 1024])
    P.din('wpg', [2, 8, 128, 1280])
    P.din('cwin', [24, 128, KC * 128])
    cw = P.din('cw', [128, 3 * KC])
    P.din('cwout', [8, 128, KC * 128])
    hm = P.din('hm', [128, NT])
    stT = P.din('stT', [128, KC, NSEQ, 2])
    P.din('nwkv', [128, KC, 1584])
    qkg = P.din('qkg', [128, 4 * 64])
    csd = P.din('cs', [128, NT + 1, 64])
    ckw = P.din('ckw', [NSEQ, 512, 256])
    cvw = P.din('cvw', [NSEQ, 512, 256])
    yT = P.dout('yT', [128, KC, NCOL])
    cvoT = P.dout('cvoT', [128, KC, 2 + 2 * NSEQ])
    P.dout('o_kvp', [NT, 128, 1536])
    P.dout('o_kvs', [NSEQ * SW, 1536])
    o_kws = P.dout('o_kws', [NSEQ, 512, 256])
    o_vws = P.dout('o_vws', [NSEQ, 512, 256])
    P.dout('gato', [128, NT + 1, 48])
    with ExitStack() as stack:
        B = Builder(P, stack)
        tk = B.tk
        sb = B.sb
        B.cws = sb("cws", [128, 3 * KC], F32)
        B.hms = sb("hms", [128, NT], F32)
        B.sts = sb("sts", [128, KC, NSEQ, 2], F32)
        B.cvo = sb("cvo", [128, KC, 2 + 2 * NSEQ], F32)
        B.qkg = sb("qkg", [128, 4, 64], F32)
        B.cs = sb("cs", [128, NT + 1, 64], F32)
        B.gat = sb("gat", [128, NT + 1, 48], F32)
        B.st4 = sb("st4", [128, 8], F32)
        tk.op('pool', lambda e: e.memset(B.onesb[:, :], 1.0 / D), writes=[('onesb',)])
        tk.op('pool', lambda e: e.memset(B.epsb[:, :], EPS), writes=[('epsb',)])
        tk.op('sp', lambda e: e.dma_start(out=B.ngs[:, :], in_=ng[:, :]), writes=[('ngs',)], dma=True)
        tk.op('sp', lambda e: e.dma_start(out=B.cws[:, :], in_=cw[:, :]), writes=[('cws',)], dma=True)
        tk.op('sp', lambda e: e.dma_start(out=B.hms[:, :], in_=hm[:, :]), writes=[('hms',)], dma=True)
        tk.op('sp', lambda e: e.dma_start(out=B.sts[:, :, :, :], in_=stT[:, :, :, :]), writes=[('sts',)], dma=True)
        tk.op('sp', lambda e: e.dma_start(out=B.qkg[:, :, :].rearrange("p a d -> p (a d)"), in_=qkg[:, :]), writes=[('qkg',)], dma=True)
        tk.op('sp', lambda e: e.dma_start(out=B.cs[:, :, :], in_=csd[:, :, :]), writes=[('cs',)], dma=True)
        for sq_ in range(NSEQ):
            tk.op('pool', lambda e: e.dma_start(out=o_kws[sq_, 0:508, :], in_=ckw[sq_, 4:512, :]), writes=[('o_kws0', sq_)], dma=True)
            tk.op('pool', lambda e: e.dma_start(out=o_vws[sq_, 0:508, :], in_=cvw[sq_, 4:512, :]), writes=[('o_vws0', sq_)], dma=True)
        for ci, (a, b) in enumerate(CTS):
            tk.op('sp', lambda e: e.dma_start(out=B.resid[:, :, a:b], in_=xT[:, :, a:b]), writes=[('resid', ci)], dma=True)
        B.ffn(0, 0)
        tk.barrier()
        tk.op('pool', lambda e: e.memset(B.yb[:, 0:2], 0.0), writes=[('yb',)])
        B.conv()
        tk.barrier()
        B.ffn(0, 1)
        B.ple(0)
        tk.op('pool', lambda e: e.dma_start(out=cvoT[:, :, :], in_=B.cvo[:, :, :]), reads=[('cvo',)], writes=[('o_cvo',)], dma=True)
        if stage >= 3:
            B.ffn(1, 0)
            tk.barrier()
            B.nsa_proj()
            tk.barrier()
        tk.op('pool', lambda e: e.dma_start(out=P.outs['gato'][:, :, :], in_=B.gat[:, :, :]), reads=[('gat',)], writes=[('o_gat',)], dma=True)
        for ci, (a, b) in enumerate(CTS):
            tk.op('pool', lambda e: e.dma_start(out=yT[:, :, a:b], in_=B.resid[:, :, a:b]), reads=[('resid', ci)], writes=[('o_y', ci)], dma=True)
        tk.wait_all('pool')
    return P


def build2(phase=3, nk=NT, ngp=2, cstop=9):
    from contextlib import ExitStack
    P = Prog()
    nc = P.nc
    xT = P.din('resid2', [128, KC, NCOL])
    ng = P.din('ng', [128, 2 * 4 * KC])
    if phase >= 3:
        P.din('pT', [2, 128, 2, NCOL])
        P.din('wgu', [2, 2, 22, 128, KC * 256])
        P.din('wdn', [2, 2, 22, 128, 1024])
        P.din('wpg', [2, 8, 128, 1280])
    qkg = P.din('qkg', [128, 4 * 64])
    csd = P.din('cs', [128, NT + 1, 64])
    gat2 = P.din('gat2', [128, NT + 1, 48])
    P.din('nwq', [2, 128, KC, 512])
    P.din('nwo', [2, 128, 4, 1024])
    P.din('kts', [2, 128, 8192])
    P.din('vss', [128, 64, 256])
    P.din('ktw', [2, 128, 8192])
    P.din('vws', [128, 64, 256])
    P.din('kcr', [2, 128, 64, 256])
    P.din('w1r', [2, 128, 64 * 128])
    P.din('w2s', [128, 128])
    posr = P.din('posr', [128, 128])
    tabs = P.din('tabs', [128, 51])
    arow = P.din('arow', [128, 384])
    dm = P.din('dm', [128, 4 * 128], BF16)
    wm = P.din('wm', [128, 8 * 128], BF16)
    selc = P.din('selc', [128, 16 * 128], BF16)
    ident = P.din('ident', [128, 128])
    P.din('selc2', [128, 16 * 128], BF16)
    do_sample = phase >= 3 or phase == -1
    if do_sample:
        for nm_ in ('pool_kc', 'pool_vc', 'pool_ks', 'pool_vs'):
            P.din(nm_, [2560 * 128, 256])
        ptab = P.din('ptab', [128, 256], I32)
        pcol = P.din('pcol', [128, 1])
        P.din('ktn', [2, 2, 128, 24])
        P.din('vn', [4, NSEQ, 2, 256])
        P.din('ckwT', [2, 128, NSEQ * 512])
        P.din('cvwt', [128, NSEQ * 4, 256])
        P.din('gats', [4, NSEQ, 48])
        nmd = P.din('nm', [128, 16], BF16)
        wm0d = P.din('wm0', [128, 16], BF16)
    yT = P.dout('yT2', [128, KC, NCOL])
    with ExitStack() as stack:
        B = Builder(P, stack)
        tk = B.tk
        sb = B.sb
        B.qkg = sb("qkg", [128, 4, 64], F32)
        B.cs = sb("cs", [128, NT + 1, 64], F32)
        B.gat = sb("gat", [128, NT + 1, 48], F32)
        B.st4 = sb("st4", [128, 8], F32)
        B.posr = sb("posr", [128, 2, 64], F32)
        B.tabs = sb("tabs", [128, 3, 17], F32)
        B.arow = sb("arow", [128, 3, 128], F32)
        B.dm = sb("dm", [128, 4, 128], BF16)
        B.wm = sb("wm", [128, 8, 128], BF16)
        B.selc = sb("selc", [128, 16, 128], BF16)
        B.ident = sb("ident", [128, 128], F32)
        B.kcT = sb("kcT", [128, 2, 128], BF16)
        B.vcb = sb("vcb", [128, 256], BF16)
        B.do_sample = do_sample
        B.d_kc = nc.dram_tensor("d_kc", [5, 128, 256], BF16, kind="Internal")
        B.d_vc = nc.dram_tensor("d_vc", [5, 128, 256], BF16, kind="Internal")
        if do_sample:
            B.idx = sb("idx", [128, 256], I32)
            B.pcol = sb("pcol", [128, 1], F32)
            B.nm = sb("nm", [128, 16], BF16)
            B.wm0 = sb("wm0", [128, 16], BF16)
        if phase == 2:
            P.dout('d_o', [128, 1024])
        tk.op('pool', lambda e: e.memset(B.onesb[:, :], 1.0 / D), writes=[('onesb',)])
        tk.op('pool', lambda e: e.memset(B.epsb[:, :], EPS), writes=[('epsb',)])
        flat = lambda t: t[:, :, :].rearrange("p a b -> p (a b)")
        for (dst, src, key) in [(B.ngs[:, :], ng[:, :], 'ngs'), (flat(B.qkg), qkg[:, :], 'qkg'), (B.cs[:, :, :], csd[:, :, :], 'cs'),
                                (B.gat[:, :, :], gat2[:, :, :], 'gat'), (flat(B.posr), posr[:, :], 'posr'), (flat(B.tabs), tabs[:, :], 'tabs'),
                                (flat(B.arow), arow[:, :], 'arow'), (flat(B.dm), dm[:, :], 'dm'), (flat(B.wm), wm[:, :], 'wm'),
                                (flat(B.selc), selc[:, :], 'selc'), (B.ident[:, :], ident[:, :], 'ident')]:
            tk.op('sp', lambda e: e.dma_start(out=dst, in_=src), writes=[(key,)], dma=True)
        for ci, (a, b) in enumerate(CTS):
            tk.op('sp', lambda e: e.dma_start(out=B.resid[:, :, a:b], in_=xT[:, :, a:b]), writes=[('resid', ci)], dma=True)
        if do_sample:
            tk.op('sp', lambda e: e.dma_start(out=B.idx[:, :], in_=ptab[:, :]), writes=[('idx',)], dma=True)
            tk.op('sp', lambda e: e.dma_start(out=B.pcol[:, :], in_=pcol[:, :]), writes=[('pcol',)], dma=True)
            tk.op('sp', lambda e: e.dma_start(out=B.nm[:, :], in_=nmd[:, :]), writes=[('nm',)], dma=True)
            tk.op('sp', lambda e: e.dma_start(out=B.wm0[:, :], in_=wm0d[:, :]), writes=[('wm0',)], dma=True)
            idf = B.carve(0, [128, 256], F32)
            tk.op('dve', lambda e: e.tensor_copy(idf[:, :], B.idx[:, :]), reads=[('idx',)], writes=[('idf',)])
            tk.op('dve', lambda e: e.tensor_scalar(idf[:, :], idf[:, :], 128.0, B.pcol[:, 0:1], ALU.mult, ALU.add), reads=[('idf',), ('pcol',)], writes=[('idf',)])
            tk.op('dve', lambda e: e.tensor_copy(B.idx[:, :], idf[:, :]), reads=[('idf',)], writes=[('idx',)])
            tk.barrier()
        B.norm()
        tk.barrier()
        if phase >= 1 or do_sample:
            B.compress(cstop)
            if do_sample:
                for sq_ in range(NSEQ):
                    tk.barrier()
                    B.compress(cstop, 1 + sq_, sq_)
        if phase >= 2 or do_sample:
            for gp in range(ngp):
                tk.barrier()
                B.attention(gp, nk)
        tk.barrier()
        if phase == 1:
            dk = P.dout('d_kcT', [128, 256], BF16)
            dv = P.dout('d_vcb', [128, 256], BF16)
            tk.op('pool', lambda e: e.dma_start(out=dk[:, :], in_=B.kcT[:, :, :].rearrange("p a b -> p (a b)")), reads=[('kcT',)], writes=[('o_dk',)], dma=True)
            tk.op('pool', lambda e: e.dma_start(out=dv[:, :], in_=B.vcb[:, :]), reads=[('vcb',)], writes=[('o_dv',)], dma=True)
        if phase >= 3:
            B.ffn(1, 1)
            B.ple(1)
        for ci, (a, b) in enumerate(CTS):
            tk.op('pool', lambda e: e.dma_start(out=yT[:, :, a:b], in_=B.resid[:, :, a:b]), reads=[('resid', ci)], writes=[('o_y', ci)], dma=True)
        tk.wait_all('pool')
    return P


def tile_w(W, cols):
    din = W.shape[0]
    sub = W[:, cols]
    return np.ascontiguousarray(sub.reshape(din // 128, 128, -1).transpose(1, 0, 2).reshape(128, -1))


def fm(a):
    ncol, F = a.shape
    return np.ascontiguousarray(a.T.reshape(F // 128, 128, ncol).transpose(1, 0, 2))


def core_cols(r, xp, xs):
    b, c = r // 4, r % 4
    F = xp.shape[-1]
    out = np.zeros((NCOL, F), np.float32)
    for k in range(NT):
        i = 4 * k + c
        lo = 128 * i - 2
        if lo < 0:
            out[k * TW + 2:(k + 1) * TW] = xp[b, 0:128]
        else:
            out[k * TW:(k + 1) * TW] = xp[b, lo:lo + TW]
    for s in range(NSEQ):
        out[PCOL + s * SW + 2:PCOL + (s + 1) * SW] = xs[4 * r + s]
    return out


def prep_shared(inp):
    f = lambda k: np.asarray(inp[k], np.float32)
    sh = {}
    ngv = f('norm_g')
    sh['ng'] = np.ascontiguousarray(ngv.reshape(2, 4, KC, 128).transpose(3, 0, 1, 2).reshape(128, 64))
    wgu = f('ffn_w_gu')
    a = np.zeros((2, 2, 22, 128, KC * 256), np.float32)
    for l in range(2):
        for ff in range(2):
            for j in range(22):
                cols = list(range(128 * j, 128 * j + 128)) + list(range(DFF + 128 * j, DFF + 128 * j + 128))
                a[l, ff, j] = tile_w(wgu[l, ff], cols)
    sh['wgu'] = a
    sh['wdn'] = np.ascontiguousarray(f('ffn_w_down').reshape(2, 2, 22, 128, 1024))
    wg, wp = f('ple_w_gate'), f('ple_w_proj')
    a = np.zeros((2, 8, 128, 1280), np.float32)
    for l in range(2):
        for m in range(8):
            cols = list(range(128 * m, 128 * m + 128))
            a[l, m, :, 0:1024] = tile_w(wg[l], cols)
            a[l, m, :, 1024:1280] = tile_w(wp[l], cols)
    sh['wpg'] = a
    cwi = f('conv_w_in')[0]
    a = np.zeros((24, 128, 1024), np.float32)
    for ff in range(8):
        for kind, base in enumerate([1024, 2048, 0]):
            a[3 * ff + kind] = tile_w(cwi, list(range(base + 128 * ff, base + 128 * ff + 128)))
    sh['cwin'] = a
    sh['cw'] = np.ascontiguousarray(f('conv_w')[0].reshape(3, KC, 128).transpose(2, 0, 1).reshape(128, 24))
    cwo = f('conv_w_out')[0]
    sh['cwout'] = np.stack([tile_w(cwo, list(range(128 * m, 128 * m + 128))) for m in range(8)])
    return sh


def prep_core(r, inp, sh):
    f = lambda k: np.asarray(inp[k], np.float32)
    m = dict(sh)
    m['xT'] = fm(core_cols(r, f('x_prompt'), f('x_sample')))
    pp, psm = f('p_prompt'), f('p_sample')
    m['pT'] = np.stack([fm(core_cols(r, pp[l], psm[l])) for l in range(2)])
    hmv = np.ones((128, NT), np.float32)
    if r % 4 == 0:
        hmv[:, 0] = 0.0
    m['hm'] = hmv
    st = f('state_conv')[0][4 * r:4 * r + 4]
    m['stT'] = np.ascontiguousarray(st.reshape(NSEQ, 2, KC, 128).transpose(3, 2, 0, 1))
    return m


def rope_tab(pos):
    half = 32
    inv = (np.float32(10000.0) ** (-(np.arange(half, dtype=np.float32) / np.float32(half)))).astype(np.float32)
    ang = (pos.astype(np.float32)[:, None] * inv[None, :]).astype(np.float32)
    return np.concatenate([np.cos(ang), np.sin(ang)], axis=1).astype(np.float32)


def prep_shared2(inp, sh):
    f = lambda k: np.asarray(inp[k], np.float32)
    win = f('nsa_w_in')[0]
    sh['nwkv'] = np.ascontiguousarray(win[:, 1024:2608].reshape(KC, 128, 1584).transpose(1, 0, 2))
    sh['qkg'] = np.ascontiguousarray(np.broadcast_to(f('nsa_qk_g')[0].reshape(1, 256), (128, 256)))
    return sh


def prep_core2(r, inp, m):
    f = lambda k: np.asarray(inp[k], np.float32)
    c = r % 4
    cs = np.zeros((128, NT + 1, 64), np.float32)
    for k in range(NT):
        cs[:, k, :] = rope_tab(128 * (4 * k + c) + np.arange(128))
    spos = np.zeros(128, np.int64)
    for s in range(NSEQ):
        spos[s * SW + 2:s * SW + 6] = 8192 + np.arange(4)
    cs[:, NT, :] = rope_tab(spos)
    m['cs'] = cs
    m['ckw'] = np.ascontiguousarray(f('cache_k_win')[0, 4 * r:4 * r + 4].reshape(NSEQ, 512, 256))
    m['cvw'] = np.ascontiguousarray(f('cache_v_win')[0, 4 * r:4 * r + 4].reshape(NSEQ, 512, 256))
    return m


_CACHE = {}


def prep2_shared(inp, sh):
    f = lambda k: np.asarray(inp[k], np.float32)
    o = {k: sh[k] for k in ('ng', 'wgu', 'wdn', 'wpg', 'qkg')}
    win = f('nsa_w_in')[0]
    nwq = np.zeros((2, 128, KC, 512), np.float32)
    for gp in range(2):
        cols = []
        for j in range(4):
            for gg in range(2):
                h = 4 * (2 * gp + gg) + j
                cols += list(range(64 * h, 64 * h + 64))
        nwq[gp] = tile_w(win, cols).reshape(128, KC, 512)
    o['nwq'] = nwq
    wo = f('nsa_w_out')[0]
    o['nwo'] = np.ascontiguousarray(wo.reshape(2, 4, 128, 1024).transpose(0, 2, 1, 3))
    w1 = f('nsa_cmp_w1')[0]
    w1r = w1.reshape(2, 1, 64, 64 * 128)
    o['w1r'] = np.ascontiguousarray(np.broadcast_to(w1r, (2, 2, 64, 64 * 128)).reshape(2, 128, 64 * 128))
    w2 = f('nsa_cmp_w2')[0]
    o['w2s'] = np.ascontiguousarray(w2.transpose(1, 0, 2).reshape(128, 128))
    pos = f('nsa_cmp_pos')[0]
    pr = pos.transpose(1, 0, 2).reshape(1, 64, 128)
    o['posr'] = np.ascontiguousarray(np.broadcast_to(pr, (2, 64, 128)).reshape(128, 128))
    n = np.arange(128, dtype=np.float32)
    ar = np.stack([64 * n + 63, n, (n == 0).astype(np.float32)], 0).reshape(1, 384)
    o['arow'] = np.ascontiguousarray(np.broadcast_to(ar, (128, 384))).astype(np.float32)
    sel = np.zeros((128, 16, 128), np.float32)
    for a in range(4):
        for m in range(32):
            for kk in range(16):
                if m // 2 == kk:
                    e = m % 2
                    sel[32 * a + m, kk, 64 * e:64 * e + 64] = 1.0
    o['selc'] = sel.reshape(128, 2048).astype(ml_dtypes.bfloat16)
    sel2 = np.zeros((128, 16, 128), np.float32)
    sel2[96:128] = sel[96:128]
    o['selc2'] = sel2.reshape(128, 2048).astype(ml_dtypes.bfloat16)
    o['ident'] = np.eye(128, dtype=np.float32)
    o['_inp'] = inp
    o['_pools'] = {nm_: np.ascontiguousarray(np.asarray(inp[key_], np.float32)[0].reshape(2560 * 128, 256))
                   for nm_, key_ in (('pool_kc', 'cache_k_cmp'), ('pool_vc', 'cache_v_cmp'), ('pool_ks', 'cache_k_sel'), ('pool_vs', 'cache_v_sel'))}
    return o


def prep2_core(r, o, m1, res1, full):
    b, c = r // 4, r % 4
    m = dict(o)
    m['resid2'] = np.asarray(res1[r]['yT'])
    m['gat2'] = np.asarray(res1[r]['gato'])
    m['pT'] = m1['pT']
    m['cs'] = m1['cs']
    m.update(full[b])
    rr = np.arange(128, dtype=np.float32)
    tabs = np.zeros((128, 3, 17), np.float32)
    for k in range(NT):
        qp = 128 * (4 * k + c) + rr
        tabs[:, 0, k] = qp
        tabs[:, 1, k] = np.floor(qp / 64)
        tabs[:, 2, k] = np.floor(qp / 64) - 1
    tabs[:, 0, 16] = 8192 + rr
    tabs[:, 1, 16] = 128
    tabs[:, 2, 16] = 127
    m['tabs'] = tabs.reshape(128, 51)
    key = np.arange(128)[:, None]
    q = np.arange(128)[None, :]
    dmv = np.zeros((128, 4, 128), np.float32)
    for rp in range(4):
        if rp < c:
            dmv[:, rp, :] = 1.0
        elif rp == c:
            dmv[:, rp, :] = (key <= q)
    m['dm'] = dmv.reshape(128, 512).astype(ml_dtypes.bfloat16)
    wmv = np.zeros((128, 8, 128), np.float32)
    for mm in range(8):
        diff = 128 * (c + 4 - mm) + (q - key)
        wmv[:, mm, :] = (diff >= 0) & (diff < 512)
    m['wm'] = wmv.reshape(128, 1024).astype(ml_dtypes.bfloat16)
    inp = o['_inp']
    for nm_, key_ in (('pool_kc', 'cache_k_cmp'), ('pool_vc', 'cache_v_cmp'), ('pool_ks', 'cache_k_sel'), ('pool_vs', 'cache_v_sel')):
        m[nm_] = o['_pools'][nm_]
    pt = np.asarray(inp['page_table'], np.int32)[4 * r:4 * r + 4].reshape(1, 256)
    m['ptab'] = np.ascontiguousarray(np.broadcast_to(pt, (128, 256))).astype(np.int32)
    m['pcol'] = np.arange(128, dtype=np.float32).reshape(128, 1)
    oks = np.asarray(res1[r]['o_kvs'])
    ktn = np.zeros((2, 2, 128, 24), np.float32)
    vnv = np.zeros((4, NSEQ, 2, 256), np.float32)
    for kind, (kc0, vc0) in enumerate([(512, 768), (1024, 1280)]):
        kk_ = oks[:, kc0:kc0 + 256]
        for gp in range(2):
            ktn[kind, gp] = kk_[:, gp * 128:(gp + 1) * 128].T
        for s_ in range(NSEQ):
            vnv[:, s_, kind, :] = oks[s_ * SW + 2:s_ * SW + 6, vc0:vc0 + 256]
    m['ktn'] = ktn
    m['vn'] = vnv
    ckw = np.asarray(inp['cache_k_win'], np.float32)[0, 4 * r:4 * r + 4].reshape(NSEQ, 512, 256)
    cvw = np.asarray(inp['cache_v_win'], np.float32)[0, 4 * r:4 * r + 4].reshape(NSEQ, 512, 256)
    m['ckwT'] = np.ascontiguousarray(ckw.transpose(2, 0, 1).reshape(2, 128, NSEQ * 512))
    m['cvwt'] = np.ascontiguousarray(cvw.reshape(NSEQ, 4, 128, 256).transpose(2, 0, 1, 3).reshape(128, NSEQ * 4, 256))
    g1 = np.asarray(res1[r]['gato'])[:, NT, :]
    gs_ = np.zeros((4, NSEQ, 48), np.float32)
    for s_ in range(NSEQ):
        gs_[:, s_, :] = g1[s_ * SW + 2:s_ * SW + 6]
    m['gats'] = gs_
    kq = np.arange(128)[:, None]
    jq = np.arange(16)[None, :] % 4
    m['nm'] = ((kq <= jq) & (kq < 4)).astype(np.float32).astype(ml_dtypes.bfloat16)
    m['wm0'] = (kq >= jq + 1).astype(np.float32).astype(ml_dtypes.bfloat16)
    return m


def kernel(**inp):
    if 'P' not in _CACHE:
        _CACHE['P'] = build()
        _CACHE['P2'] = build2()
    P, P2 = _CACHE['P'], _CACHE['P2']
    sh = prep_shared2(inp, prep_shared(inp))
    maps = []
    full_maps = []
    for r in range(NCORE):
        m = prep_core2(r, inp, prep_core(r, inp, sh))
        full_maps.append(m)
        maps.append({k: v for k, v in m.items() if k in P.ins})
    res = run_bass_kernel_spmd(P.nc, maps, core_ids=list(range(NCORE)))
    R = res.results
    B_, S_ = 2, 8192
    y_p = np.zeros((B_, S_, D), np.float32)
    y_s = np.zeros((32, 4, D), np.float32)
    conv_p = np.zeros((1, B_, 2, D), np.float32)
    conv_s = np.zeros((1, 32, 2, D), np.float32)
    kvp = [np.zeros((1, B_, S_, 4, 64), np.float32) for _ in range(6)]
    kwp = [np.zeros((1, B_, 512, 4, 64), np.float32) for _ in range(2)]
    kvs = [np.zeros((1, 32, 4, 4, 64), np.float32) for _ in range(4)]
    kws = [np.zeros((1, 32, 512, 4, 64), np.float32) for _ in range(2)]
    for r in range(NCORE):
        b, c = r // 4, r % 4
        cv = np.asarray(R[r]['cvoT']).transpose(2, 1, 0).reshape(2 + 2 * NSEQ, D)
        okp = np.asarray(R[r]['o_kvp'])
        oks = np.asarray(R[r]['o_kvs'])
        for k in range(NT):
            i = 4 * k + c
            for j in range(6):
                kvp[j][0, b, 128 * i:128 * i + 128] = okp[k, :, 256 * j:256 * j + 256].reshape(128, 4, 64)
        if c == 3:
            conv_p[0, b] = cv[0:2]
        for s in range(NSEQ):
            sg = 4 * r + s
            conv_s[0, sg] = cv[2 + 2 * s:4 + 2 * s]
            for j in range(4):
                kvs[j][0, sg] = oks[s * SW + 2:s * SW + 6, 256 * j:256 * j + 256].reshape(4, 4, 64)
            kws[0][0, sg] = np.asarray(R[r]['o_kws'])[s].reshape(512, 4, 64)
            kws[1][0, sg] = np.asarray(R[r]['o_vws'])[s].reshape(512, 4, 64)
    for j in range(2):
        kwp[j][0] = kvp[4 + j][0][:, S_ - 512:]
    full = []
    for b in range(B_):
        d = {}
        d['kcr'] = np.ascontiguousarray(np.stack([kvp[0][0, b].reshape(64, 128, 256), kvp[1][0, b].reshape(64, 128, 256)]).transpose(0, 2, 1, 3))
        ks = kvp[2][0, b].reshape(S_, 256)
        d['kts'] = np.ascontiguousarray(ks.T.reshape(2, 128, S_))
        d['vss'] = np.ascontiguousarray(kvp[3][0, b].reshape(64, 128, 256).transpose(1, 0, 2))
        kw = kvp[4][0, b].reshape(S_, 256)
        d['ktw'] = np.ascontiguousarray(kw.T.reshape(2, 128, S_))
        d['vws'] = np.ascontiguousarray(kvp[5][0, b].reshape(64, 128, 256).transpose(1, 0, 2))
        full.append(d)
    o2 = prep2_shared(inp, sh)
    maps2 = []
    for r in range(NCORE):
        m = prep2_core(r, o2, full_maps[r], R, full)
        maps2.append({k: v for k, v in m.items() if k in P2.ins})
    if _CACHE.get('hook') is not None:
        return _CACHE['hook'](maps2)
    res2 = run_bass_kernel_spmd(P2.nc, maps2, core_ids=list(range(NCORE)))
    R2 = res2.results
    for r in range(NCORE):
        b, c = r // 4, r % 4
        yt = np.asarray(R2[r]['yT2']).transpose(2, 1, 0).reshape(NCOL, D)
        for k in range(NT):
            i = 4 * k + c
            y_p[b, 128 * i:128 * i + 128] = yt[k * TW + 2:(k + 1) * TW]
        for s in range(NSEQ):
            y_s[4 * r + s] = yt[PCOL + s * SW + 2:PCOL + (s + 1) * SW]
    return (y_p, y_s, conv_p, conv_s, kvp[0], kvp[1], kvp[2], kvp[3], kwp[0], kwp[1],
            kvs[0], kvs[1], kvs[2], kvs[3], kws[0], kws[1])
```

```python
import numpy as np
import ml_dtypes
import concourse.bass as bass
import concourse.mybir as mybir
from concourse.bass_utils import run_bass_kernel_spmd

F32 = mybir.dt.float32
BF16 = mybir.dt.bfloat16
I32 = mybir.dt.int32
ALU = mybir.AluOpType
AF = mybir.ActivationFunctionType
AX = mybir.AxisListType

D = 1024
KC = 8
DFF = 2816
NCORE = 8
NT = 16
TW = 130
NSEQ = 4
SW = 6
PCOL = NT * TW
NCOL = PCOL + NSEQ * SW
CTS = [(0, 390), (390, 780), (780, 1170), (1170, 1560), (1560, 1950), (1950, NCOL)]
HGROUPS = [(0, 6), (6, 12), (12, 17), (17, 22)]
EPS = 1e-6


def bc(ap, pos, count):
    l = [list(x) for x in ap.ap]
    l.insert(pos, [0, count])
    return bass.AP(ap.tensor, ap.offset, l)


class TK:
    NDS = 24

    def __init__(self, nc, stack):
        self.nc = nc
        self.stack = stack
        self.eng = {'pe': nc.tensor, 'act': nc.scalar, 'dve': nc.vector, 'pool': nc.gpsimd, 'sp': nc.sync}
        self.sem = {}
        self.cnt = {}
        self.nsem = 0
        for e in self.eng:
            self._newsem(e)
        self.dsem = [stack.enter_context(nc.semaphore(f"dq{i}")) for i in range(self.NDS)]
        self.dcnt = [0] * self.NDS
        self.dnext = 0
        self.waited = {e: {} for e in self.eng}
        self.recs = {}
        self.sid = {}

    def _newsem(self, e):
        self.nsem += 1
        self.sem[e] = self.stack.enter_context(self.nc.semaphore(f"e{e}{self.nsem}"))
        self.cnt[e] = 0

    def _overlap(self, k):
        out = []
        g = self.recs.get(k[0])
        if g:
            n = len(k)
            for kk, rec in g.items():
                m = min(n, len(kk))
                if kk[:m] == k[:m]:
                    out.append((kk, rec))
        return out

    def _wait(self, e, deps):
        w = self.waited[e]
        best = {}
        for (sem, val) in deps:
            i = id(sem)
            if val > w.get(i, 0) and val > best.get(i, (None, 0))[1]:
                best[i] = (sem, val)
        for i, (sem, val) in best.items():
            self.eng[e].wait_ge(sem, val)
            w[i] = val

    def op(self, e, fn, reads=(), writes=(), dma=False, pg=(0, 128)):
        deps = []
        for k in reads:
            for kk, rec in self._overlap(k):
                if rec['w'] is not None:
                    deps.append(rec['w'])
        for k in writes:
            for kk, rec in self._overlap(k):
                if rec['w'] is not None:
                    deps.append(rec['w'])
                deps.extend(rec['r'].values())
        if e == 'pe':
            deps = [d for d in deps if d[0] is not self.sem['pe']]
        di = None
        if dma:
            di = self.dnext
            self.dnext = (di + 1) % self.NDS
            if self.dcnt[di] > 0:
                deps.append((self.dsem[di], 16 * self.dcnt[di]))
        if e == 'pe':
            if getattr(self, 'last_pg', pg) != pg and self.cnt['pe'] > 0:
                deps.append((self.sem['pe'], self.cnt['pe']))
            self.last_pg = pg
        self._wait(e, deps)
        ins = fn(self.eng[e])
        if dma:
            self.dcnt[di] += 1
            ins.then_inc(self.dsem[di], 16)
            tok = (self.dsem[di], 16 * self.dcnt[di])
        else:
            if self.cnt[e] >= 30000:
                self._newsem(e)
            self.cnt[e] += 1
            ins.then_inc(self.sem[e], 1)
            tok = (self.sem[e], self.cnt[e])
        for k in reads:
            g = self.recs.setdefault(k[0], {})
            rec = g.get(k)
            if rec is None:
                rec = {'w': None, 'r': {}}
                g[k] = rec
            rec['r'][id(tok[0])] = tok
        for k in writes:
            g = self.recs.setdefault(k[0], {})
            n = len(k)
            for kk in [kk for kk in g if len(kk) > n and kk[:n] == k]:
                del g[kk]
            g[k] = {'w': tok, 'r': {}}
        return tok

    def barrier(self):
        deps = []
        for g in self.recs.values():
            for rec in g.values():
                if rec['w'] is not None:
                    deps.append(rec['w'])
                deps.extend(rec['r'].values())
        for e in self.eng:
            self._wait(e, deps)
        self.recs = {}

    def wait_all(self, e):
        deps = []
        for g in self.recs.values():
            for rec in g.values():
                if rec['w'] is not None:
                    deps.append(rec['w'])
                deps.extend(rec['r'].values())
        self._wait(e, deps)


class Prog:
    def __init__(self, dbg=None):
        self.dbg = dbg
        self.nc = bass.Bass("TRN2", target_bir_lowering=False)
        self.ins = {}
        self.outs = {}

    def din(self, name, shape, dt=F32):
        t = self.nc.dram_tensor(name, list(shape), dt, kind="ExternalInput")
        self.ins[name] = t
        return t

    def dout(self, name, shape, dt=F32):
        t = self.nc.dram_tensor(name, list(shape), dt, kind="ExternalOutput")
        self.outs[name] = t
        return t


class Builder:
    def __init__(self, P, stack):
        self.P = P
        nc = self.nc = P.nc
        self.stack = stack
        self.tk = TK(nc, stack)
        sb = lambda name, shape, dt: stack.enter_context(nc.sbuf_tensor("s_" + name, list(shape), dt))
        self.sb = sb
        self.resid = sb("resid", [128, KC, NCOL], F32)
        self.xn = sb("xn", [128, KC, NCOL], BF16)
        self.onesb = sb("onesb", [128, 128], BF16)
        self.ngs = sb("ngs", [128, 2 * 4 * KC], F32)
        self.epsb = sb("epsb", [128, 1], F32)
        AR = 90112
        self.arena = sb("arena", [128, AR // 4], F32)

        def carve(off, shape, dt):
            n = 1
            for d in shape[1:]:
                n *= d
            esz = 4 if dt == F32 else 2
            assert off % 4 == 0 and off + n * esz <= AR, (off, shape)
            ap = self.arena[:, off // 4:(off + n * esz + 3) // 4]
            if dt != F32:
                ap = ap.bitcast(dt)
                ap = ap[:, 0:n]
            if len(shape) == 3:
                ap = ap.rearrange("p (a b) -> p a b", b=shape[2])
            elif len(shape) == 4:
                ap = ap.rearrange("p (a b c) -> p a b c", b=shape[2], c=shape[3])
            return ap
        self.carve = carve
        self.hid = carve(0, [128, KC, NCOL], BF16)
        self.wst = carve(33664, [128, 2, 2048], F32)
        self.wbf = carve(50048, [128, 3, 2048], BF16)
        self.sq = carve(62336, [128, KC, 390], BF16)
        self.tmp = carve(68576, [128, 2, 390], F32)
        self.rstd = carve(71696, [128, 390], F32)
        self.wdg = carve(73256, [128, 6, 1024], BF16)
        self.ub = carve(73256, [128, NCOL], F32)
        self.yb = carve(81672, [128, NCOL], F32)
        self.ps = [stack.enter_context(nc.psum_tensor(f"ps{i}", [128, 512], F32)) for i in range(8)]
        self.bank = 0
        self.wslot = 0
        self.sslot = 0
        self.tmpi = 0

    def nb(self):
        b = self.bank
        self.bank = (b + 1) % 8
        return b

    def ns(self):
        q = self.sslot
        self.sslot = (q + 1) % 2
        return q

    def nt(self):
        t = self.tmpi
        self.tmpi = (t + 1) % 2
        return t

    def load_w(self, src, n, g=None, m=None, parts=None):
        tk = self.tk
        s = self.wslot
        self.wslot = (s + 1) % 3
        q = self.ns()
        wst, wbf = self.wst, self.wbf
        tk.op('sp', lambda e: e.dma_start(out=wst[:, q, 0:n], in_=src), writes=[('wst', q)], dma=True)
        if parts is None:
            parts = [(0, n, g, m)]
        for (a, b, gg, mm) in parts:
            if gg is None:
                tk.op('dve', lambda e: e.tensor_copy(wbf[:, s, a:b], wst[:, q, a:b]),
                      reads=[('wst', q)], writes=[('wbf', s, a)])
            else:
                tk.op('dve', lambda e: e.tensor_tensor(
                    wbf[:, s, a:b].rearrange("p (c m) -> p c m", m=mm),
                    wst[:, q, a:b].rearrange("p (c m) -> p c m", m=mm),
                    bc(gg, 2, mm), ALU.mult),
                    reads=[('wst', q), ('ngs',)], writes=[('wbf', s, a)])
        return s

    def stream(self, blocks, body):
        pend = [self.load_w(*blocks[0])]
        for i in range(len(blocks)):
            if i + 1 < len(blocks):
                pend.append(self.load_w(*blocks[i + 1]))
            body(i, pend.pop(0))

    def gain(self, l, i):
        o = (l * 4 + i) * KC
        return self.ngs[:, o:o + KC]

    def norm(self):
        tk = self.tk
        resid, xn, sq, rstd, ps = self.resid, self.xn, self.sq, self.rstd, self.ps
        for ci, (a, b) in enumerate(CTS):
            n = b - a
            tk.op('act', lambda e: e.activation(sq[:, :, 0:n], resid[:, :, a:b], AF.Square),
                  reads=[('resid', ci)], writes=[('sq',)])
            bk = self.nb()
            for c in range(KC):
                tk.op('pe', lambda e: e.matmul(ps[bk][:, 0:n], self.onesb[:, :], sq[:, c, 0:n],
                                               start=(c == 0), stop=(c == KC - 1)),
                      reads=[('sq',), ('onesb',)], writes=[('ps', bk)])
            tk.op('act', lambda e: e.activation(rstd[:, 0:n], ps[bk][:, 0:n], AF.Ln, bias=self.epsb[:, :]),
                  reads=[('ps', bk), ('epsb',)], writes=[('rstd',)])
            tk.op('act', lambda e: e.activation(rstd[:, 0:n], rstd[:, 0:n], AF.Exp, scale=-0.5),
                  reads=[('rstd',)], writes=[('rstd',)])
            tk.op('pool', lambda e: e.tensor_tensor(xn[:, :, a:b], resid[:, :, a:b], bc(rstd[:, 0:n], 1, KC), ALU.mult),
                  reads=[('resid', ci), ('rstd',)], writes=[('xn', ci)])

    def ffn(self, l, f):
        tk = self.tk
        P = self.P
        resid, xn, hid, wbf, wdg, wst, tmp, ps = self.resid, self.xn, self.hid, self.wbf, self.wdg, self.wst, self.tmp, self.ps
        self.norm()
        g = self.gain(l, 2 * f)
        wgu = P.ins['wgu']
        wdn = P.ins['wdn']
        for (h0, h1) in HGROUPS:
            blocks = [(wgu[l, f, j], KC * 256, g, 256) for j in range(h0, h1)]

            def body(i, s, h0=h0):
                j = h0 + i
                for ci, (a, b) in enumerate(CTS):
                    n = b - a
                    bg, bu = self.nb(), self.nb()
                    for c in range(KC):
                        tk.op('pe', lambda e: e.matmul(ps[bg][:, 0:n], wbf[:, s, c * 256:c * 256 + 128], xn[:, c, a:b],
                                                       start=(c == 0), stop=(c == KC - 1)),
                              reads=[('wbf', s), ('xn', ci)], writes=[('ps', bg)])
                    for c in range(KC):
                        tk.op('pe', lambda e: e.matmul(ps[bu][:, 0:n], wbf[:, s, c * 256 + 128:c * 256 + 256], xn[:, c, a:b],
                                                       start=(c == 0), stop=(c == KC - 1)),
                              reads=[('wbf', s), ('xn', ci)], writes=[('ps', bu)])
                    t = self.nt()
                    tk.op('act', lambda e: e.activation(tmp[:, t, 0:n], ps[bg][:, 0:n], AF.Silu),
                          reads=[('ps', bg)], writes=[('tmp', t)])
                    tk.op('dve', lambda e: e.tensor_tensor(hid[:, i, a:b], tmp[:, t, 0:n], ps[bu][:, 0:n], ALU.mult),
                          reads=[('tmp', t), ('ps', bu)], writes=[('hid', i, ci)])
                s2 = self.ns()
                tk.op('sp', lambda e: e.dma_start(out=wst[:, s2, 0:1024], in_=wdn[l, f, j]), writes=[('wst', s2)], dma=True)
                tk.op('dve', lambda e: e.tensor_copy(wdg[:, i, :], wst[:, s2, 0:1024]),
                      reads=[('wst', s2)], writes=[('wdg', i)])

            self.stream(blocks, body)
            ng_ = h1 - h0
            for m in range(KC):
                for ci, (a, b) in enumerate(CTS):
                    n = b - a
                    bk = self.nb()
                    for i in range(ng_):
                        tk.op('pe', lambda e: e.matmul(ps[bk][:, 0:n], wdg[:, i, m * 128:(m + 1) * 128], hid[:, i, a:b],
                                                       start=(i == 0), stop=(i == ng_ - 1)),
                              reads=[('wdg', i), ('hid', i, ci)], writes=[('ps', bk)])
                    tk.op('dve', lambda e: e.scalar_tensor_tensor(resid[:, m, a:b], ps[bk][:, 0:n], 0.5, resid[:, m, a:b],
                                                                  ALU.mult, ALU.add),
                          reads=[('ps', bk), ('resid', ci, m)], writes=[('resid', ci, m)])

    def ple(self, l):
        tk = self.tk
        P = self.P
        resid, xn, hid, wbf, wst, tmp, ps = self.resid, self.xn, self.hid, self.wbf, self.wst, self.tmp, self.ps
        self.norm()
        g = self.gain(l, 3)
        pT = P.ins['pT']
        for ci, (a, b) in enumerate(CTS):
            n = b - a
            s = self.ns()
            tk.op('sp', lambda e: e.dma_start(out=wst[:, s, 0:2 * n].rearrange("p (c n) -> p c n", c=2), in_=pT[l, :, :, a:b]),
                  writes=[('wst', s)], dma=True)
            tk.op('dve', lambda e: e.tensor_copy(hid[:, 0:2, a:b], wst[:, s, 0:2 * n].rearrange("p (c n) -> p c n", c=2)),
                  reads=[('wst', s)], writes=[('hid', 0, ci), ('hid', 1, ci)])
        wpg = P.ins['wpg']
        blocks = [(wpg[l, m], 1280, None, None, [(0, 1024, g, 128), (1024, 1280, None, None)]) for m in range(KC)]

        def body(m, s):
            for ci, (a, b) in enumerate(CTS):
                n = b - a
                bg, bp = self.nb(), self.nb()
                for c in range(KC):
                    tk.op('pe', lambda e: e.matmul(ps[bg][:, 0:n], wbf[:, s, c * 128:(c + 1) * 128], xn[:, c, a:b],
                                                   start=(c == 0), stop=(c == KC - 1)),
                          reads=[('wbf', s), ('xn', ci)], writes=[('ps', bg)])
                for c in range(2):
                    tk.op('pe', lambda e: e.matmul(ps[bp][:, 0:n], wbf[:, s, 1024 + c * 128:1024 + (c + 1) * 128], hid[:, c, a:b],
                                                   start=(c == 0), stop=(c == 1)),
                          reads=[('wbf', s), ('hid', c, ci)], writes=[('ps', bp)])
                t = self.nt()
                tk.op('act', lambda e: e.activation(tmp[:, t, 0:n], ps[bg][:, 0:n], AF.Sigmoid),
                      reads=[('ps', bg)], writes=[('tmp', t)])
                tk.op('dve', lambda e: e.tensor_tensor(tmp[:, t, 0:n], tmp[:, t, 0:n], ps[bp][:, 0:n], ALU.mult),
                      reads=[('tmp', t), ('ps', bp)], writes=[('tmp', t)])
                tk.op('pool', lambda e: e.tensor_tensor(resid[:, m, a:b], resid[:, m, a:b], tmp[:, t, 0:n], ALU.add),
                      reads=[('tmp', t), ('resid', ci, m)], writes=[('resid', ci, m)])

        self.stream(blocks, body)

    def linear_resid(self, wtiles, g):
        tk = self.tk
        resid, hid, wbf, ps = self.resid, self.hid, self.wbf, self.ps
        blocks = [(wtiles[m], KC * 128, g, 128) for m in range(KC)]

        def body(m, s):
            for ci, (a, b) in enumerate(CTS):
                n = b - a
                bk = self.nb()
                for c in range(KC):
                    tk.op('pe', lambda e: e.matmul(ps[bk][:, 0:n], wbf[:, s, c * 128:(c + 1) * 128], hid[:, c, a:b],
                                                   start=(c == 0), stop=(c == KC - 1)),
                          reads=[('wbf', s), ('hid', c, ci)], writes=[('ps', bk)])
                tk.op('dve', lambda e: e.tensor_tensor(resid[:, m, a:b], resid[:, m, a:b], ps[bk][:, 0:n], ALU.add),
                      reads=[('ps', bk), ('resid', ci, m)], writes=[('resid', ci, m)])

        self.stream(blocks, body)

    def conv(self):
        tk = self.tk
        P = self.P
        resid, xn, hid, wbf, tmp, ps = self.resid, self.xn, self.hid, self.wbf, self.tmp, self.ps
        ub, yb, cws, hms, sts, cvo = self.ub, self.yb, self.cws, self.hms, self.sts, self.cvo
        self.norm()
        g = self.gain(0, 1)
        cwin = P.ins['cwin']
        blocks = [(cwin[i], KC * 128, g, 128) for i in range(24)]

        def mm(s, ci, a, b):
            n = b - a
            bk = self.nb()
            for c in range(KC):
                tk.op('pe', lambda e: e.matmul(ps[bk][:, 0:n], wbf[:, s, c * 128:(c + 1) * 128], xn[:, c, a:b],
                                               start=(c == 0), stop=(c == KC - 1)),
                      reads=[('wbf', s), ('xn', ci)], writes=[('ps', bk)])
            return bk

        def body(i, s):
            f, kind = i // 3, i % 3
            if kind == 0:
                for ci, (a, b) in enumerate(CTS):
                    bk = mm(s, ci, a, b)
                    tk.op('act', lambda e: e.activation(ub[:, a:b], ps[bk][:, 0:b - a], AF.Copy),
                          reads=[('ps', bk)], writes=[('ub', ci)])
            elif kind == 1:
                for ci, (a, b) in enumerate(CTS):
                    bk = mm(s, ci, a, b)
                    tk.op('dve', lambda e: e.tensor_tensor(ub[:, a:b], ub[:, a:b], ps[bk][:, 0:b - a], ALU.mult),
                          reads=[('ps', bk), ('ub', ci)], writes=[('ub', ci)])
                uh = ub[:, 0:PCOL].rearrange("p (k w) -> p k w", w=TW)[:, :, 0:2]
                tk.op('dve', lambda e: e.tensor_tensor(uh, uh, bc(hms[:, :], 2, 2), ALU.mult),
                      reads=[('ub',), ('hms',)], writes=[('ub',)])
                us = ub[:, PCOL:NCOL].rearrange("p (s w) -> p s w", w=SW)[:, :, 0:2]
                tk.op('dve', lambda e: e.tensor_copy(us, sts[:, f, :, :]),
                      reads=[('sts',)], writes=[('ub',)])
                tk.op('dve', lambda e: e.tensor_copy(cvo[:, f, 0:2], ub[:, PCOL - 2:PCOL]),
                      reads=[('ub',)], writes=[('cvo', f)])
                usn = ub[:, PCOL:NCOL].rearrange("p (s w) -> p s w", w=SW)[:, :, 4:6]
                tk.op('dve', lambda e: e.tensor_copy(cvo[:, f, 2:2 + 2 * NSEQ].rearrange("p (s w) -> p s w", w=2), usn),
                      reads=[('ub',)], writes=[('cvo', f)])
                n2 = NCOL - 2
                tk.op('dve', lambda e: e.tensor_scalar(yb[:, 2:NCOL], ub[:, 2:NCOL], cws[:, 2 * KC + f:2 * KC + f + 1], None, ALU.mult),
                      reads=[('ub',), ('cws',)], writes=[('yb',)])
                tk.op('dve', lambda e: e.scalar_tensor_tensor(yb[:, 2:NCOL], ub[:, 1:NCOL - 1], cws[:, KC + f:KC + f + 1], yb[:, 2:NCOL],
                                                               ALU.mult, ALU.add),
                      reads=[('ub',), ('cws',), ('yb',)], writes=[('yb',)])
                tk.op('dve', lambda e: e.scalar_tensor_tensor(yb[:, 2:NCOL], ub[:, 0:n2], cws[:, f:f + 1], yb[:, 2:NCOL],
                                                               ALU.mult, ALU.add),
                      reads=[('ub',), ('cws',), ('yb',)], writes=[('yb',)])
            else:
                for ci, (a, b) in enumerate(CTS):
                    bk = mm(s, ci, a, b)
                    tk.op('dve', lambda e: e.tensor_tensor(hid[:, f, a:b], yb[:, a:b], ps[bk][:, 0:b - a], ALU.mult),
                          reads=[('ps', bk), ('yb',)], writes=[('hid', f, ci)])

        self.stream(blocks, body)
        cwout = P.ins['cwout']
        self.linear_resid([cwout[m] for m in range(KC)], None)


    def nsa_proj(self):
        tk = self.tk
        P = self.P
        xn, hid, wst, tmp, ps, rstd = self.xn, self.hid, self.wst, self.tmp, self.ps, self.rstd
        self.norm()
        g = self.gain(1, 1)
        nwkv = P.ins['nwkv']
        NKV = 1584
        hf = hid[:, :, :].rearrange("p c n -> p (c n)")
        for c in range(KC):
            q = self.ns()
            tk.op('sp', lambda e: e.dma_start(out=wst[:, q, 0:NKV], in_=nwkv[:, c, :]), writes=[('wst', q)], dma=True)
            tk.op('dve', lambda e: e.tensor_scalar(hf[:, c * NKV:(c + 1) * NKV], wst[:, q, 0:NKV], g[:, c:c + 1], None, ALU.mult),
                  reads=[('wst', q), ('ngs',)], writes=[('hid',)])
        kvo = [self.ub, self.yb]
        kvk = [('ub',), ('yb',)]
        s1 = rstd[:, 0:128]
        s2 = rstd[:, 128:256]
        st4 = self.st4
        o_kvp, o_kvs = P.outs['o_kvp'], P.outs['o_kvs']
        for ti in range(NT + 1):
            if ti < NT:
                c0, R = ti * TW + 2, 128
            else:
                c0, R = PCOL, NSEQ * SW
            ko = kvo[ti % 2]
            kk = kvk[ti % 2]
            banks = []
            for (a, b) in [(0, 512), (512, 1024), (1024, 1536), (1536, NKV)]:
                bk = self.nb()
                banks.append(bk)
                for c in range(KC):
                    tk.op('pe', lambda e: e.matmul(ps[bk][0:R, 0:b - a], xn[:, c, c0:c0 + R], hf[:, c * NKV + a:c * NKV + b],
                                                   start=(c == 0), stop=(c == KC - 1)),
                          reads=[('hid',), ('xn',)], writes=[('ps', bk)])
            bA, bB, bC, bD = banks
            tk.op('act', lambda e: e.activation(ko[0:R, 0:512], ps[bA][0:R, 0:512], AF.Copy), reads=[('ps', bA)], writes=[kk + (0,)])
            tk.op('act', lambda e: e.activation(ko[0:R, 768:1024], ps[bB][0:R, 256:512], AF.Copy), reads=[('ps', bB)], writes=[kk + (3,)])
            tk.op('act', lambda e: e.activation(ko[0:R, 1280:1536], ps[bC][0:R, 256:512], AF.Copy), reads=[('ps', bC)], writes=[kk + (5,)])
            tk.op('act', lambda e: e.activation(self.gat[0:R, ti, :], ps[bD][0:R, 0:48], AF.Sigmoid), reads=[('ps', bD)], writes=[('gat', ti)])
            for (bk, gi, oc, part) in [(bB, 2, 512, 2), (bC, 3, 1024, 4)]:
                x = ps[bk][0:R, 0:256]
                x3 = x.rearrange("p (g d) -> p g d", d=64)
                tk.op('act', lambda e: e.activation(tmp[0:R, 0, 0:256], x, AF.Square), reads=[('ps', bk)], writes=[('tmp', 0)])
                tk.op('dve', lambda e: e.tensor_reduce(st4[0:R, 0:4], tmp[0:R, 0, 0:256].rearrange("p (g d) -> p g d", d=64), AX.X, ALU.add),
                      reads=[('tmp', 0)], writes=[('st4',)])
                tk.op('act', lambda e: e.activation(st4[0:R, 4:8], st4[0:R, 0:4], AF.Ln, bias=self.epsb[0:R, :], scale=1.0 / 64),
                      reads=[('st4',), ('epsb',)], writes=[('st4',)])
                tk.op('act', lambda e: e.activation(st4[0:R, 4:8], st4[0:R, 4:8], AF.Exp, scale=-0.5),
                      reads=[('st4',)], writes=[('st4',)])
                t1 = tmp[0:R, 1, 0:256].rearrange("p (g d) -> p g d", d=64)
                tk.op('dve', lambda e: e.tensor_tensor(t1, x3, bc(st4[0:R, 4:8], 2, 64), ALU.mult),
                      reads=[('ps', bk), ('st4',)], writes=[('tmp', 1)])
                tk.op('dve', lambda e: e.tensor_tensor(t1, t1, bc(self.qkg[0:R, gi, :], 1, 4), ALU.mult),
                      reads=[('tmp', 1), ('qkg',)], writes=[('tmp', 1)])
                x1, x2 = t1[:, :, 0:32], t1[:, :, 32:64]
                cosb = bc(self.cs[0:R, ti, 0:32], 1, 4)
                sinb = bc(self.cs[0:R, ti, 32:64], 1, 4)
                o3 = ko[0:R, oc:oc + 256].rearrange("p (g d) -> p g d", d=64)
                a1 = s1[0:R, :].rearrange("p (g d) -> p g d", d=32)
                a2 = s2[0:R, :].rearrange("p (g d) -> p g d", d=32)
                rd = [('tmp', 1), ('cs',)]
                tk.op('pool', lambda e: e.tensor_tensor(a1, x1, cosb, ALU.mult), reads=rd, writes=[('rstd', 0)])
                tk.op('pool', lambda e: e.tensor_tensor(a2, x2, sinb, ALU.mult), reads=rd, writes=[('rstd', 1)])
                tk.op('pool', lambda e: e.tensor_tensor(o3[:, :, 0:32], a1, a2, ALU.subtract), reads=[('rstd',)], writes=[kk + (part,)])
                tk.op('pool', lambda e: e.tensor_tensor(a1, x2, cosb, ALU.mult), reads=rd, writes=[('rstd', 0)])
                tk.op('pool', lambda e: e.tensor_tensor(a2, x1, sinb, ALU.mult), reads=rd, writes=[('rstd', 1)])
                tk.op('pool', lambda e: e.tensor_tensor(o3[:, :, 32:64], a1, a2, ALU.add), reads=[('rstd',)], writes=[kk + (part + 100,)])
            if ti < NT:
                tk.op('pool', lambda e: e.dma_start(out=o_kvp[ti, :, :], in_=ko[0:R, 0:1536]), reads=[kk], writes=[('o_kvp', ti)], dma=True)
            else:
                tk.op('pool', lambda e: e.dma_start(out=o_kvs[:, :], in_=ko[0:R, 0:1536]), reads=[kk], writes=[('o_kvs',)], dma=True)
                for sq_ in range(NSEQ):
                    r0 = sq_ * SW + 2
                    tk.op('pool', lambda e: e.dma_start(out=P.outs['o_kws'][sq_, 508:512, :], in_=ko[r0:r0 + 4, 1024:1280]),
                          reads=[kk], writes=[('o_kws', sq_)], dma=True)
                    tk.op('pool', lambda e: e.dma_start(out=P.outs['o_vws'][sq_, 508:512, :], in_=ko[r0:r0 + 4, 1280:1536]),
                          reads=[kk], writes=[('o_vws', sq_)], dma=True)


    def stage_cast(self, src, dst, n, eng, stg, in1=None):
        tk = self.tk
        q = self.ns()
        tk.op('sp', lambda e: e.dma_start(out=stg[:, q, 0:n], in_=src), writes=[('stg', q)], dma=True)
        return q

    def compress(self, cstop=9, slot=0, seq=None):
        tk = self.tk
        P = self.P
        ps = self.ps
        cv = self.carve
        w1b = cv(0, [128, 64, 128], BF16)
        kcb = cv(16384, [128, 64, 256], BF16)
        stg = cv(49152, [128, 2, 2048], F32)
        gel = cv(65536, [128, 512], BF16)
        wk = cv(66560, [128, 3, 512], F32)
        cmt = cv(72704, [128, 256], F32)
        w2b = cv(73728, [128, 2, 64], BF16)
        st4 = self.st4
        w1r, kcr, w2s = P.ins['w1r'], P.ins['kcr'], P.ins['w2s']
        q = self.ns()
        tk.op('sp', lambda e: e.dma_start(out=stg[:, q, 0:128], in_=w2s[:, :]), writes=[('stg', q)], dma=True)
        tk.op('dve', lambda e: e.tensor_copy(w2b[:, :, :].rearrange("p a b -> p (a b)"), stg[:, q, 0:128]), reads=[('stg', q)], writes=[('w2b',)])
        w1f = w1b[:, :, :].rearrange("p a b -> p (a b)")
        for kv in range(2):
            for pc in range(4):
                q = self.ns()
                tk.op('sp', lambda e: e.dma_start(out=stg[:, q, :], in_=w1r[kv, :, pc * 2048:(pc + 1) * 2048]), writes=[('stg', q)], dma=True)
                tk.op('dve', lambda e: e.tensor_copy(w1f[:, pc * 2048:(pc + 1) * 2048], stg[:, q, :]), reads=[('stg', q)], writes=[('w1b', pc)])
            if seq is None:
                for pc in range(8):
                    q = self.ns()
                    tk.op('sp', lambda e: e.dma_start(out=stg[:, q, :].rearrange("p (t c) -> p t c", c=256), in_=kcr[kv, :, pc * 8:(pc + 1) * 8, :]),
                          writes=[('stg', q)], dma=True)
                    tk.op('pool', lambda e: e.tensor_tensor(
                        kcb[:, pc * 8:(pc + 1) * 8, :].rearrange("p t (g d) -> p t g d", d=64),
                        stg[:, q, :].rearrange("p (t g d) -> p t g d", g=4, d=64),
                        bc(bc(self.posr[:, kv, :], 1, 4), 1, 8), ALU.add),
                        reads=[('stg', q), ('posr',)], writes=[('kcb', pc)])
            else:
                pool_d = P.ins['pool_kc' if kv == 0 else 'pool_vc']
                sg = stg[:, :, :].rearrange("p a (b c) -> p (a b) c", c=256)
                for page in range(64):
                    q = page % 16
                    col = seq * 64 + page
                    tk.op('pool', lambda e: e.indirect_dma_start(out=sg[:, q, :], out_offset=None, in_=pool_d[:, :],
                                                                 in_offset=bass.IndirectOffsetOnAxis(ap=self.idx[:, col:col + 1], axis=0)),
                          reads=[('idx',)], writes=[('stg', q // 8, q % 8)], dma=True)
                    tk.op('dve', lambda e: e.tensor_tensor(
                        kcb[:, page, :].rearrange("p (g d) -> p g d", d=64),
                        sg[:, q, :].rearrange("p (g d) -> p g d", d=64),
                        bc(self.posr[:, kv, :], 1, 4), ALU.add),
                        reads=[('stg', q // 8, q % 8), ('posr',)], writes=[('kcb', page)])
            if cstop <= 1:
                continue
            for e_ in range(2):
                for g in range(4):
                    po = ps[1][:, g * 128:(g + 1) * 128].rearrange("p (t e) -> p t e", e=2)[:, :, e_]
                    for d in range(64):
                        tk.op('pe', lambda e: e.matmul(po, w1b[64 * e_:64 * e_ + 64, d, :],
                                                       kcb[64 * e_:64 * e_ + 64, :, g * 64 + d], start=(d == 0), stop=(d == 63)),
                              reads=[('w1b',), ('kcb',)], writes=[('ps', 1)], pg=(64 * e_, 64))
            if cstop <= 2:
                continue
            x = ps[1][:, 0:512]
            tk.op('act', lambda e: e.activation(wk[:, 0, :], x, AF.Square), reads=[('ps', 1)], writes=[('wk', 0)])
            tk.op('dve', lambda e: e.tensor_scalar(wk[:, 0, :], wk[:, 0, :], 0.044715, 1.0, ALU.mult, ALU.add), reads=[('wk', 0)], writes=[('wk', 0)])
            tk.op('dve', lambda e: e.tensor_tensor(wk[:, 1, :], wk[:, 0, :], x, ALU.mult), reads=[('wk', 0), ('ps', 1)], writes=[('wk', 1)])
            tk.op('act', lambda e: e.activation(wk[:, 2, :], wk[:, 1, :], AF.Sigmoid, scale=1.5957691216057308), reads=[('wk', 1)], writes=[('wk', 2)])
            tk.op('dve', lambda e: e.tensor_tensor(gel[:, :], wk[:, 2, :], x, ALU.mult), reads=[('wk', 2), ('ps', 1)], writes=[('gel',)])
            if cstop <= 3:
                continue
            for g in range(4):
                tk.op('pe', lambda e: e.matmul(ps[2][:, g * 64:(g + 1) * 64], gel[:, g * 128:(g + 1) * 128], w2b[:, kv, :],
                                               start=True, stop=True),
                      reads=[('gel',), ('w2b',)], writes=[('ps', 2)])
            if cstop <= 4:
                continue
            y = ps[2][:, 0:256]
            if kv == 0:
                tk.op('act', lambda e: e.activation(wk[:, 0, 0:256], y, AF.Square), reads=[('ps', 2)], writes=[('wk', 0)])
                tk.op('dve', lambda e: e.tensor_reduce(st4[:, 0:4], wk[:, 0, 0:256].rearrange("p (g d) -> p g d", d=64), AX.X, ALU.add),
                      reads=[('wk', 0)], writes=[('st4',)])
                tk.op('act', lambda e: e.activation(st4[:, 4:8], st4[:, 0:4], AF.Ln, bias=self.epsb[:, :], scale=1.0 / 64),
                      reads=[('st4',), ('epsb',)], writes=[('st4',)])
                tk.op('act', lambda e: e.activation(st4[:, 4:8], st4[:, 4:8], AF.Exp, scale=-0.5), reads=[('st4',)], writes=[('st4',)])
                c3 = cmt[:, :].rearrange("p (g d) -> p g d", d=64)
                tk.op('dve', lambda e: e.tensor_tensor(c3, y.rearrange("p (g d) -> p g d", d=64), bc(st4[:, 4:8], 2, 64), ALU.mult),
                      reads=[('ps', 2), ('st4',)], writes=[('cmt',)])
                tk.op('dve', lambda e: e.tensor_tensor(c3, c3, bc(self.qkg[:, 1, :], 1, 4), ALU.mult), reads=[('cmt',), ('qkg',)], writes=[('cmt',)])
                for gp in range(2):
                    tk.op('pe', lambda e: e.transpose(ps[3][:, gp * 128:(gp + 1) * 128], cmt[:, gp * 128:(gp + 1) * 128], self.ident[:, :]),
                          reads=[('cmt',), ('ident',)], writes=[('ps', 3)])
                tk.op('act', lambda e: e.activation(self.kcT[:, :, :].rearrange("p a b -> p (a b)"), ps[3][:, 0:256], AF.Copy),
                      reads=[('ps', 3)], writes=[('kcT',)])
            else:
                tk.op('act', lambda e: e.activation(self.vcb[:, :], y, AF.Copy), reads=[('ps', 2)], writes=[('vcb',)])
        if cstop >= 9:
            tk.op('sp', lambda e: e.dma_start(out=self.d_kc[slot, :, :], in_=self.kcT[:, :, :].rearrange("p a b -> p (a b)")),
                  reads=[('kcT',)], writes=[('d_kc', slot)], dma=True)
            tk.op('sp', lambda e: e.dma_start(out=self.d_vc[slot, :, :], in_=self.vcb[:, :]), reads=[('vcb',)], writes=[('d_vc', slot)], dma=True)

    def qk_norm_rope(self, src, R, H, gi, ti, dst0, dst1, scr_sq, sm, a1, a2, scale, k0, k1, ka):
        tk = self.tk
        x3 = src.rearrange("p (g d) -> p g d", d=64)
        tk.op('act', lambda e: e.activation(scr_sq, src, AF.Square), reads=[ka], writes=[k1])
        tk.op('dve', lambda e: e.tensor_reduce(sm[0:R, 0:H], scr_sq.rearrange("p (g d) -> p g d", d=64), AX.X, ALU.add), reads=[k1], writes=[('sm', 0)])
        tk.op('act', lambda e: e.activation(sm[0:R, H:2 * H], sm[0:R, 0:H], AF.Ln, bias=self.epsb[0:R, :], scale=1.0 / 64),
              reads=[('sm', 0), ('epsb',)], writes=[('sm', 1)])
        tk.op('act', lambda e: e.activation(sm[0:R, H:2 * H], sm[0:R, H:2 * H], AF.Exp, scale=-0.5), reads=[('sm', 1)], writes=[('sm', 1)])
        d0 = dst0.rearrange("p (g d) -> p g d", d=64)
        d1 = dst1.rearrange("p (g d) -> p g d", d=64)
        tk.op('dve', lambda e: e.tensor_tensor(d0, x3, bc(sm[0:R, H:2 * H], 2, 64), ALU.mult), reads=[ka, ('sm', 1)], writes=[k0])
        tk.op('dve', lambda e: e.scalar_tensor_tensor(d0, d0, scale, bc(self.qkg[0:R, gi, :], 1, H), ALU.mult, ALU.mult),
              reads=[k0, ('qkg',)], writes=[k0])
        x1, x2 = d0[:, :, 0:32], d0[:, :, 32:64]
        cosb = bc(self.cs[0:R, ti, 0:32], 1, H)
        sinb = bc(self.cs[0:R, ti, 32:64], 1, H)
        b1 = a1.rearrange("p (g d) -> p g d", d=32)
        b2 = a2.rearrange("p (g d) -> p g d", d=32)
        rd = [k0, ('cs',)]
        tk.op('pool', lambda e: e.tensor_tensor(b1, x1, cosb, ALU.mult), reads=rd, writes=[('ow', 0)])
        tk.op('pool', lambda e: e.tensor_tensor(b2, x2, sinb, ALU.mult), reads=rd, writes=[('ow', 1)])
        tk.op('pool', lambda e: e.tensor_tensor(d1[:, :, 0:32], b1, b2, ALU.subtract), reads=[('ow',)], writes=[k1 + (0,)])
        tk.op('pool', lambda e: e.tensor_tensor(b1, x2, cosb, ALU.mult), reads=rd, writes=[('ow', 0)])
        tk.op('pool', lambda e: e.tensor_tensor(b2, x1, sinb, ALU.mult), reads=rd, writes=[('ow', 1)])
        tk.op('pool', lambda e: e.tensor_tensor(d1[:, :, 32:64], b1, b2, ALU.add), reads=[('ow',)], writes=[k1 + (1,)])

    def attention(self, gp, nk=NT):
        tk = self.tk
        P = self.P
        ps, xn, resid = self.ps, self.xn, self.resid
        cv = self.carve
        Wq = cv(0, [128, 8, 512], BF16)
        Wo = cv(8192, [128, 4, 1024], BF16)
        stg = cv(16384, [128, 2, 512], F32)
        qz = cv(20480, [128, 2, 2, 512], BF16)
        qsz = cv(24576, [128, 2, 2, 16], BF16)
        mbzs = cv(24832, [128, 4, 16], BF16)
        Es4 = cv(25088, [128, 2, 64], BF16)
        rmask = self.rmask
        KT = cv(32768, [128, 8192], BF16)
        V = cv(49152, [128, 64, 2, 65], BF16)
        kwT = cv(65792, [128, 8, 128], BF16)
        vw = cv(67840, [128, 8, 2, 65], BF16)
        qf = cv(69920, [128, 2, 512], F32)
        qT = cv(74016, [128, 2, 4, 128], BF16)
        cE = cv(76064, [128, 4, 128], F32)
        mk = cv(78112, [128, 4, 128], F32)
        mbz = cv(69920, [128, 4, 512], BF16)
        E = cv(81184, [128, 2, 512], BF16)
        PT = cv(83232, [128, 4, 128], BF16)
        oacc = cv(84256, [128, 512], F32)
        ow = cv(86304, [128, 2, 256], F32)
        oT = cv(88352, [128, 4, 128], BF16)
        sm = cv(89376, [128, 32], F32)
        osw = cv(28672, [128, 2, 260], F32)
        g11 = self.gain(1, 1)
        nwq, nwo, kts, vss, ktw, vws = [P.ins[k] for k in ('nwq', 'nwo', 'kts', 'vss', 'ktw', 'vws')]
        arow, tabs, dm, wm, selc, gat, kcT, vcb = self.arow, self.tabs, self.dm, self.wm, self.selc, self.gat, self.kcT, self.vcb
        for h in range(8):
            q = self.ns()
            tk.op('sp', lambda e: e.dma_start(out=stg[:, q, :], in_=nwq[gp, :, h, :]), writes=[('stg', q)], dma=True)
            tk.op('dve', lambda e: e.tensor_scalar(Wq[:, h, :], stg[:, q, :], g11[:, h:h + 1], None, ALU.mult),
                  reads=[('stg', q), ('ngs',)], writes=[('Wq', h)])
            q = self.ns()
            tk.op('sp', lambda e: e.dma_start(out=stg[:, q, :], in_=nwo[gp, :, h // 2, (h % 2) * 512:(h % 2 + 1) * 512]), writes=[('stg', q)], dma=True)
            tk.op('pool', lambda e: e.tensor_copy(Wo[:, h // 2, (h % 2) * 512:(h % 2 + 1) * 512], stg[:, q, :]), reads=[('stg', q)], writes=[('Wo', h)])
        for c8 in range(16):
            q = self.ns()
            tk.op('sp', lambda e: e.dma_start(out=stg[:, q, :], in_=kts[gp, :, c8 * 512:(c8 + 1) * 512]), writes=[('stg', q)], dma=True)
            tk.op('dve', lambda e: e.tensor_copy(KT[:, c8 * 512:(c8 + 1) * 512], stg[:, q, :]), reads=[('stg', q)], writes=[('KT', c8)])
            q = self.ns()
            tk.op('sp', lambda e: e.dma_start(out=stg[:, q, :].rearrange("p (t c) -> p t c", c=128), in_=vss[:, c8 * 4:(c8 + 1) * 4, gp * 128:(gp + 1) * 128]),
                  writes=[('stg', q)], dma=True)
            tk.op('pool', lambda e: e.tensor_copy(V[:, c8 * 4:(c8 + 1) * 4, :, 0:64], stg[:, q, :].rearrange("p (t g d) -> p t g d", g=2, d=64)),
                  reads=[('stg', q)], writes=[('V', c8)])
        tk.op('sp', lambda e: e.dma_start(out=kcT[:, :, :].rearrange("p a b -> p (a b)"), in_=self.d_kc[0, :, :]), reads=[('d_kc', 0)], writes=[('kcT',)], dma=True)
        tk.op('sp', lambda e: e.dma_start(out=vcb[:, :], in_=self.d_vc[0, :, :]), reads=[('d_vc', 0)], writes=[('vcb',)], dma=True)
        tk.op('pool', lambda e: e.memset(V[:, :, :, 64:65], 1.0), writes=[('V1',)])
        tk.op('pool', lambda e: e.memset(vw[:, :, :, 64:65], 1.0), writes=[('vw1',)])
        for k in range(nk):
            c0 = k * TW + 2
            ci = k // 3
            m0 = 0 if k > 0 else 4
            j0 = 4 * k - 4
            for mh in range(m0 // 4, 2):
                ma = 4 * mh
                q = self.ns()
                tk.op('sp', lambda e: e.dma_start(out=stg[:, q, :], in_=ktw[gp, :, (j0 + ma) * 128:(j0 + ma + 4) * 128]), writes=[('stg', q)], dma=True)
                tk.op('dve', lambda e: e.tensor_copy(kwT[:, ma:ma + 4, :].rearrange("p a b -> p (a b)"), stg[:, q, :]), reads=[('stg', q)], writes=[('kwT', mh)])
                q = self.ns()
                tk.op('sp', lambda e: e.dma_start(out=stg[:, q, :].rearrange("p (t c) -> p t c", c=128),
                                                  in_=vws[:, j0 + ma:j0 + ma + 4, gp * 128:(gp + 1) * 128]), writes=[('stg', q)], dma=True)
                tk.op('pool', lambda e: e.tensor_copy(vw[:, ma:ma + 4, :, 0:64], stg[:, q, :].rearrange("p (t g d) -> p t g d", g=2, d=64)),
                      reads=[('stg', q)], writes=[('vw', mh)])
            for c in range(KC):
                tk.op('pe', lambda e: e.matmul(ps[0][:, 0:512], xn[:, c, c0:c0 + 128], Wq[:, c, :], start=(c == 0), stop=(c == KC - 1)),
                      reads=[('xn', ci), ('Wq',)], writes=[('ps', 0)])
            self.qk_norm_rope(ps[0][:, 0:512], 128, 8, 0, k, qf[:, 0, :], qf[:, 1, :], qf[:, 1, :], sm, ow[:, 0, :], ow[:, 1, :], 0.125,
                              ('qf', 0), ('qf', 1), ('ps', 0))
            for v in range(2):
                bkv = 0 if v == 0 else 2
                for t in range(4):
                    tk.op('pe', lambda e: e.transpose(ps[bkv][:, t * 128:(t + 1) * 128], qf[:, v, t * 128:(t + 1) * 128], self.ident[:, :]),
                          reads=[('qf', v), ('ident',)], writes=[('ps', bkv)])
                tk.op('act', lambda e: e.activation(qT[:, v, :, :].rearrange("p a b -> p (a b)"), ps[bkv][:, 0:512], AF.Copy),
                      reads=[('ps', bkv)], writes=[('qT', v)])
            for v in range(2):
                for gg in range(2):
                    tk.op('pool', lambda e: e.tensor_scalar(qz[:, v, gg, :], qT[:, v, :, :].rearrange("p a b -> p (a b)"), rmask[:, gg:gg + 1], None, ALU.mult),
                          reads=[('qT', v), ('rmask',)], writes=[('qz', v, gg)])
            for gg in range(2):
                g = 2 * gp + gg
                h0, h1 = 64 * gg, 64 * gg + 64
                for j in range(4):
                    tk.op('pe', lambda e: e.matmul(ps[1][:, j * 128:(j + 1) * 128], qz[:, 0, gg, j * 128:(j + 1) * 128], kcT[:, gp, :], start=True, stop=True),
                          reads=[('qz', 0, gg), ('kcT',)], writes=[('ps', 1)])
                cEf = cE[:, :, :].rearrange("p a b -> p (a b)")
                tk.op('act', lambda e: e.activation(cEf, ps[1][:, 0:512], AF.Exp), reads=[('ps', 1)], writes=[('cE',)])
                tk.op('dve', lambda e: e.tensor_scalar(mk[:, 0, :], arow[:, 0, :], tabs[:, 0, k:k + 1], None, ALU.is_le),
                      reads=[('arow',), ('tabs',)], writes=[('mk', 0)])
                tk.op('dve', lambda e: e.tensor_tensor(cE[:, :, :], cE[:, :, :], bc(mk[:, 0, :], 1, 4), ALU.mult), reads=[('cE',), ('mk', 0)], writes=[('cE',)])
                tk.op('dve', lambda e: e.tensor_reduce(sm[:, 16:20], cE[:, :, :], AX.X, ALU.add), reads=[('cE',)], writes=[('sm', 2)])
                tk.op('dve', lambda e: e.tensor_scalar(sm[:, 16:20], sm[:, 16:20], 1e-30, None, ALU.max), reads=[('sm', 2)], writes=[('sm', 2)])
                tk.op('dve', lambda e: e.reciprocal(sm[:, 20:24], sm[:, 16:20]), reads=[('sm', 2)], writes=[('sm', 3)])
                tk.op('dve', lambda e: e.tensor_tensor(cE[:, :, :], cE[:, :, :], bc(sm[:, 20:24], 2, 128), ALU.mult), reads=[('cE',), ('sm', 3)], writes=[('cE',)])
                tk.op('dve', lambda e: e.tensor_reduce(mk[:, 3, :], cE[:, :, :].rearrange("p j n -> p n j"), AX.X, ALU.add), reads=[('cE',)], writes=[('mk', 3)])
                for j in range(4):
                    tk.op('pe', lambda e: e.transpose(ps[0][:, j * 128:(j + 1) * 128], cE[:, j, :], self.ident[:, :]),
                          reads=[('cE',), ('ident',)], writes=[('ps', 0)])
                tk.op('act', lambda e: e.activation(PT[:, :, :].rearrange("p a b -> p (a b)"), ps[0][:, 0:512], AF.Copy), reads=[('ps', 0)], writes=[('PT',)])
                for j in range(4):
                    tk.op('pe', lambda e: e.matmul(ps[3][:, j * 64:(j + 1) * 64], PT[:, j, :], vcb[:, g * 64:(g + 1) * 64], start=True, stop=True),
                          reads=[('PT',), ('vcb',)], writes=[('ps', 3)])
                tk.op('dve', lambda e: e.tensor_scalar(mk[:, 0, :], arow[:, 1, :], tabs[:, 1, k:k + 1], None, ALU.is_le), reads=[('arow',), ('tabs',)], writes=[('mk', 0)])
                tk.op('dve', lambda e: e.tensor_scalar(mk[:, 1, :], arow[:, 1, :], tabs[:, 2, k:k + 1], None, ALU.is_ge), reads=[('arow',), ('tabs',)], writes=[('mk', 1)])
                tk.op('dve', lambda e: e.tensor_tensor(mk[:, 1, :], mk[:, 1, :], arow[:, 2, :], ALU.max), reads=[('mk', 1), ('arow',)], writes=[('mk', 1)])
                tk.op('dve', lambda e: e.scalar_tensor_tensor(mk[:, 1, :], mk[:, 1, :], 1e4, mk[:, 3, :], ALU.mult, ALU.add), reads=[('mk', 1), ('mk', 3)], writes=[('mk', 1)])
                tk.op('dve', lambda e: e.scalar_tensor_tensor(mk[:, 1, :], mk[:, 1, :], 1.0, mk[:, 0, :], ALU.add, ALU.mult), reads=[('mk', 1), ('mk', 0)], writes=[('mk', 1)])
                tk.op('dve', lambda e: e.max(sm[:, 0:8], mk[:, 1, :]), reads=[('mk', 1)], writes=[('sm', 0)])
                tk.op('dve', lambda e: e.match_replace(mk[:, 2, :], sm[:, 0:8], mk[:, 1, :], -1.0), reads=[('mk', 1), ('sm', 0)], writes=[('mk', 2)])
                tk.op('dve', lambda e: e.max(sm[:, 8:16], mk[:, 2, :]), reads=[('mk', 2)], writes=[('sm', 1)])
                tk.op('dve', lambda e: e.scalar_tensor_tensor(mk[:, 2, :], mk[:, 1, :], sm[:, 15:16], mk[:, 0, :], ALU.is_ge, ALU.mult),
                      reads=[('mk', 1), ('sm', 1), ('mk', 0)], writes=[('mk', 2)])
                tk.op('dve', lambda e: e.tensor_scalar(mk[:, 2, :], mk[:, 2, :], -1.0, 30000.0, ALU.add, ALU.mult), reads=[('mk', 2)], writes=[('mk', 2)])
                tk.op('pe', lambda e: e.transpose(ps[2][:, 0:128], mk[:, 2, :], self.ident[:, :]), reads=[('mk', 2), ('ident',)], writes=[('ps', 2)])
                for a4 in range(4):
                    tk.op('act', lambda e: e.activation(mbz[:, a4, :].rearrange("p (a b) -> p a b", b=128), bc(ps[2][:, 0:128], 1, 4), AF.Copy,
                                                        scale=rmask[:, 2 + a4:3 + a4]),
                          reads=[('ps', 2), ('rmask',), ('qf',)], writes=[('qf', 9, a4)])
                qr = qz[:, 1, gg, :]
                ntile = 4 * k + 4
                for j in range(ntile):
                    b = 6 + (j % 2)
                    a_, kk = j // 16, j % 16
                    tk.op('pe', lambda e: e.matmul(ps[b][:, 0:512], KT[:, j * 128:(j + 1) * 128], qr, start=True, stop=False),
                          reads=[('KT',), ('qz', 1, gg)], writes=[('ps', b)])
                    tk.op('pe', lambda e: e.matmul(ps[b][:, 0:512], selc[:, kk, :], mbz[:, a_, :], start=False, stop=True),
                          reads=[('selc',), ('qf', 9, a_)], writes=[('ps', b)])
                    tk.op('act', lambda e: e.activation(E[:, j % 2, :], ps[b][:, 0:512], AF.Exp), reads=[('ps', b)], writes=[('E', j % 2)])
                    if j >= 4 * k:
                        tk.op('pool', lambda e: e.tensor_tensor(E[:, j % 2, :].rearrange("p (a b) -> p a b", b=128), E[:, j % 2, :].rearrange("p (a b) -> p a b", b=128),
                                                                bc(dm[:, j - 4 * k, :], 1, 4), ALU.mult),
                              reads=[('E', j % 2), ('dm',)], writes=[('E', j % 2)])
                    bpv = 4 + (j % 2)
                    for jj in range(4):
                        tk.op('pe', lambda e: e.matmul(ps[bpv][:, jj * 65:(jj + 1) * 65], E[:, j % 2, jj * 128:(jj + 1) * 128], V[:, j, gg, :],
                                                       start=True, stop=True),
                              reads=[('E', j % 2), ('V',), ('V1',)], writes=[('ps', bpv)])
                    if j == 0:
                        tk.op('dve', lambda e: e.tensor_copy(osw[:, 0, :], ps[bpv][:, 0:260]), reads=[('ps', bpv)], writes=[('osw', 0)])
                    else:
                        tk.op('dve', lambda e: e.tensor_tensor(osw[:, 0, :], osw[:, 0, :], ps[bpv][:, 0:260], ALU.add), reads=[('ps', bpv), ('osw', 0)], writes=[('osw', 0)])
                for m in range(m0, 8):
                    b = 6 + (m % 2)
                    tk.op('pe', lambda e: e.matmul(ps[b][:, 0:512], kwT[:, m, :], qr, start=True, stop=True),
                          reads=[('kwT',), ('qz', 1, gg)], writes=[('ps', b)])
                    tk.op('act', lambda e: e.activation(E[:, m % 2, :], ps[b][:, 0:512], AF.Exp), reads=[('ps', b)], writes=[('E', m % 2)])
                    tk.op('pool', lambda e: e.tensor_tensor(E[:, m % 2, :].rearrange("p (a b) -> p a b", b=128), E[:, m % 2, :].rearrange("p (a b) -> p a b", b=128),
                                                            bc(wm[:, m, :], 1, 4), ALU.mult),
                          reads=[('E', m % 2), ('wm',)], writes=[('E', m % 2)])
                    bpv = 4 + (m % 2)
                    for jj in range(4):
                        tk.op('pe', lambda e: e.matmul(ps[bpv][:, jj * 65:(jj + 1) * 65], E[:, m % 2, jj * 128:(jj + 1) * 128], vw[:, m, gg, :],
                                                       start=True, stop=True),
                              reads=[('E', m % 2), ('vw',), ('vw1',)], writes=[('ps', bpv)])
                    if m == m0:
                        tk.op('dve', lambda e: e.tensor_copy(osw[:, 1, :], ps[bpv][:, 0:260]), reads=[('ps', bpv)], writes=[('osw', 1)])
                    else:
                        tk.op('dve', lambda e: e.tensor_tensor(osw[:, 1, :], osw[:, 1, :], ps[bpv][:, 0:260], ALU.add), reads=[('ps', bpv), ('osw', 1)], writes=[('osw', 1)])
                if 'd_o' in P.outs and k == 0 and gg == 0 and gp == 0:
                    dbgt = cv(0, [128, 1024], F32)
                    tk.op('act', lambda e: e.activation(dbgt[:, 0:260], ps[3][:, 0:260], AF.Copy), reads=[('ps', 3)], writes=[('dbgt', 0)])
                    for bi in (1, 2):
                        tk.op('act', lambda e: e.activation(dbgt[:, bi * 260:(bi + 1) * 260], osw[:, bi - 1, :], AF.Copy), reads=[('osw', bi - 1)], writes=[('dbgt', bi)])
                    tk.op('act', lambda e: e.activation(dbgt[:, 780:908], mk[:, 2, :], AF.Copy), reads=[('mk', 2)], writes=[('dbgt', 3)])
                    tk.op('act', lambda e: e.activation(dbgt[:, 908:1024], mk[:, 3, 0:116], AF.Copy), reads=[('mk', 3)], writes=[('dbgt', 4)])
                    tk.op('pool', lambda e: e.dma_start(out=P.outs['d_o'][:, :], in_=dbgt[:, :]), reads=[('dbgt',)], writes=[('o_do',)], dma=True)
                gv = gat[:, k, g * 12:(g + 1) * 12].rearrange("p (j b) -> p j b", b=3)
                for (bk_, o_) in [(4, 24), (5, 28)]:
                    p3 = osw[:, bk_ - 4, :].rearrange("p (j d) -> p j d", d=65)
                    tk.op('dve', lambda e: e.tensor_scalar(sm[:, o_:o_ + 4], p3[:, :, 64], 1e-30, None, ALU.max), reads=[('osw', bk_ - 4)], writes=[('sm', o_)])
                    tk.op('dve', lambda e: e.reciprocal(sm[:, o_:o_ + 4], sm[:, o_:o_ + 4]), reads=[('sm', o_)], writes=[('sm', o_)])
                    tk.op('dve', lambda e: e.tensor_tensor(sm[:, o_:o_ + 4], sm[:, o_:o_ + 4], gv[:, :, 1 if bk_ == 4 else 2], ALU.mult),
                          reads=[('sm', o_), ('gat',)], writes=[('sm', o_)])
                oa = oacc[:, gg * 256:(gg + 1) * 256].rearrange("p (j d) -> p j d", d=64)
                tk.op('dve', lambda e: e.tensor_tensor(oa, ps[3][:, 0:256].rearrange("p (j d) -> p j d", d=64), bc(gv[:, :, 0], 2, 64), ALU.mult),
                      reads=[('ps', 3), ('gat',)], writes=[('oacc', gg)])
                for (bk_, o_, wi) in [(4, 24, 0), (5, 28, 1)]:
                    p3 = osw[:, wi, :].rearrange("p (j d) -> p j d", d=65)
                    w3 = ow[:, wi, :].rearrange("p (j d) -> p j d", d=64)
                    tk.op('dve', lambda e: e.tensor_tensor(w3, p3[:, :, 0:64], bc(sm[:, o_:o_ + 4], 2, 64), ALU.mult),
                          reads=[('osw', wi), ('sm', o_)], writes=[('ow', wi)])
                    tk.op('dve', lambda e: e.tensor_tensor(oa, oa, w3, ALU.add), reads=[('oacc', gg), ('ow', wi)], writes=[('oacc', gg)])
            for t in range(4):
                tk.op('pe', lambda e: e.transpose(ps[0][:, t * 128:(t + 1) * 128], oacc[:, t * 128:(t + 1) * 128], self.ident[:, :]),
                      reads=[('oacc',), ('ident',)], writes=[('ps', 0)])
            tk.op('act', lambda e: e.activation(oT[:, :, :].rearrange("p a b -> p (a b)"), ps[0][:, 0:512], AF.Copy), reads=[('ps', 0)], writes=[('oT',)])
            for half in range(2):
                bko = 1 if half == 0 else 2
                for mm in range(4):
                    m = half * 4 + mm
                    for t in range(4):
                        tk.op('pe', lambda e: e.matmul(ps[bko][:, mm * 128:(mm + 1) * 128], Wo[:, t, m * 128:(m + 1) * 128], oT[:, t, :],
                                                       start=(t == 0), stop=(t == 3)),
                              reads=[('Wo',), ('oT',)], writes=[('ps', bko)])
                tk.op('dve', lambda e: e.tensor_tensor(resid[:, 4 * half:4 * half + 4, c0:c0 + 128], resid[:, 4 * half:4 * half + 4, c0:c0 + 128],
                                                       ps[bko][:, 0:512].rearrange("p (a b) -> p a b", b=128), ALU.add),
                      reads=[('ps', bko), ('resid', ci)], writes=[('resid', ci)])


        if not self.do_sample:
            return
        tk.barrier()
        R = NSEQ * SW
        c0 = PCOL
        sg = stg[:, :, :].rearrange("p a (b c) -> p (a b) c", c=256)
        sgn = [0]

        def nsg():
            q = sgn[0]
            sgn[0] = (q + 1) % 4
            return q
        qs = cv(30752, [128, 2, 16], BF16)
        Es = cv(30816, [128, 2, 16], BF16)
        PTs = cv(30880, [128, 16], BF16)
        mbTs = cv(30912, [128, 16], BF16)
        oTs = cv(30944, [128, 4, 16], BF16)
        vnb = cv(31072, [128, 2, 2, 65], BF16)
        ktnb = cv(31592, [128, 2, 24], BF16)
        gts = cv(31688, [128, 4, 48], F32)
        nm, wm0, idx = self.nm, self.wm0, self.idx
        ktn, vn, ckwT, cvwt, gats = [P.ins[k_] for k_ in ('ktn', 'vn', 'ckwT', 'cvwt', 'gats')]
        pool_ks, pool_vs = P.ins['pool_ks'], P.ins['pool_vs']
        tk.op('sp', lambda e: e.dma_start(out=gts[0:4, :, :], in_=gats[:, :, :]), writes=[('gts',)], dma=True)
        tk.op('pool', lambda e: e.memset(vnb[:, :, :, 64:65], 1.0), writes=[('vnb1',)])
        for kind in range(2):
            q = nsg()
            tk.op('sp', lambda e: e.dma_start(out=sg[:, q, 0:24], in_=ktn[kind, gp, :, :]), writes=[('sg', q)], dma=True)
            tk.op('dve', lambda e: e.tensor_copy(ktnb[:, kind, :], sg[:, q, 0:24]), reads=[('sg', q)], writes=[('ktnb', kind)])
        for c in range(KC):
            tk.op('pe', lambda e: e.matmul(ps[0][0:R, 0:512], xn[:, c, c0:c0 + R], Wq[:, c, :], start=(c == 0), stop=(c == KC - 1)),
                  reads=[('xn',), ('Wq',)], writes=[('ps', 0)])
        self.qk_norm_rope(ps[0][0:R, 0:512], R, 8, 0, NT, qf[0:R, 0, :], qf[0:R, 1, :], qf[0:R, 1, :], sm, ow[0:R, 0, :], ow[0:R, 1, :], 0.125,
                          ('qf', 0), ('qf', 1), ('ps', 0))
        for v in range(2):
            bkv = 0 if v == 0 else 2
            for t in range(4):
                tk.op('pe', lambda e: e.transpose(ps[bkv][:, t * 32:t * 32 + R], qf[0:R, v, t * 128:(t + 1) * 128], self.ident[0:R, 0:R]),
                      reads=[('qf', v), ('ident',)], writes=[('ps', bkv)])
            tk.op('act', lambda e: e.activation(qT[:, v, :, 0:R], ps[bkv][:, 0:128].rearrange("p (a b) -> p a b", b=32)[:, :, 0:R], AF.Copy),
                  reads=[('ps', bkv)], writes=[('qT', v)])
        for sq_ in range(NSEQ):
            r0 = sq_ * SW + 2
            tk.op('sp', lambda e: e.dma_start(out=kcT[:, :, :].rearrange("p a b -> p (a b)"), in_=self.d_kc[1 + sq_, :, :]),
                  reads=[('d_kc', 1 + sq_)], writes=[('kcT',)], dma=True)
            tk.op('sp', lambda e: e.dma_start(out=vcb[:, :], in_=self.d_vc[1 + sq_, :, :]), reads=[('d_vc', 1 + sq_)], writes=[('vcb',)], dma=True)
            q = nsg()
            tk.op('sp', lambda e: e.dma_start(out=sg[0:4, q, :].rearrange("p (k c) -> p k c", c=128), in_=vn[:, sq_, :, gp * 128:(gp + 1) * 128]),
                  writes=[('sg', q)], dma=True)
            tk.op('dve', lambda e: e.tensor_copy(vnb[0:4, :, :, 0:64], sg[0:4, q, :].rearrange("p (k g d) -> p k g d", g=2, d=64)),
                  reads=[('sg', q)], writes=[('vnb',)])
            for page in range(64):
                col = sq_ * 64 + page
                q = nsg()
                tk.op('pool', lambda e: e.indirect_dma_start(out=sg[:, q, :], out_offset=None, in_=pool_ks[:, :],
                                                             in_offset=bass.IndirectOffsetOnAxis(ap=idx[:, col:col + 1], axis=0)),
                      reads=[('idx',)], writes=[('sg', q)], dma=True)
                b = 6 + (page % 2)
                tk.op('pe', lambda e: e.transpose(ps[b][:, 0:128], sg[:, q, gp * 128:(gp + 1) * 128], self.ident[:, :]),
                      reads=[('sg', q), ('ident',)], writes=[('ps', b)])
                tk.op('act', lambda e: e.activation(KT[:, page * 128:(page + 1) * 128], ps[b][:, 0:128], AF.Copy), reads=[('ps', b)], writes=[('KT', page)])
                q = nsg()
                tk.op('pool', lambda e: e.indirect_dma_start(out=sg[:, q, :], out_offset=None, in_=pool_vs[:, :],
                                                             in_offset=bass.IndirectOffsetOnAxis(ap=idx[:, col:col + 1], axis=0)),
                      reads=[('idx',)], writes=[('sg', q)], dma=True)
                tk.op('dve', lambda e: e.tensor_copy(V[:, page, :, 0:64], sg[:, q, gp * 128:(gp + 1) * 128].rearrange("p (g d) -> p g d", d=64)),
                      reads=[('sg', q)], writes=[('V', page)])
            for h in range(2):
                q = nsg()
                tk.op('sp', lambda e: e.dma_start(out=sg[:, q, :], in_=ckwT[gp, :, sq_ * 512 + h * 256:sq_ * 512 + (h + 1) * 256]), writes=[('sg', q)], dma=True)
                tk.op('dve', lambda e: e.tensor_copy(kwT[:, 2 * h:2 * h + 2, :].rearrange("p a b -> p (a b)"), sg[:, q, :]), reads=[('sg', q)], writes=[('kwT', h)])
                q = nsg()
                tk.op('sp', lambda e: e.dma_start(out=sg[:, q, :].rearrange("p (t c) -> p t c", c=128),
                                                  in_=cvwt[:, sq_ * 4 + 2 * h:sq_ * 4 + 2 * h + 2, gp * 128:(gp + 1) * 128]), writes=[('sg', q)], dma=True)
                tk.op('dve', lambda e: e.tensor_copy(vw[:, 2 * h:2 * h + 2, :, 0:64], sg[:, q, :].rearrange("p (t g d) -> p t g d", g=2, d=64)),
                      reads=[('sg', q)], writes=[('vw', h)])
            for v in range(2):
                tk.op('act', lambda e: e.activation(qs[:, v, :].rearrange("p (a b) -> p a b", b=4), qT[:, v, :, r0:r0 + 4], AF.Copy),
                      reads=[('qT', v)], writes=[('qs', v)])
            for v in range(2):
                for gg in range(2):
                    tk.op('pool', lambda e: e.tensor_scalar(qsz[:, v, gg, :], qs[:, v, :], rmask[:, gg:gg + 1], None, ALU.mult),
                          reads=[('qs', v), ('rmask',)], writes=[('qsz', v, gg)])
            for gg in range(2):
                g = 2 * gp + gg
                h0, h1 = 64 * gg, 64 * gg + 64
                for j in range(4):
                    tk.op('pe', lambda e: e.matmul(ps[1][0:4, j * 128:(j + 1) * 128], qsz[:, 0, gg, j * 4:(j + 1) * 4], kcT[:, gp, :], start=True, stop=True),
                          reads=[('qsz', 0, gg), ('kcT',)], writes=[('ps', 1)])
                tk.op('act', lambda e: e.activation(cE[0:4, :, :].rearrange("p a b -> p (a b)"), ps[1][0:4, 0:512], AF.Exp), reads=[('ps', 1)], writes=[('cE',)])
                tk.op('dve', lambda e: e.tensor_reduce(sm[0:4, 16:20], cE[0:4, :, :], AX.X, ALU.add), reads=[('cE',)], writes=[('sm', 2)])
                tk.op('dve', lambda e: e.tensor_scalar(sm[0:4, 16:20], sm[0:4, 16:20], 1e-30, None, ALU.max), reads=[('sm', 2)], writes=[('sm', 2)])
                tk.op('dve', lambda e: e.reciprocal(sm[0:4, 20:24], sm[0:4, 16:20]), reads=[('sm', 2)], writes=[('sm', 3)])
                tk.op('dve', lambda e: e.tensor_tensor(cE[0:4, :, :], cE[0:4, :, :], bc(sm[0:4, 20:24], 2, 128), ALU.mult), reads=[('cE',), ('sm', 3)], writes=[('cE',)])
                tk.op('dve', lambda e: e.tensor_reduce(mk[0:4, 3, :], cE[0:4, :, :].rearrange("p j n -> p n j"), AX.X, ALU.add), reads=[('cE',)], writes=[('mk', 3)])
                for j in range(4):
                    tk.op('pe', lambda e: e.transpose(ps[0][:, j * 4:(j + 1) * 4], cE[0:4, j, :], self.ident[0:4, 0:4]),
                          reads=[('cE',), ('ident',)], writes=[('ps', 0)])
                tk.op('act', lambda e: e.activation(PTs[:, :], ps[0][:, 0:16], AF.Copy), reads=[('ps', 0)], writes=[('PTs',)])
                for j in range(4):
                    tk.op('pe', lambda e: e.matmul(ps[3][0:4, j * 64:(j + 1) * 64], PTs[:, j * 4:(j + 1) * 4], vcb[:, g * 64:(g + 1) * 64], start=True, stop=True),
                          reads=[('PTs',), ('vcb',)], writes=[('ps', 3)])
                kq = NT
                tk.op('dve', lambda e: e.tensor_scalar(mk[0:4, 1, :], arow[0:4, 1, :], tabs[0:4, 2, kq:kq + 1], None, ALU.is_ge), reads=[('arow',), ('tabs',)], writes=[('mk', 1)])
                tk.op('dve', lambda e: e.tensor_tensor(mk[0:4, 1, :], mk[0:4, 1, :], arow[0:4, 2, :], ALU.max), reads=[('mk', 1), ('arow',)], writes=[('mk', 1)])
                tk.op('dve', lambda e: e.scalar_tensor_tensor(mk[0:4, 1, :], mk[0:4, 1, :], 1e4, mk[0:4, 3, :], ALU.mult, ALU.add), reads=[('mk', 1), ('mk', 3)], writes=[('mk', 1)])
                tk.op('dve', lambda e: e.max(sm[0:4, 0:8], mk[0:4, 1, :]), reads=[('mk', 1)], writes=[('sm', 0)])
                tk.op('dve', lambda e: e.match_replace(mk[0:4, 2, :], sm[0:4, 0:8], mk[0:4, 1, :], -1.0), reads=[('mk', 1), ('sm', 0)], writes=[('mk', 2)])
                tk.op('dve', lambda e: e.max(sm[0:4, 8:16], mk[0:4, 2, :]), reads=[('mk', 2)], writes=[('sm', 1)])
                tk.op('dve', lambda e: e.tensor_scalar(mk[0:4, 2, :], mk[0:4, 1, :], sm[0:4, 14:15], None, ALU.is_ge), reads=[('mk', 1), ('sm', 1)], writes=[('mk', 2)])
                tk.op('dve', lambda e: e.tensor_scalar(mk[0:4, 2, :], mk[0:4, 2, :], -1.0, 30000.0, ALU.add, ALU.mult), reads=[('mk', 2)], writes=[('mk', 2)])
                tk.op('pe', lambda e: e.transpose(ps[2][:, 0:4], mk[0:4, 2, :], self.ident[0:4, 0:4]), reads=[('mk', 2), ('ident',)], writes=[('ps', 2)])
                for a4 in range(4):
                    tk.op('act', lambda e: e.activation(mbzs[:, a4, :].rearrange("p (a b) -> p a b", b=4), bc(ps[2][:, 0:4], 1, 4), AF.Copy,
                                                        scale=rmask[:, 2 + a4:3 + a4]),
                          reads=[('ps', 2), ('rmask',)], writes=[('mbzs', a4)])
                qr = qsz[:, 1, gg, :]
                for grp in range(16):
                    b = 6 + (grp % 2)
                    for p4 in range(4):
                        page = 4 * grp + p4
                        a_, kk = page // 16, page % 16
                        tk.op('pe', lambda e: e.matmul(ps[b][:, p4 * 16:(p4 + 1) * 16], KT[:, page * 128:(page + 1) * 128], qr, start=True, stop=False),
                              reads=[('KT',), ('qsz', 1, gg)], writes=[('ps', b)])
                        tk.op('pe', lambda e: e.matmul(ps[b][:, p4 * 16:(p4 + 1) * 16], selc[:, kk, :], mbzs[:, a_, :], start=False, stop=True),
                              reads=[('selc',), ('mbzs', a_)], writes=[('ps', b)])
                    tk.op('act', lambda e: e.activation(Es4[:, grp % 2, :], ps[b][:, 0:64], AF.Exp), reads=[('ps', b)], writes=[('Es4', grp % 2)])
                    bpv = 4 + (grp % 2)
                    for jj in range(4):
                        for p4 in range(4):
                            page = 4 * grp + p4
                            tk.op('pe', lambda e: e.matmul(ps[bpv][0:4, jj * 65:(jj + 1) * 65], Es4[:, grp % 2, p4 * 16 + jj * 4:p4 * 16 + (jj + 1) * 4], V[:, page, gg, :],
                                                           start=(p4 == 0), stop=(p4 == 3)),
                                  reads=[('Es4', grp % 2), ('V',), ('V1',)], writes=[('ps', bpv)])
                    if grp == 0:
                        tk.op('dve', lambda e: e.tensor_copy(osw[0:4, 0, :], ps[bpv][0:4, 0:260]), reads=[('ps', bpv)], writes=[('osw', 0)])
                    else:
                        tk.op('dve', lambda e: e.tensor_tensor(osw[0:4, 0, :], osw[0:4, 0, :], ps[bpv][0:4, 0:260], ALU.add), reads=[('ps', bpv), ('osw', 0)], writes=[('osw', 0)])
                b = 6
                for m in range(4):
                    tk.op('pe', lambda e: e.matmul(ps[b][:, m * 16:(m + 1) * 16], kwT[:, m, :], qr, start=True, stop=True),
                          reads=[('kwT',), ('qsz', 1, gg)], writes=[('ps', b)])
                tk.op('act', lambda e: e.activation(Es4[:, 0, :], ps[b][:, 0:64], AF.Exp), reads=[('ps', b)], writes=[('Es4', 0)])
                tk.op('pool', lambda e: e.tensor_tensor(Es4[:, 0, 0:16], Es4[:, 0, 0:16], wm0[:, :], ALU.mult), reads=[('Es4', 0), ('wm0',)], writes=[('Es4', 0)])
                bpv = 4
                for jj in range(4):
                    for m in range(4):
                        tk.op('pe', lambda e: e.matmul(ps[bpv][0:4, jj * 65:(jj + 1) * 65], Es4[:, 0, m * 16 + jj * 4:m * 16 + (jj + 1) * 4], vw[:, m, gg, :],
                                                       start=(m == 0), stop=(m == 3)),
                              reads=[('Es4', 0), ('vw',), ('vw1',)], writes=[('ps', bpv)])
                tk.op('dve', lambda e: e.tensor_copy(osw[0:4, 1, :], ps[bpv][0:4, 0:260]), reads=[('ps', bpv)], writes=[('osw', 1)])
                for kind in range(2):
                    b = 6 + kind
                    tk.op('pe', lambda e: e.matmul(ps[b][0:4, 0:16], ktnb[:, kind, r0:r0 + 4], qr, start=True, stop=True),
                          reads=[('ktnb',), ('qsz', 1, gg)], writes=[('ps', b)])
                    tk.op('act', lambda e: e.activation(Es[0:4, kind, :], ps[b][0:4, 0:16], AF.Exp), reads=[('ps', b)], writes=[('Es', kind)])
                    tk.op('pool', lambda e: e.tensor_tensor(Es[0:4, kind, :], Es[0:4, kind, :], nm[0:4, :], ALU.mult), reads=[('Es', kind), ('nm',)], writes=[('Es', kind)])
                    bpv = 4 + kind
                    for jj in range(4):
                        tk.op('pe', lambda e: e.matmul(ps[bpv][0:4, jj * 65:(jj + 1) * 65], Es[0:4, kind, jj * 4:(jj + 1) * 4], vnb[0:4, kind, gg, :], start=True, stop=True),
                              reads=[('Es', kind), ('vnb',), ('vnb1',)], writes=[('ps', bpv)], pg=(0, 4))
                    tk.op('dve', lambda e: e.tensor_tensor(osw[0:4, kind, :], osw[0:4, kind, :], ps[bpv][0:4, 0:260], ALU.add), reads=[('ps', bpv), ('osw', kind)], writes=[('osw', kind)])
                gv = gts[0:4, sq_, g * 12:(g + 1) * 12].rearrange("p (j b) -> p j b", b=3)
                for (wi, o_) in [(0, 24), (1, 28)]:
                    p3 = osw[0:4, wi, :].rearrange("p (j d) -> p j d", d=65)
                    tk.op('dve', lambda e: e.tensor_scalar(sm[0:4, o_:o_ + 4], p3[:, :, 64], 1e-30, None, ALU.max), reads=[('osw', wi)], writes=[('sm', o_)])
                    tk.op('dve', lambda e: e.reciprocal(sm[0:4, o_:o_ + 4], sm[0:4, o_:o_ + 4]), reads=[('sm', o_)], writes=[('sm', o_)])
                    tk.op('dve', lambda e: e.tensor_tensor(sm[0:4, o_:o_ + 4], sm[0:4, o_:o_ + 4], gv[:, :, 1 + wi], ALU.mult), reads=[('sm', o_), ('gts',)], writes=[('sm', o_)])
                oa = oacc[0:4, gg * 256:(gg + 1) * 256].rearrange("p (j d) -> p j d", d=64)
                tk.op('dve', lambda e: e.tensor_tensor(oa, ps[3][0:4, 0:256].rearrange("p (j d) -> p j d", d=64), bc(gv[:, :, 0], 2, 64), ALU.mult),
                      reads=[('ps', 3), ('gts',)], writes=[('oacc', gg)])
                for (wi, o_) in [(0, 24), (1, 28)]:
                    p3 = osw[0:4, wi, :].rearrange("p (j d) -> p j d", d=65)
                    w3 = ow[0:4, wi, :].rearrange("p (j d) -> p j d", d=64)
                    tk.op('dve', lambda e: e.tensor_tensor(w3, p3[:, :, 0:64], bc(sm[0:4, o_:o_ + 4], 2, 64), ALU.mult), reads=[('osw', wi), ('sm', o_)], writes=[('ow', wi)])
                    tk.op('dve', lambda e: e.tensor_tensor(oa, oa, w3, ALU.add), reads=[('oacc', gg), ('ow', wi)], writes=[('oacc', gg)])
            for t in range(4):
                tk.op('pe', lambda e: e.transpose(ps[0][:, 32 + t * 4:32 + (t + 1) * 4], oacc[0:4, t * 128:(t + 1) * 128], self.ident[0:4, 0:4]),
                      reads=[('oacc',), ('ident',)], writes=[('ps', 0)])
            tk.op('act', lambda e: e.activation(oTs[:, :, sq_ * 4:(sq_ + 1) * 4], ps[0][:, 32:48].rearrange("p (a b) -> p a b", b=4), AF.Copy),
                  reads=[('ps', 0)], writes=[('oTs', sq_)])
        for half in range(2):
            bko = 1 if half == 0 else 2
            for mm in range(4):
                m = half * 4 + mm
                for t in range(4):
                    tk.op('pe', lambda e: e.matmul(ps[bko][:, mm * 16:(mm + 1) * 16], Wo[:, t, m * 128:(m + 1) * 128], oTs[:, t, :], start=(t == 0), stop=(t == 3)),
                          reads=[('Wo',), ('oTs',)], writes=[('ps', bko)])
            rv = resid[:, 4 * half:4 * half + 4, PCOL:NCOL].rearrange("p m (s w) -> p m s w", w=SW)[:, :, :, 2:6]
            tk.op('dve', lambda e: e.tensor_tensor(rv, rv, ps[bko][:, 0:64].rearrange("p (m s q) -> p m s q", s=4, q=4), ALU.add),
                  reads=[('ps', bko), ('resid', 5)], writes=[('resid', 5)])


def build(stage=99):
    from contextlib import ExitStack
    P = Prog()
    nc = P.nc
    xT = P.din('xT', [128, KC, NCOL])
    P.din('pT', [2, 128, 2, NCOL])
    ng = P.din('ng', [128, 2 * 4 * KC])
    P.din('wgu', [2, 2, 22, 128, KC * 256])
    P.din('wdn', [2, 2, 22, 128, 1024])
    P.din('wpg', [2, 8, 128, 1280])
    P.din('cwin', [24, 128, KC * 128])
    cw = P.din('cw', [128, 3 * KC])
    P.din('cwout', [8, 128, KC * 128])
    hm = P.din('hm', [128, NT])
    stT = P.din('stT', [128, KC, NSEQ, 2])
    P.din('nwkv', [128, KC, 1584])
    qkg = P.din('qkg', [128, 4 * 64])
    csd = P.din('cs', [128, NT + 1, 64])
    ckw = P.din('ckw', [NSEQ, 512, 256])
    cvw = P.din('cvw', [NSEQ, 512, 256])
    yT = P.dout('yT', [128, KC, NCOL])
    cvoT = P.dout('cvoT', [128, KC, 2 + 2 * NSEQ])
    P.dout('o_kvp', [NT, 128, 1536])
    P.dout('o_kvs', [NSEQ * SW, 1536])
    o_kws = P.dout('o_kws', [NSEQ, 512, 256])
    o_vws = P.dout('o_vws', [NSEQ, 512, 256])
    P.dout('gato', [128, NT + 1, 48])
    with ExitStack() as stack:
        B = Builder(P, stack)
        tk = B.tk
        sb = B.sb
        B.cws = sb("cws", [128, 3 * KC], F32)
        B.hms = sb("hms", [128, NT], F32)
        B.sts = sb("sts", [128, KC, NSEQ, 2], F32)
        B.cvo = sb("cvo", [128, KC, 2 + 2 * NSEQ], F32)
        B.qkg = sb("qkg", [128, 4, 64], F32)
        B.cs = sb("cs", [128, NT + 1, 64], F32)
        B.gat = sb("gat", [128, NT + 1, 48], F32)
        B.st4 = sb("st4", [128, 8], F32)
        tk.op('pool', lambda e: e.memset(B.onesb[:, :], 1.0 / D), writes=[('onesb',)])
        tk.op('pool', lambda e: e.memset(B.epsb[:, :], EPS), writes=[('epsb',)])
        tk.op('sp', lambda e: e.dma_start(out=B.ngs[:, :], in_=ng[:, :]), writes=[('ngs',)], dma=True)
        tk.op('sp', lambda e: e.dma_start(out=B.cws[:, :], in_=cw[:, :]), writes=[('cws',)], dma=True)
        tk.op('sp', lambda e: e.dma_start(out=B.hms[:, :], in_=hm[:, :]), writes=[('hms',)], dma=True)
        tk.op('sp', lambda e: e.dma_start(out=B.sts[:, :, :, :], in_=stT[:, :, :, :]), writes=[('sts',)], dma=True)
        tk.op('sp', lambda e: e.dma_start(out=B.qkg[:, :, :].rearrange("p a d -> p (a d)"), in_=qkg[:, :]), writes=[('qkg',)], dma=True)
        tk.op('sp', lambda e: e.dma_start(out=B.cs[:, :, :], in_=csd[:, :, :]), writes=[('cs',)], dma=True)
        for sq_ in range(NSEQ):
            tk.op('pool', lambda e: e.dma_start(out=o_kws[sq_, 0:508, :], in_=ckw[sq_, 4:512, :]), writes=[('o_kws0', sq_)], dma=True)
            tk.op('pool', lambda e: e.dma_start(out=o_vws[sq_, 0:508, :], in_=cvw[sq_, 4:512, :]), writes=[('o_vws0', sq_)], dma=True)
        for ci, (a, b) in enumerate(CTS):
            tk.op('sp', lambda e: e.dma_start(out=B.resid[:, :, a:b], in_=xT[:, :, a:b]), writes=[('resid', ci)], dma=True)
        B.ffn(0, 0)
        tk.barrier()
        tk.op('pool', lambda e: e.memset(B.yb[:, 0:2], 0.0), writes=[('yb',)])
        B.conv()
        tk.barrier()
        B.ffn(0, 1)
        B.ple(0)
        tk.op('pool', lambda e: e.dma_start(out=cvoT[:, :, :], in_=B.cvo[:, :, :]), reads=[('cvo',)], writes=[('o_cvo',)], dma=True)
        if stage >= 3:
            B.ffn(1, 0)
            tk.barrier()
            B.nsa_proj()
            tk.barrier()
        tk.op('pool', lambda e: e.dma_start(out=P.outs['gato'][:, :, :], in_=B.gat[:, :, :]), reads=[('gat',)], writes=[('o_gat',)], dma=True)
        for ci, (a, b) in enumerate(CTS):
            tk.op('pool', lambda e: e.dma_start(out=yT[:, :, a:b], in_=B.resid[:, :, a:b]), reads=[('resid', ci)], writes=[('o_y', ci)], dma=True)
        tk.wait_all('pool')
    return P


def build2(phase=3, nk=NT, ngp=2, cstop=9):
    from contextlib import ExitStack
    P = Prog()
    nc = P.nc
    xT = P.din('resid2', [128, KC, NCOL])
    ng = P.din('ng', [128, 2 * 4 * KC])
    if phase >= 3:
        P.din('pT', [2, 128, 2, NCOL])
        P.din('wgu', [2, 2, 22, 128, KC * 256])
        P.din('wdn', [2, 2, 22, 128, 1024])
        P.din('wpg', [2, 8, 128, 1280])
    qkg = P.din('qkg', [128, 4 * 64])
    csd = P.din('cs', [128, NT + 1, 64])
    gat2 = P.din('gat2', [128, NT + 1, 48])
    P.din('nwq', [2, 128, KC, 512])
    P.din('nwo', [2, 128, 4, 1024])
    P.din('kts', [2, 128, 8192])
    P.din('vss', [128, 64, 256])
    P.din('ktw', [2, 128, 8192])
    P.din('vws', [128, 64, 256])
    P.din('kcr', [2, 128, 64, 256])
    P.din('w1r', [2, 128, 64 * 128])
    P.din('w2s', [128, 128])
    posr = P.din('posr', [128, 128])
    tabs = P.din('tabs', [128, 51])
    arow = P.din('arow', [128, 384])
    dm = P.din('dm', [128, 4 * 128], BF16)
    wm = P.din('wm', [128, 8 * 128], BF16)
    selc = P.din('selc', [128, 16 * 128], BF16)
    ident = P.din('ident', [128, 128])
    rmd = P.din('rmask', [128, 6])
    do_sample = phase >= 3 or phase == -1
    if do_sample:
        for nm_ in ('pool_kc', 'pool_vc', 'pool_ks', 'pool_vs'):
            P.din(nm_, [2560 * 128, 256])
        ptab = P.din('ptab', [128, 256], I32)
        pcol = P.din('pcol', [128, 1])
        P.din('ktn', [2, 2, 128, 24])
        P.din('vn', [4, NSEQ, 2, 256])
        P.din('ckwT', [2, 128, NSEQ * 512])
        P.din('cvwt', [128, NSEQ * 4, 256])
        P.din('gats', [4, NSEQ, 48])
        nmd = P.din('nm', [128, 16], BF16)
        wm0d = P.din('wm0', [128, 16], BF16)
    yT = P.dout('yT2', [128, KC, NCOL])
    with ExitStack() as stack:
        B = Builder(P, stack)
        tk = B.tk
        sb = B.sb
        B.qkg = sb("qkg", [128, 4, 64], F32)
        B.cs = sb("cs", [128, NT + 1, 64], F32)
        B.gat = sb("gat", [128, NT + 1, 48], F32)
        B.st4 = sb("st4", [128, 8], F32)
        B.posr = sb("posr", [128, 2, 64], F32)
        B.tabs = sb("tabs", [128, 3, 17], F32)
        B.arow = sb("arow", [128, 3, 128], F32)
        B.dm = sb("dm", [128, 4, 128], BF16)
        B.wm = sb("wm", [128, 8, 128], BF16)
        B.selc = sb("selc", [128, 16, 128], BF16)
        B.ident = sb("ident", [128, 128], F32)
        B.rmask = sb("rmask", [128, 6], F32)
        B.kcT = sb("kcT", [128, 2, 128], BF16)
        B.vcb = sb("vcb", [128, 256], BF16)
        B.do_sample = do_sample
        B.d_kc = nc.dram_tensor("d_kc", [5, 128, 256], BF16, kind="Internal")
        B.d_vc = nc.dram_tensor("d_vc", [5, 128, 256], BF16, kind="Internal")
        if do_sample:
            B.idx = sb("idx", [128, 256], I32)
            B.pcol = sb("pcol", [128, 1], F32)
            B.nm = sb("nm", [128, 16], BF16)
            B.wm0 = sb("wm0", [128, 16], BF16)
        if phase == 2:
            P.dout('d_o', [128, 1024])
        tk.op('pool', lambda e: e.memset(B.onesb[:, :], 1.0 / D), writes=[('onesb',)])
        tk.op('pool', lambda e: e.memset(B.epsb[:, :], EPS), writes=[('epsb',)])
        flat = lambda t: t[:, :, :].rearrange("p a b -> p (a b)")
        for (dst, src, key) in [(B.ngs[:, :], ng[:, :], 'ngs'), (flat(B.qkg), qkg[:, :], 'qkg'), (B.cs[:, :, :], csd[:, :, :], 'cs'),
                                (B.gat[:, :, :], gat2[:, :, :], 'gat'), (flat(B.posr), posr[:, :], 'posr'), (flat(B.tabs), tabs[:, :], 'tabs'),
                                (flat(B.arow), arow[:, :], 'arow'), (flat(B.dm), dm[:, :], 'dm'), (flat(B.wm), wm[:, :], 'wm'),
                                (flat(B.selc), selc[:, :], 'selc'), (B.ident[:, :], ident[:, :], 'ident'), (B.rmask[:, :], rmd[:, :], 'rmask')]:
            tk.op('sp', lambda e: e.dma_start(out=dst, in_=src), writes=[(key,)], dma=True)
        for ci, (a, b) in enumerate(CTS):
            tk.op('sp', lambda e: e.dma_start(out=B.resid[:, :, a:b], in_=xT[:, :, a:b]), writes=[('resid', ci)], dma=True)
        if do_sample:
            tk.op('sp', lambda e: e.dma_start(out=B.idx[:, :], in_=ptab[:, :]), writes=[('idx',)], dma=True)
            tk.op('sp', lambda e: e.dma_start(out=B.pcol[:, :], in_=pcol[:, :]), writes=[('pcol',)], dma=True)
            tk.op('sp', lambda e: e.dma_start(out=B.nm[:, :], in_=nmd[:, :]), writes=[('nm',)], dma=True)
            tk.op('sp', lambda e: e.dma_start(out=B.wm0[:, :], in_=wm0d[:, :]), writes=[('wm0',)], dma=True)
            idf = B.carve(0, [128, 256], F32)
            tk.op('dve', lambda e: e.tensor_copy(idf[:, :], B.idx[:, :]), reads=[('idx',)], writes=[('idf',)])
            tk.op('dve', lambda e: e.tensor_scalar(idf[:, :], idf[:, :], 128.0, B.pcol[:, 0:1], ALU.mult, ALU.add), reads=[('idf',), ('pcol',)], writes=[('idf',)])
            tk.op('dve', lambda e: e.tensor_copy(B.idx[:, :], idf[:, :]), reads=[('idf',)], writes=[('idx',)])
            tk.barrier()
        B.norm()
        tk.barrier()
        if phase >= 1 or do_sample:
            B.compress(cstop)
            if do_sample:
                for sq_ in range(NSEQ):
                    tk.barrier()
                    B.compress(cstop, 1 + sq_, sq_)
        if phase >= 2 or do_sample:
            for gp in range(ngp):
                tk.barrier()
                B.attention(gp, nk)
        tk.barrier()
        if phase == 1:
            dk = P.dout('d_kcT', [128, 256], BF16)
            dv = P.dout('d_vcb', [128, 256], BF16)
            tk.op('pool', lambda e: e.dma_start(out=dk[:, :], in_=B.kcT[:, :, :].rearrange("p a b -> p (a b)")), reads=[('kcT',)], writes=[('o_dk',)], dma=True)
            tk.op('pool', lambda e: e.dma_start(out=dv[:, :], in_=B.vcb[:, :]), reads=[('vcb',)], writes=[('o_dv',)], dma=True)
        if phase >= 3:
            B.ffn(1, 1)
            B.ple(1)
        for ci, (a, b) in enumerate(CTS):
            tk.op('pool', lambda e: e.dma_start(out=yT[:, :, a:b], in_=B.resid[:, :, a:b]), reads=[('resid', ci)], writes=[('o_y', ci)], dma=True)
        tk.wait_all('pool')
    return P


def tile_w(W, cols):
    din = W.shape[0]
    sub = W[:, cols]
    return np.ascontiguousarray(sub.reshape(din // 128, 128, -1).transpose(1, 0, 2).reshape(128, -1))


def fm(a):
    ncol, F = a.shape
    return np.ascontiguousarray(a.T.reshape(F // 128, 128, ncol).transpose(1, 0, 2))


def core_cols(r, xp, xs):
    b, c = r // 4, r % 4
    F = xp.shape[-1]
    out = np.zeros((NCOL, F), np.float32)
    for k in range(NT):
        i = 4 * k + c
        lo = 128 * i - 2
        if lo < 0:
            out[k * TW + 2:(k + 1) * TW] = xp[b, 0:128]
        else:
            out[k * TW:(k + 1) * TW] = xp[b, lo:lo + TW]
    for s in range(NSEQ):
        out[PCOL + s * SW + 2:PCOL + (s + 1) * SW] = xs[4 * r + s]
    return out


def prep_shared(inp):
    f = lambda k: np.asarray(inp[k], np.float32)
    sh = {}
    ngv = f('norm_g')
    sh['ng'] = np.ascontiguousarray(ngv.reshape(2, 4, KC, 128).transpose(3, 0, 1, 2).reshape(128, 64))
    wgu = f('ffn_w_gu')
    a = np.zeros((2, 2, 22, 128, KC * 256), np.float32)
    for l in range(2):
        for ff in range(2):
            for j in range(22):
                cols = list(range(128 * j, 128 * j + 128)) + list(range(DFF + 128 * j, DFF + 128 * j + 128))
                a[l, ff, j] = tile_w(wgu[l, ff], cols)
    sh['wgu'] = a
    sh['wdn'] = np.ascontiguousarray(f('ffn_w_down').reshape(2, 2, 22, 128, 1024))
    wg, wp = f('ple_w_gate'), f('ple_w_proj')
    a = np.zeros((2, 8, 128, 1280), np.float32)
    for l in range(2):
        for m in range(8):
            cols = list(range(128 * m, 128 * m + 128))
            a[l, m, :, 0:1024] = tile_w(wg[l], cols)
            a[l, m, :, 1024:1280] = tile_w(wp[l], cols)
    sh['wpg'] = a
    cwi = f('conv_w_in')[0]
    a = np.zeros((24, 128, 1024), np.float32)
    for ff in range(8):
        for kind, base in enumerate([1024, 2048, 0]):
            a[3 * ff + kind] = tile_w(cwi, list(range(base + 128 * ff, base + 128 * ff + 128)))
    sh['cwin'] = a
    sh['cw'] = np.ascontiguousarray(f('conv_w')[0].reshape(3, KC, 128).transpose(2, 0, 1).reshape(128, 24))
    cwo = f('conv_w_out')[0]
    sh['cwout'] = np.stack([tile_w(cwo, list(range(128 * m, 128 * m + 128))) for m in range(8)])
    return sh


def prep_core(r, inp, sh):
    f = lambda k: np.asarray(inp[k], np.float32)
    m = dict(sh)
    m['xT'] = fm(core_cols(r, f('x_prompt'), f('x_sample')))
    pp, psm = f('p_prompt'), f('p_sample')
    m['pT'] = np.stack([fm(core_cols(r, pp[l], psm[l])) for l in range(2)])
    hmv = np.ones((128, NT), np.float32)
    if r % 4 == 0:
        hmv[:, 0] = 0.0
    m['hm'] = hmv
    st = f('state_conv')[0][4 * r:4 * r + 4]
    m['stT'] = np.ascontiguousarray(st.reshape(NSEQ, 2, KC, 128).transpose(3, 2, 0, 1))
    return m


def rope_tab(pos):
    half = 32
    inv = (np.float32(10000.0) ** (-(np.arange(half, dtype=np.float32) / np.float32(half)))).astype(np.float32)
    ang = (pos.astype(np.float32)[:, None] * inv[None, :]).astype(np.float32)
    return np.concatenate([np.cos(ang), np.sin(ang)], axis=1).astype(np.float32)


def prep_shared2(inp, sh):
    f = lambda k: np.asarray(inp[k], np.float32)
    win = f('nsa_w_in')[0]
    sh['nwkv'] = np.ascontiguousarray(win[:, 1024:2608].reshape(KC, 128, 1584).transpose(1, 0, 2))
    sh['qkg'] = np.ascontiguousarray(np.broadcast_to(f('nsa_qk_g')[0].reshape(1, 256), (128, 256)))
    return sh


def prep_core2(r, inp, m):
    f = lambda k: np.asarray(inp[k], np.float32)
    c = r % 4
    cs = np.zeros((128, NT + 1, 64), np.float32)
    for k in range(NT):
        cs[:, k, :] = rope_tab(128 * (4 * k + c) + np.arange(128))
    spos = np.zeros(128, np.int64)
    for s in range(NSEQ):
        spos[s * SW + 2:s * SW + 6] = 8192 + np.arange(4)
    cs[:, NT, :] = rope_tab(spos)
    m['cs'] = cs
    m['ckw'] = np.ascontiguousarray(f('cache_k_win')[0, 4 * r:4 * r + 4].reshape(NSEQ, 512, 256))
    m['cvw'] = np.ascontiguousarray(f('cache_v_win')[0, 4 * r:4 * r + 4].reshape(NSEQ, 512, 256))
    return m


_CACHE = {}


def prep2_shared(inp, sh):
    f = lambda k: np.asarray(inp[k], np.float32)
    o = {k: sh[k] for k in ('ng', 'wgu', 'wdn', 'wpg', 'qkg')}
    win = f('nsa_w_in')[0]
    nwq = np.zeros((2, 128, KC, 512), np.float32)
    for gp in range(2):
        cols = []
        for j in range(4):
            for gg in range(2):
                h = 4 * (2 * gp + gg) + j
                cols += list(range(64 * h, 64 * h + 64))
        nwq[gp] = tile_w(win, cols).reshape(128, KC, 512)
    o['nwq'] = nwq
    wo = f('nsa_w_out')[0]
    o['nwo'] = np.ascontiguousarray(wo.reshape(2, 4, 128, 1024).transpose(0, 2, 1, 3))
    w1 = f('nsa_cmp_w1')[0]
    w1r = w1.reshape(2, 1, 64, 64 * 128)
    o['w1r'] = np.ascontiguousarray(np.broadcast_to(w1r, (2, 2, 64, 64 * 128)).reshape(2, 128, 64 * 128))
    w2 = f('nsa_cmp_w2')[0]
    o['w2s'] = np.ascontiguousarray(w2.transpose(1, 0, 2).reshape(128, 128))
    pos = f('nsa_cmp_pos')[0]
    pr = pos.transpose(1, 0, 2).reshape(1, 64, 128)
    o['posr'] = np.ascontiguousarray(np.broadcast_to(pr, (2, 64, 128)).reshape(128, 128))
    n = np.arange(128, dtype=np.float32)
    ar = np.stack([64 * n + 63, n, (n == 0).astype(np.float32)], 0).reshape(1, 384)
    o['arow'] = np.ascontiguousarray(np.broadcast_to(ar, (128, 384))).astype(np.float32)
    sel = np.zeros((128, 16, 128), np.float32)
    for a in range(4):
        for m in range(32):
            for kk in range(16):
                if m // 2 == kk:
                    e = m % 2
                    sel[32 * a + m, kk, 64 * e:64 * e + 64] = 1.0
    o['selc'] = sel.reshape(128, 2048).astype(ml_dtypes.bfloat16)
    pp = np.arange(128)
    o['rmask'] = np.stack([(pp < 64), (pp >= 64), (pp // 32 == 0), (pp // 32 == 1), (pp // 32 == 2), (pp // 32 == 3)], 1).astype(np.float32)
    o['ident'] = np.eye(128, dtype=np.float32)
    o['_inp'] = inp
    o['_pools'] = {nm_: np.ascontiguousarray(np.asarray(inp[key_], np.float32)[0].reshape(2560 * 128, 256))
                   for nm_, key_ in (('pool_kc', 'cache_k_cmp'), ('pool_vc', 'cache_v_cmp'), ('pool_ks', 'cache_k_sel'), ('pool_vs', 'cache_v_sel'))}
    return o


def prep2_core(r, o, m1, res1, full):
    b, c = r // 4, r % 4
    m = dict(o)
    m['resid2'] = np.asarray(res1[r]['yT'])
    m['gat2'] = np.asarray(res1[r]['gato'])
    m['pT'] = m1['pT']
    m['cs'] = m1['cs']
    m.update(full[b])
    rr = np.arange(128, dtype=np.float32)
    tabs = np.zeros((128, 3, 17), np.float32)
    for k in range(NT):
        qp = 128 * (4 * k + c) + rr
        tabs[:, 0, k] = qp
        tabs[:, 1, k] = np.floor(qp / 64)
        tabs[:, 2, k] = np.floor(qp / 64) - 1
    tabs[:, 0, 16] = 8192 + rr
    tabs[:, 1, 16] = 128
    tabs[:, 2, 16] = 127
    m['tabs'] = tabs.reshape(128, 51)
    key = np.arange(128)[:, None]
    q = np.arange(128)[None, :]
    dmv = np.zeros((128, 4, 128), np.float32)
    for rp in range(4):
        if rp < c:
            dmv[:, rp, :] = 1.0
        elif rp == c:
            dmv[:, rp, :] = (key <= q)
    m['dm'] = dmv.reshape(128, 512).astype(ml_dtypes.bfloat16)
    wmv = np.zeros((128, 8, 128), np.float32)
    for mm in range(8):
        diff = 128 * (c + 4 - mm) + (q - key)
        wmv[:, mm, :] = (diff >= 0) & (diff < 512)
    m['wm'] = wmv.reshape(128, 1024).astype(ml_dtypes.bfloat16)
    inp = o['_inp']
    for nm_, key_ in (('pool_kc', 'cache_k_cmp'), ('pool_vc', 'cache_v_cmp'), ('pool_ks', 'cache_k_sel'), ('pool_vs', 'cache_v_sel')):
        m[nm_] = o['_pools'][nm_]
    pt = np.asarray(inp['page_table'], np.int32)[4 * r:4 * r + 4].reshape(1, 256)
    m['ptab'] = np.ascontiguousarray(np.broadcast_to(pt, (128, 256))).astype(np.int32)
    m['pcol'] = np.arange(128, dtype=np.float32).reshape(128, 1)
    oks = np.asarray(res1[r]['o_kvs'])
    ktn = np.zeros((2, 2, 128, 24), np.float32)
    vnv = np.zeros((4, NSEQ, 2, 256), np.float32)
    for kind, (kc0, vc0) in enumerate([(512, 768), (1024, 1280)]):
        kk_ = oks[:, kc0:kc0 + 256]
        for gp in range(2):
            ktn[kind, gp] = kk_[:, gp * 128:(gp + 1) * 128].T
        for s_ in range(NSEQ):
            vnv[:, s_, kind, :] = oks[s_ * SW + 2:s_ * SW + 6, vc0:vc0 + 256]
    m['ktn'] = ktn
    m['vn'] = vnv
    ckw = np.asarray(inp['cache_k_win'], np.float32)[0, 4 * r:4 * r + 4].reshape(NSEQ, 512, 256)
    cvw = np.asarray(inp['cache_v_win'], np.float32)[0, 4 * r:4 * r + 4].reshape(NSEQ, 512, 256)
    m['ckwT'] = np.ascontiguousarray(ckw.transpose(2, 0, 1).reshape(2, 128, NSEQ * 512))
    m['cvwt'] = np.ascontiguousarray(cvw.reshape(NSEQ, 4, 128, 256).transpose(2, 0, 1, 3).reshape(128, NSEQ * 4, 256))
    g1 = np.asarray(res1[r]['gato'])[:, NT, :]
    gs_ = np.zeros((4, NSEQ, 48), np.float32)
    for s_ in range(NSEQ):
        gs_[:, s_, :] = g1[s_ * SW + 2:s_ * SW + 6]
    m['gats'] = gs_
    kq = np.arange(128)[:, None]
    jq = np.arange(16)[None, :] % 4
    m['nm'] = ((kq <= jq) & (kq < 4)).astype(np.float32).astype(ml_dtypes.bfloat16)
    m['wm0'] = (kq >= jq + 1).astype(np.float32).astype(ml_dtypes.bfloat16)
    return m


def kernel(**inp):
    if 'P' not in _CACHE:
        _CACHE['P'] = build()
        _CACHE['P2'] = build2()
    P, P2 = _CACHE['P'], _CACHE['P2']
    sh = prep_shared2(inp, prep_shared(inp))
    maps = []
    full_maps = []
    for r in range(NCORE):
        m = prep_core2(r, inp, prep_core(r, inp, sh))
        full_maps.append(m)
        maps.append({k: v for k, v in m.items() if k in P.ins})
    res = run_bass_kernel_spmd(P.nc, maps, core_ids=list(range(NCORE)))
    R = res.results
    B_, S_ = 2, 8192
    y_p = np.zeros((B_, S_, D), np.float32)
    y_s = np.zeros((32, 4, D), np.float32)
    conv_p = np.zeros((1, B_, 2, D), np.float32)
    conv_s = np.zeros((1, 32, 2, D), np.float32)
    kvp = [np.zeros((1, B_, S_, 4, 64), np.float32) for _ in range(6)]
    kwp = [np.zeros((1, B_, 512, 4, 64), np.float32) for _ in range(2)]
    kvs = [np.zeros((1, 32, 4, 4, 64), np.float32) for _ in range(4)]
    kws = [np.zeros((1, 32, 512, 4, 64), np.float32) for _ in range(2)]
    for r in range(NCORE):
        b, c = r // 4, r % 4
        cv = np.asarray(R[r]['cvoT']).transpose(2, 1, 0).reshape(2 + 2 * NSEQ, D)
        okp = np.asarray(R[r]['o_kvp'])
        oks = np.asarray(R[r]['o_kvs'])
        for k in range(NT):
            i = 4 * k + c
            for j in range(6):
                kvp[j][0, b, 128 * i:128 * i + 128] = okp[k, :, 256 * j:256 * j + 256].reshape(128, 4, 64)
        if c == 3:
            conv_p[0, b] = cv[0:2]
        for s in range(NSEQ):
            sg = 4 * r + s
            conv_s[0, sg] = cv[2 + 2 * s:4 + 2 * s]
            for j in range(4):
                kvs[j][0, sg] = oks[s * SW + 2:s * SW + 6, 256 * j:256 * j + 256].reshape(4, 4, 64)
            kws[0][0, sg] = np.asarray(R[r]['o_kws'])[s].reshape(512, 4, 64)
            kws[1][0, sg] = np.asarray(R[r]['o_vws'])[s].reshape(512, 4, 64)
    for j in range(2):
        kwp[j][0] = kvp[4 + j][0][:, S_ - 512:]
    full = []
    for b in range(B_):
        d = {}
        d['kcr'] = np.ascontiguousarray(np.stack([kvp[0][0, b].reshape(64, 128, 256), kvp[1][0, b].reshape(64, 128, 256)]).transpose(0, 2, 1, 3))
        ks = kvp[2][0, b].reshape(S_, 256)
        d['kts'] = np.ascontiguousarray(ks.T.reshape(2, 128, S_))
        d['vss'] = np.ascontiguousarray(kvp[3][0, b].reshape(64, 128, 256).transpose(1, 0, 2))
        kw = kvp[4][0, b].reshape(S_, 256)
        d['ktw'] = np.ascontiguousarray(kw.T.reshape(2, 128, S_))
        d['vws'] = np.ascontiguousarray(kvp[5][0, b].reshape(64, 128, 256).transpose(1, 0, 2))
        full.append(d)
    o2 = prep2_shared(inp, sh)
    maps2 = []
    for r in range(NCORE):
        m = prep2_core(r, o2, full_maps[r], R, full)
        maps2.append({k: v for k, v in m.items() if k in P2.ins})
    if _CACHE.get('hook') is not None:
        return _CACHE['hook'](maps2)
    res2 = run_bass_kernel_spmd(P2.nc, maps2, core_ids=list(range(NCORE)))
    R2 = res2.results
    for r in range(NCORE):
        b, c = r // 4, r % 4
        yt = np.asarray(R2[r]['yT2']).transpose(2, 1, 0).reshape(NCOL, D)
        for k in range(NT):
            i = 4 * k + c
            y_p[b, 128 * i:128 * i + 128] = yt[k * TW + 2:(k + 1) * TW]
        for s in range(NSEQ):
            y_s[4 * r + s] = yt[PCOL + s * SW + 2:PCOL + (s + 1) * SW]
    return (y_p, y_s, conv_p, conv_s, kvp[0], kvp[1], kvp[2], kvp[3], kwp[0], kwp[1],
            kvs[0], kvs[1], kvs[2], kvs[3], kws[0], kws[1])
```

```python
import numpy as np
import ml_dtypes
import concourse.bass as bass
import concourse.mybir as mybir
from concourse.bass_utils import run_bass_kernel_spmd

F32 = mybir.dt.float32
BF16 = mybir.dt.bfloat16
I32 = mybir.dt.int32
ALU = mybir.AluOpType
AF = mybir.ActivationFunctionType
AX = mybir.AxisListType

D = 1024
KC = 8
DFF = 2816
NCORE = 8
NT = 16
TW = 130
NSEQ = 4
SW = 6
PCOL = NT * TW
NCOL = PCOL + NSEQ * SW
CTS = [(0, 390), (390, 780), (780, 1170), (1170, 1560), (1560, 1950), (1950, NCOL)]
HGROUPS = [(0, 6), (6, 12), (12, 17), (17, 22)]
EPS = 1e-6


def bc(ap, pos, count):
    l = [list(x) for x in ap.ap]
    l.insert(pos, [0, count])
    return bass.AP(ap.tensor, ap.offset, l)


class TK:
    NDS = 24

    def __init__(self, nc, stack):
        self.nc = nc
        self.stack = stack
        self.eng = {'pe': nc.tensor, 'act': nc.scalar, 'dve': nc.vector, 'pool': nc.gpsimd, 'sp': nc.sync}
        self.sem = {}
        self.cnt = {}
        self.nsem = 0
        for e in self.eng:
            self._newsem(e)
        self.dsem = [stack.enter_context(nc.semaphore(f"dq{i}")) for i in range(self.NDS)]
        self.dcnt = [0] * self.NDS
        self.dnext = 0
        self.waited = {e: {} for e in self.eng}
        self.recs = {}
        self.sid = {}

    def _newsem(self, e):
        self.nsem += 1
        self.sem[e] = self.stack.enter_context(self.nc.semaphore(f"e{e}{self.nsem}"))
        self.cnt[e] = 0

    def _overlap(self, k):
        out = []
        g = self.recs.get(k[0])
        if g:
            n = len(k)
            for kk, rec in g.items():
                m = min(n, len(kk))
                if kk[:m] == k[:m]:
                    out.append((kk, rec))
        return out

    def _wait(self, e, deps):
        w = self.waited[e]
        best = {}
        for (sem, val) in deps:
            i = id(sem)
            if val > w.get(i, 0) and val > best.get(i, (None, 0))[1]:
                best[i] = (sem, val)
        for i, (sem, val) in best.items():
            self.eng[e].wait_ge(sem, val)
            w[i] = val

    def op(self, e, fn, reads=(), writes=(), dma=False, pg=(0, 128)):
        deps = []
        for k in reads:
            for kk, rec in self._overlap(k):
                if rec['w'] is not None:
                    deps.append(rec['w'])
        for k in writes:
            for kk, rec in self._overlap(k):
                if rec['w'] is not None:
                    deps.append(rec['w'])
                deps.extend(rec['r'].values())
        if e == 'pe':
            deps = [d for d in deps if d[0] is not self.sem['pe']]
        di = None
        if dma:
            di = self.dnext
            self.dnext = (di + 1) % self.NDS
            if self.dcnt[di] > 0:
                deps.append((self.dsem[di], 16 * self.dcnt[di]))
        if e == 'pe':
            if getattr(self, 'last_pg', pg) != pg and self.cnt['pe'] > 0:
                deps.append((self.sem['pe'], self.cnt['pe']))
            self.last_pg = pg
        self._wait(e, deps)
        ins = fn(self.eng[e])
        if dma:
            self.dcnt[di] += 1
            ins.then_inc(self.dsem[di], 16)
            tok = (self.dsem[di], 16 * self.dcnt[di])
        else:
            if self.cnt[e] >= 30000:
                self._newsem(e)
            self.cnt[e] += 1
            ins.then_inc(self.sem[e], 1)
            tok = (self.sem[e], self.cnt[e])
        for k in reads:
            g = self.recs.setdefault(k[0], {})
            rec = g.get(k)
            if rec is None:
                rec = {'w': None, 'r': {}}
                g[k] = rec
            rec['r'][id(tok[0])] = tok
        for k in writes:
            g = self.recs.setdefault(k[0], {})
            n = len(k)
            for kk in [kk for kk in g if len(kk) > n and kk[:n] == k]:
                del g[kk]
            g[k] = {'w': tok, 'r': {}}
        return tok

    def barrier(self):
        deps = []
        for g in self.recs.values():
            for rec in g.values():
                if rec['w'] is not None:
                    deps.append(rec['w'])
                deps.extend(rec['r'].values())
        for e in self.eng:
            self._wait(e, deps)
        self.recs = {}

    def wait_all(self, e):
        deps = []
        for g in self.recs.values():
            for rec in g.values():
                if rec['w'] is not None:
                    deps.append(rec['w'])
                deps.extend(rec['r'].values())
        self._wait(e, deps)


class Prog:
    def __init__(self, dbg=None):
        self.dbg = dbg
        self.nc = bass.Bass("TRN2", target_bir_lowering=False)
        self.ins = {}
        self.outs = {}

    def din(self, name, shape, dt=F32):
        t = self.nc.dram_tensor(name, list(shape), dt, kind="ExternalInput")
        self.ins[name] = t
        return t

    def dout(self, name, shape, dt=F32):
        t = self.nc.dram_tensor(name, list(shape), dt, kind="ExternalOutput")
        self.outs[name] = t
        return t


class Builder:
    def __init__(self, P, stack):
        self.P = P
        nc = self.nc = P.nc
        self.stack = stack
        self.tk = TK(nc, stack)
        sb = lambda name, shape, dt: stack.enter_context(nc.sbuf_tensor("s_" + name, list(shape), dt))
        self.sb = sb
        self.resid = sb("resid", [128, KC, NCOL], F32)
        self.xn = sb("xn", [128, KC, NCOL], BF16)
        self.onesb = sb("onesb", [128, 128], BF16)
        self.ngs = sb("ngs", [128, 2 * 4 * KC], F32)
        self.epsb = sb("epsb", [128, 1], F32)
        AR = 90112
        self.arena = sb("arena", [128, AR // 4], F32)

        def carve(off, shape, dt):
            n = 1
            for d in shape[1:]:
                n *= d
            esz = 4 if dt == F32 else 2
            assert off % 4 == 0 and off + n * esz <= AR, (off, shape)
            ap = self.arena[:, off // 4:(off + n * esz + 3) // 4]
            if dt != F32:
                ap = ap.bitcast(dt)
                ap = ap[:, 0:n]
            if len(shape) == 3:
                ap = ap.rearrange("p (a b) -> p a b", b=shape[2])
            elif len(shape) == 4:
                ap = ap.rearrange("p (a b c) -> p a b c", b=shape[2], c=shape[3])
            return ap
        self.carve = carve
        self.hid = carve(0, [128, KC, NCOL], BF16)
        self.wst = carve(33664, [128, 2, 2048], F32)
        self.wbf = carve(50048, [128, 3, 2048], BF16)
        self.sq = carve(62336, [128, KC, 390], BF16)
        self.tmp = carve(68576, [128, 2, 390], F32)
        self.rstd = carve(71696, [128, 390], F32)
        self.wdg = carve(73256, [128, 6, 1024], BF16)
        self.ub = carve(73256, [128, NCOL], F32)
        self.yb = carve(81672, [128, NCOL], F32)
        self.ps = [stack.enter_context(nc.psum_tensor(f"ps{i}", [128, 512], F32)) for i in range(8)]
        self.bank = 0
        self.wslot = 0
        self.sslot = 0
        self.tmpi = 0

    def nb(self):
        b = self.bank
        self.bank = (b + 1) % 8
        return b

    def ns(self):
        q = self.sslot
        self.sslot = (q + 1) % 2
        return q

    def nt(self):
        t = self.tmpi
        self.tmpi = (t + 1) % 2
        return t

    def load_w(self, src, n, g=None, m=None, parts=None):
        tk = self.tk
        s = self.wslot
        self.wslot = (s + 1) % 3
        q = self.ns()
        wst, wbf = self.wst, self.wbf
        tk.op('sp', lambda e: e.dma_start(out=wst[:, q, 0:n], in_=src), writes=[('wst', q)], dma=True)
        if parts is None:
            parts = [(0, n, g, m)]
        for (a, b, gg, mm) in parts:
            if gg is None:
                tk.op('dve', lambda e: e.tensor_copy(wbf[:, s, a:b], wst[:, q, a:b]),
                      reads=[('wst', q)], writes=[('wbf', s, a)])
            else:
                tk.op('dve', lambda e: e.tensor_tensor(
                    wbf[:, s, a:b].rearrange("p (c m) -> p c m", m=mm),
                    wst[:, q, a:b].rearrange("p (c m) -> p c m", m=mm),
                    bc(gg, 2, mm), ALU.mult),
                    reads=[('wst', q), ('ngs',)], writes=[('wbf', s, a)])
        return s

    def stream(self, blocks, body):
        pend = [self.load_w(*blocks[0])]
        for i in range(len(blocks)):
            if i + 1 < len(blocks):
                pend.append(self.load_w(*blocks[i + 1]))
            body(i, pend.pop(0))

    def gain(self, l, i):
        o = (l * 4 + i) * KC
        return self.ngs[:, o:o + KC]

    def norm(self):
        tk = self.tk
        resid, xn, sq, rstd, ps = self.resid, self.xn, self.sq, self.rstd, self.ps
        for ci, (a, b) in enumerate(CTS):
            n = b - a
            tk.op('act', lambda e: e.activation(sq[:, :, 0:n], resid[:, :, a:b], AF.Square),
                  reads=[('resid', ci)], writes=[('sq',)])
            bk = self.nb()
            for c in range(KC):
                tk.op('pe', lambda e: e.matmul(ps[bk][:, 0:n], self.onesb[:, :], sq[:, c, 0:n],
                                               start=(c == 0), stop=(c == KC - 1)),
                      reads=[('sq',), ('onesb',)], writes=[('ps', bk)])
            tk.op('act', lambda e: e.activation(rstd[:, 0:n], ps[bk][:, 0:n], AF.Ln, bias=self.epsb[:, :]),
                  reads=[('ps', bk), ('epsb',)], writes=[('rstd',)])
            tk.op('act', lambda e: e.activation(rstd[:, 0:n], rstd[:, 0:n], AF.Exp, scale=-0.5),
                  reads=[('rstd',)], writes=[('rstd',)])
            tk.op('pool', lambda e: e.tensor_tensor(xn[:, :, a:b], resid[:, :, a:b], bc(rstd[:, 0:n], 1, KC), ALU.mult),
                  reads=[('resid', ci), ('rstd',)], writes=[('xn', ci)])

    def ffn(self, l, f):
        tk = self.tk
        P = self.P
        resid, xn, hid, wbf, wdg, wst, tmp, ps = self.resid, self.xn, self.hid, self.wbf, self.wdg, self.wst, self.tmp, self.ps
        self.norm()
        g = self.gain(l, 2 * f)
        wgu = P.ins['wgu']
        wdn = P.ins['wdn']
        for (h0, h1) in HGROUPS:
            blocks = [(wgu[l, f, j], KC * 256, g, 256) for j in range(h0, h1)]

            def body(i, s, h0=h0):
                j = h0 + i
                for ci, (a, b) in enumerate(CTS):
                    n = b - a
                    bg, bu = self.nb(), self.nb()
                    for c in range(KC):
                        tk.op('pe', lambda e: e.matmul(ps[bg][:, 0:n], wbf[:, s, c * 256:c * 256 + 128], xn[:, c, a:b],
                                                       start=(c == 0), stop=(c == KC - 1)),
                              reads=[('wbf', s), ('xn', ci)], writes=[('ps', bg)])
                    for c in range(KC):
                        tk.op('pe', lambda e: e.matmul(ps[bu][:, 0:n], wbf[:, s, c * 256 + 128:c * 256 + 256], xn[:, c, a:b],
                                                       start=(c == 0), stop=(c == KC - 1)),
                              reads=[('wbf', s), ('xn', ci)], writes=[('ps', bu)])
                    t = self.nt()
                    tk.op('act', lambda e: e.activation(tmp[:, t, 0:n], ps[bg][:, 0:n], AF.Silu),
                          reads=[('ps', bg)], writes=[('tmp', t)])
                    tk.op('dve', lambda e: e.tensor_tensor(hid[:, i, a:b], tmp[:, t, 0:n], ps[bu][:, 0:n], ALU.mult),
                          reads=[('tmp', t), ('ps', bu)], writes=[('hid', i, ci)])
                s2 = self.ns()
                tk.op('sp', lambda e: e.dma_start(out=wst[:, s2, 0:1024], in_=wdn[l, f, j]), writes=[('wst', s2)], dma=True)
                tk.op('dve', lambda e: e.tensor_copy(wdg[:, i, :], wst[:, s2, 0:1024]),
                      reads=[('wst', s2)], writes=[('wdg', i)])

            self.stream(blocks, body)
            ng_ = h1 - h0
            for m in range(KC):
                for ci, (a, b) in enumerate(CTS):
                    n = b - a
                    bk = self.nb()
                    for i in range(ng_):
                        tk.op('pe', lambda e: e.matmul(ps[bk][:, 0:n], wdg[:, i, m * 128:(m + 1) * 128], hid[:, i, a:b],
                                                       start=(i == 0), stop=(i == ng_ - 1)),
                              reads=[('wdg', i), ('hid', i, ci)], writes=[('ps', bk)])
                    tk.op('dve', lambda e: e.scalar_tensor_tensor(resid[:, m, a:b], ps[bk][:, 0:n], 0.5, resid[:, m, a:b],
                                                                  ALU.mult, ALU.add),
                          reads=[('ps', bk), ('resid', ci, m)], writes=[('resid', ci, m)])

    def ple(self, l):
        tk = self.tk
        P = self.P
        resid, xn, hid, wbf, wst, tmp, ps = self.resid, self.xn, self.hid, self.wbf, self.wst, self.tmp, self.ps
        self.norm()
        g = self.gain(l, 3)
        pT = P.ins['pT']
        for ci, (a, b) in enumerate(CTS):
            n = b - a
            s = self.ns()
            tk.op('sp', lambda e: e.dma_start(out=wst[:, s, 0:2 * n].rearrange("p (c n) -> p c n", c=2), in_=pT[l, :, :, a:b]),
                  writes=[('wst', s)], dma=True)
            tk.op('dve', lambda e: e.tensor_copy(hid[:, 0:2, a:b], wst[:, s, 0:2 * n].rearrange("p (c n) -> p c n", c=2)),
                  reads=[('wst', s)], writes=[('hid', 0, ci), ('hid', 1, ci)])
        wpg = P.ins['wpg']
        blocks = [(wpg[l, m], 1280, None, None, [(0, 1024, g, 128), (1024, 1280, None, None)]) for m in range(KC)]

        def body(m, s):
            for ci, (a, b) in enumerate(CTS):
                n = b - a
                bg, bp = self.nb(), self.nb()
                for c in range(KC):
                    tk.op('pe', lambda e: e.matmul(ps[bg][:, 0:n], wbf[:, s, c * 128:(c + 1) * 128], xn[:, c, a:b],
                                                   start=(c == 0), stop=(c == KC - 1)),
                          reads=[('wbf', s), ('xn', ci)], writes=[('ps', bg)])
                for c in range(2):
                    tk.op('pe', lambda e: e.matmul(ps[bp][:, 0:n], wbf[:, s, 1024 + c * 128:1024 + (c + 1) * 128], hid[:, c, a:b],
                                                   start=(c == 0), stop=(c == 1)),
                          reads=[('wbf', s), ('hid', c, ci)], writes=[('ps', bp)])
                t = self.nt()
                tk.op('act', lambda e: e.activation(tmp[:, t, 0:n], ps[bg][:, 0:n], AF.Sigmoid),
                      reads=[('ps', bg)], writes=[('tmp', t)])
                tk.op('dve', lambda e: e.tensor_tensor(tmp[:, t, 0:n], tmp[:, t, 0:n], ps[bp][:, 0:n], ALU.mult),
                      reads=[('tmp', t), ('ps', bp)], writes=[('tmp', t)])
                tk.op('pool', lambda e: e.tensor_tensor(resid[:, m, a:b], resid[:, m, a:b], tmp[:, t, 0:n], ALU.add),
                      reads=[('tmp', t), ('resid', ci, m)], writes=[('resid', ci, m)])

        self.stream(blocks, body)

    def linear_resid(self, wtiles, g):
        tk = self.tk
        resid, hid, wbf, ps = self.resid, self.hid, self.wbf, self.ps
        blocks = [(wtiles[m], KC * 128, g, 128) for m in range(KC)]

        def body(m, s):
            for ci, (a, b) in enumerate(CTS):
                n = b - a
                bk = self.nb()
                for c in range(KC):
                    tk.op('pe', lambda e: e.matmul(ps[bk][:, 0:n], wbf[:, s, c * 128:(c + 1) * 128], hid[:, c, a:b],
                                                   start=(c == 0), stop=(c == KC - 1)),
                          reads=[('wbf', s), ('hid', c, ci)], writes=[('ps', bk)])
                tk.op('dve', lambda e: e.tensor_tensor(resid[:, m, a:b], resid[:, m, a:b], ps[bk][:, 0:n], ALU.add),
                      reads=[('ps', bk), ('resid', ci, m)], writes=[('resid', ci, m)])

        self.stream(blocks, body)

    def conv(self):
        tk = self.tk
        P = self.P
        resid, xn, hid, wbf, tmp, ps = self.resid, self.xn, self.hid, self.wbf, self.tmp, self.ps
        ub, yb, cws, hms, sts, cvo = self.ub, self.yb, self.cws, self.hms, self.sts, self.cvo
        self.norm()
        g = self.gain(0, 1)
        cwin = P.ins['cwin']
        blocks = [(cwin[i], KC * 128, g, 128) for i in range(24)]

        def mm(s, ci, a, b):
            n = b - a
            bk = self.nb()
            for c in range(KC):
                tk.op('pe', lambda e: e.matmul(ps[bk][:, 0:n], wbf[:, s, c * 128:(c + 1) * 128], xn[:, c, a:b],
                                               start=(c == 0), stop=(c == KC - 1)),
                      reads=[('wbf', s), ('xn', ci)], writes=[('ps', bk)])
            return bk

        def body(i, s):
            f, kind = i // 3, i % 3
            if kind == 0:
                for ci, (a, b) in enumerate(CTS):
                    bk = mm(s, ci, a, b)
                    tk.op('act', lambda e: e.activation(ub[:, a:b], ps[bk][:, 0:b - a], AF.Copy),
                          reads=[('ps', bk)], writes=[('ub', ci)])
            elif kind == 1:
                for ci, (a, b) in enumerate(CTS):
                    bk = mm(s, ci, a, b)
                    tk.op('dve', lambda e: e.tensor_tensor(ub[:, a:b], ub[:, a:b], ps[bk][:, 0:b - a], ALU.mult),
                          reads=[('ps', bk), ('ub', ci)], writes=[('ub', ci)])
                uh = ub[:, 0:PCOL].rearrange("p (k w) -> p k w", w=TW)[:, :, 0:2]
                tk.op('dve', lambda e: e.tensor_tensor(uh, uh, bc(hms[:, :], 2, 2), ALU.mult),
                      reads=[('ub',), ('hms',)], writes=[('ub',)])
                us = ub[:, PCOL:NCOL].rearrange("p (s w) -> p s w", w=SW)[:, :, 0:2]
                tk.op('dve', lambda e: e.tensor_copy(us, sts[:, f, :, :]),
                      reads=[('sts',)], writes=[('ub',)])
                tk.op('dve', lambda e: e.tensor_copy(cvo[:, f, 0:2], ub[:, PCOL - 2:PCOL]),
                      reads=[('ub',)], writes=[('cvo', f)])
                usn = ub[:, PCOL:NCOL].rearrange("p (s w) -> p s w", w=SW)[:, :, 4:6]
                tk.op('dve', lambda e: e.tensor_copy(cvo[:, f, 2:2 + 2 * NSEQ].rearrange("p (s w) -> p s w", w=2), usn),
                      reads=[('ub',)], writes=[('cvo', f)])
                n2 = NCOL - 2
                tk.op('dve', lambda e: e.tensor_scalar(yb[:, 2:NCOL], ub[:, 2:NCOL], cws[:, 2 * KC + f:2 * KC + f + 1], None, ALU.mult),
                      reads=[('ub',), ('cws',)], writes=[('yb',)])
                tk.op('dve', lambda e: e.scalar_tensor_tensor(yb[:, 2:NCOL], ub[:, 1:NCOL - 1], cws[:, KC + f:KC + f + 1], yb[:, 2:NCOL],
                                                               ALU.mult, ALU.add),
                      reads=[('ub',), ('cws',), ('yb',)], writes=[('yb',)])
                tk.op('dve', lambda e: e.scalar_tensor_tensor(yb[:, 2:NCOL], ub[:, 0:n2], cws[:, f:f + 1], yb[:, 2:NCOL],
                                                               ALU.mult, ALU.add),
                      reads=[('ub',), ('cws',), ('yb',)], writes=[('yb',)])
            else:
                for ci, (a, b) in enumerate(CTS):
                    bk = mm(s, ci, a, b)
                    tk.op('dve', lambda e: e.tensor_tensor(hid[:, f, a:b], yb[:, a:b], ps[bk][:, 0:b - a], ALU.mult),
                          reads=[('ps', bk), ('yb',)], writes=[('hid', f, ci)])

        self.stream(blocks, body)
        cwout = P.ins['cwout']
        self.linear_resid([cwout[m] for m in range(KC)], None)


    def nsa_proj(self):
        tk = self.tk
        P = self.P
        xn, hid, wst, tmp, ps, rstd = self.xn, self.hid, self.wst, self.tmp, self.ps, self.rstd
        self.norm()
        g = self.gain(1, 1)
        nwkv = P.ins['nwkv']
        NKV = 1584
        hf = hid[:, :, :].rearrange("p c n -> p (c n)")
        for c in range(KC):
            q = self.ns()
            tk.op('sp', lambda e: e.dma_start(out=wst[:, q, 0:NKV], in_=nwkv[:, c, :]), writes=[('wst', q)], dma=True)
            tk.op('dve', lambda e: e.tensor_scalar(hf[:, c * NKV:(c + 1) * NKV], wst[:, q, 0:NKV], g[:, c:c + 1], None, ALU.mult),
                  reads=[('wst', q), ('ngs',)], writes=[('hid',)])
        kvo = [self.ub, self.yb]
        kvk = [('ub',), ('yb',)]
        s1 = rstd[:, 0:128]
        s2 = rstd[:, 128:256]
        st4 = self.st4
        o_kvp, o_kvs = P.outs['o_kvp'], P.outs['o_kvs']
        for ti in range(NT + 1):
            if ti < NT:
                c0, R = ti * TW + 2, 128
            else:
                c0, R = PCOL, NSEQ * SW
            ko = kvo[ti % 2]
            kk = kvk[ti % 2]
            banks = []
            for (a, b) in [(0, 512), (512, 1024), (1024, 1536), (1536, NKV)]:
                bk = self.nb()
                banks.append(bk)
                for c in range(KC):
                    tk.op('pe', lambda e: e.matmul(ps[bk][0:R, 0:b - a], xn[:, c, c0:c0 + R], hf[:, c * NKV + a:c * NKV + b],
                                                   start=(c == 0), stop=(c == KC - 1)),
                          reads=[('hid',), ('xn',)], writes=[('ps', bk)])
            bA, bB, bC, bD = banks
            tk.op('act', lambda e: e.activation(ko[0:R, 0:512], ps[bA][0:R, 0:512], AF.Copy), reads=[('ps', bA)], writes=[kk + (0,)])
            tk.op('act', lambda e: e.activation(ko[0:R, 768:1024], ps[bB][0:R, 256:512], AF.Copy), reads=[('ps', bB)], writes=[kk + (3,)])
            tk.op('act', lambda e: e.activation(ko[0:R, 1280:1536], ps[bC][0:R, 256:512], AF.Copy), reads=[('ps', bC)], writes=[kk + (5,)])
            tk.op('act', lambda e: e.activation(self.gat[0:R, ti, :], ps[bD][0:R, 0:48], AF.Sigmoid), reads=[('ps', bD)], writes=[('gat', ti)])
            for (bk, gi, oc, part) in [(bB, 2, 512, 2), (bC, 3, 1024, 4)]:
                x = ps[bk][0:R, 0:256]
                x3 = x.rearrange("p (g d) -> p g d", d=64)
                tk.op('act', lambda e: e.activation(tmp[0:R, 0, 0:256], x, AF.Square), reads=[('ps', bk)], writes=[('tmp', 0)])
                tk.op('dve', lambda e: e.tensor_reduce(st4[0:R, 0:4], tmp[0:R, 0, 0:256].rearrange("p (g d) -> p g d", d=64), AX.X, ALU.add),
                      reads=[('tmp', 0)], writes=[('st4',)])
                tk.op('act', lambda e: e.activation(st4[0:R, 4:8], st4[0:R, 0:4], AF.Ln, bias=self.epsb[0:R, :], scale=1.0 / 64),
                      reads=[('st4',), ('epsb',)], writes=[('st4',)])
                tk.op('act', lambda e: e.activation(st4[0:R, 4:8], st4[0:R, 4:8], AF.Exp, scale=-0.5),
                      reads=[('st4',)], writes=[('st4',)])
                t1 = tmp[0:R, 1, 0:256].rearrange("p (g d) -> p g d", d=64)
                tk.op('dve', lambda e: e.tensor_tensor(t1, x3, bc(st4[0:R, 4:8], 2, 64), ALU.mult),
                      reads=[('ps', bk), ('st4',)], writes=[('tmp', 1)])
                tk.op('dve', lambda e: e.tensor_tensor(t1, t1, bc(self.qkg[0:R, gi, :], 1, 4), ALU.mult),
                      reads=[('tmp', 1), ('qkg',)], writes=[('tmp', 1)])
                x1, x2 = t1[:, :, 0:32], t1[:, :, 32:64]
                cosb = bc(self.cs[0:R, ti, 0:32], 1, 4)
                sinb = bc(self.cs[0:R, ti, 32:64], 1, 4)
                o3 = ko[0:R, oc:oc + 256].rearrange("p (g d) -> p g d", d=64)
                a1 = s1[0:R, :].rearrange("p (g d) -> p g d", d=32)
                a2 = s2[0:R, :].rearrange("p (g d) -> p g d", d=32)
                rd = [('tmp', 1), ('cs',)]
                tk.op('pool', lambda e: e.tensor_tensor(a1, x1, cosb, ALU.mult), reads=rd, writes=[('rstd', 0)])
                tk.op('pool', lambda e: e.tensor_tensor(a2, x2, sinb, ALU.mult), reads=rd, writes=[('rstd', 1)])
                tk.op('pool', lambda e: e.tensor_tensor(o3[:, :, 0:32], a1, a2, ALU.subtract), reads=[('rstd',)], writes=[kk + (part,)])
                tk.op('pool', lambda e: e.tensor_tensor(a1, x2, cosb, ALU.mult), reads=rd, writes=[('rstd', 0)])
                tk.op('pool', lambda e: e.tensor_tensor(a2, x1, sinb, ALU.mult), reads=rd, writes=[('rstd', 1)])
                tk.op('pool', lambda e: e.tensor_tensor(o3[:, :, 32:64], a1, a2, ALU.add), reads=[('rstd',)], writes=[kk + (part + 100,)])
            if ti < NT:
                tk.op('pool', lambda e: e.dma_start(out=o_kvp[ti, :, :], in_=ko[0:R, 0:1536]), reads=[kk], writes=[('o_kvp', ti)], dma=True)
            else:
                tk.op('pool', lambda e: e.dma_start(out=o_kvs[:, :], in_=ko[0:R, 0:1536]), reads=[kk], writes=[('o_kvs',)], dma=True)
                for sq_ in range(NSEQ):
                    r0 = sq_ * SW + 2
                    tk.op('pool', lambda e: e.dma_start(out=P.outs['o_kws'][sq_, 508:512, :], in_=ko[r0:r0 + 4, 1024:1280]),
                          reads=[kk], writes=[('o_kws', sq_)], dma=True)
                    tk.op('pool', lambda e: e.dma_start(out=P.outs['o_vws'][sq_, 508:512, :], in_=ko[r0:r0 + 4, 1280:1536]),
                          reads=[kk], writes=[('o_vws', sq_)], dma=True)


    def stage_cast(self, src, dst, n, eng, stg, in1=None):
        tk = self.tk
        q = self.ns()
        tk.op('sp', lambda e: e.dma_start(out=stg[:, q, 0:n], in_=src), writes=[('stg', q)], dma=True)
        return q

    def compress(self, cstop=9, slot=0, seq=None):
        tk = self.tk
        P = self.P
        ps = self.ps
        cv = self.carve
        w1b = cv(0, [128, 64, 128], BF16)
        kcb = cv(16384, [128, 64, 256], BF16)
        stg = cv(49152, [128, 2, 2048], F32)
        gel = cv(65536, [128, 512], BF16)
        wk = cv(66560, [128, 3, 512], F32)
        cmt = cv(72704, [128, 256], F32)
        w2b = cv(73728, [128, 2, 64], BF16)
        st4 = self.st4
        w1r, kcr, w2s = P.ins['w1r'], P.ins['kcr'], P.ins['w2s']
        q = self.ns()
        tk.op('sp', lambda e: e.dma_start(out=stg[:, q, 0:128], in_=w2s[:, :]), writes=[('stg', q)], dma=True)
        tk.op('dve', lambda e: e.tensor_copy(w2b[:, :, :].rearrange("p a b -> p (a b)"), stg[:, q, 0:128]), reads=[('stg', q)], writes=[('w2b',)])
        w1f = w1b[:, :, :].rearrange("p a b -> p (a b)")
        for kv in range(2):
            for pc in range(4):
                q = self.ns()
                tk.op('sp', lambda e: e.dma_start(out=stg[:, q, :], in_=w1r[kv, :, pc * 2048:(pc + 1) * 2048]), writes=[('stg', q)], dma=True)
                tk.op('dve', lambda e: e.tensor_copy(w1f[:, pc * 2048:(pc + 1) * 2048], stg[:, q, :]), reads=[('stg', q)], writes=[('w1b', pc)])
            if seq is None:
                for pc in range(8):
                    q = self.ns()
                    tk.op('sp', lambda e: e.dma_start(out=stg[:, q, :].rearrange("p (t c) -> p t c", c=256), in_=kcr[kv, :, pc * 8:(pc + 1) * 8, :]),
                          writes=[('stg', q)], dma=True)
                    tk.op('pool', lambda e: e.tensor_tensor(
                        kcb[:, pc * 8:(pc + 1) * 8, :].rearrange("p t (g d) -> p t g d", d=64),
                        stg[:, q, :].rearrange("p (t g d) -> p t g d", g=4, d=64),
                        bc(bc(self.posr[:, kv, :], 1, 4), 1, 8), ALU.add),
                        reads=[('stg', q), ('posr',)], writes=[('kcb', pc)])
            else:
                pool_d = P.ins['pool_kc' if kv == 0 else 'pool_vc']
                sg = stg[:, :, :].rearrange("p a (b c) -> p (a b) c", c=256)
                for page in range(64):
                    q = page % 16
                    col = seq * 64 + page
                    tk.op('pool', lambda e: e.indirect_dma_start(out=sg[:, q, :], out_offset=None, in_=pool_d[:, :],
                                                                 in_offset=bass.IndirectOffsetOnAxis(ap=self.idx[:, col:col + 1], axis=0)),
                          reads=[('idx',)], writes=[('stg', q // 8, q % 8)], dma=True)
                    tk.op('dve', lambda e: e.tensor_tensor(
                        kcb[:, page, :].rearrange("p (g d) -> p g d", d=64),
                        sg[:, q, :].rearrange("p (g d) -> p g d", d=64),
                        bc(self.posr[:, kv, :], 1, 4), ALU.add),
                        reads=[('stg', q // 8, q % 8), ('posr',)], writes=[('kcb', page)])
            if cstop <= 1:
                continue
            for e_ in range(2):
                for g in range(4):
                    po = ps[1][:, g * 128:(g + 1) * 128].rearrange("p (t e) -> p t e", e=2)[:, :, e_]
                    for d in range(64):
                        tk.op('pe', lambda e: e.matmul(po, w1b[64 * e_:64 * e_ + 64, d, :],
                                                       kcb[64 * e_:64 * e_ + 64, :, g * 64 + d], start=(d == 0), stop=(d == 63)),
                              reads=[('w1b',), ('kcb',)], writes=[('ps', 1)], pg=(64 * e_, 64))
            if cstop <= 2:
                continue
            x = ps[1][:, 0:512]
            tk.op('act', lambda e: e.activation(wk[:, 0, :], x, AF.Square), reads=[('ps', 1)], writes=[('wk', 0)])
            tk.op('dve', lambda e: e.tensor_scalar(wk[:, 0, :], wk[:, 0, :], 0.044715, 1.0, ALU.mult, ALU.add), reads=[('wk', 0)], writes=[('wk', 0)])
            tk.op('dve', lambda e: e.tensor_tensor(wk[:, 1, :], wk[:, 0, :], x, ALU.mult), reads=[('wk', 0), ('ps', 1)], writes=[('wk', 1)])
            tk.op('act', lambda e: e.activation(wk[:, 2, :], wk[:, 1, :], AF.Sigmoid, scale=1.5957691216057308), reads=[('wk', 1)], writes=[('wk', 2)])
            tk.op('dve', lambda e: e.tensor_tensor(gel[:, :], wk[:, 2, :], x, ALU.mult), reads=[('wk', 2), ('ps', 1)], writes=[('gel',)])
            if cstop <= 3:
                continue
            for g in range(4):
                tk.op('pe', lambda e: e.matmul(ps[2][:, g * 64:(g + 1) * 64], gel[:, g * 128:(g + 1) * 128], w2b[:, kv, :],
                                               start=True, stop=True),
                      reads=[('gel',), ('w2b',)], writes=[('ps', 2)])
            if cstop <= 4:
                continue
            y = ps[2][:, 0:256]
            if kv == 0:
                tk.op('act', lambda e: e.activation(wk[:, 0, 0:256], y, AF.Square), reads=[('ps', 2)], writes=[('wk', 0)])
                tk.op('dve', lambda e: e.tensor_reduce(st4[:, 0:4], wk[:, 0, 0:256].rearrange("p (g d) -> p g d", d=64), AX.X, ALU.add),
                      reads=[('wk', 0)], writes=[('st4',)])
                tk.op('act', lambda e: e.activation(st4[:, 4:8], st4[:, 0:4], AF.Ln, bias=self.epsb[:, :], scale=1.0 / 64),
                      reads=[('st4',), ('epsb',)], writes=[('st4',)])
                tk.op('act', lambda e: e.activation(st4[:, 4:8], st4[:, 4:8], AF.Exp, scale=-0.5), reads=[('st4',)], writes=[('st4',)])
                c3 = cmt[:, :].rearrange("p (g d) -> p g d", d=64)
                tk.op('dve', lambda e: e.tensor_tensor(c3, y.rearrange("p (g d) -> p g d", d=64), bc(st4[:, 4:8], 2, 64), ALU.mult),
                      reads=[('ps', 2), ('st4',)], writes=[('cmt',)])
                tk.op('dve', lambda e: e.tensor_tensor(c3, c3, bc(self.qkg[:, 1, :], 1, 4), ALU.mult), reads=[('cmt',), ('qkg',)], writes=[('cmt',)])
                for gp in range(2):
                    tk.op('pe', lambda e: e.transpose(ps[3][:, gp * 128:(gp + 1) * 128], cmt[:, gp * 128:(gp + 1) * 128], self.ident[:, :]),
                          reads=[('cmt',), ('ident',)], writes=[('ps', 3)])
                tk.op('act', lambda e: e.activation(self.kcT[:, :, :].rearrange("p a b -> p (a b)"), ps[3][:, 0:256], AF.Copy),
                      reads=[('ps', 3)], writes=[('kcT',)])
            else:
                tk.op('act', lambda e: e.activation(self.vcb[:, :], y, AF.Copy), reads=[('ps', 2)], writes=[('vcb',)])
        if cstop >= 9:
            tk.op('sp', lambda e: e.dma_start(out=self.d_kc[slot, :, :], in_=self.kcT[:, :, :].rearrange("p a b -> p (a b)")),
                  reads=[('kcT',)], writes=[('d_kc', slot)], dma=True)
            tk.op('sp', lambda e: e.dma_start(out=self.d_vc[slot, :, :], in_=self.vcb[:, :]), reads=[('vcb',)], writes=[('d_vc', slot)], dma=True)

    def qk_norm_rope(self, src, R, H, gi, ti, dst0, dst1, scr_sq, sm, a1, a2, scale, k0, k1, ka):
        tk = self.tk
        x3 = src.rearrange("p (g d) -> p g d", d=64)
        tk.op('act', lambda e: e.activation(scr_sq, src, AF.Square), reads=[ka], writes=[k1])
        tk.op('dve', lambda e: e.tensor_reduce(sm[0:R, 0:H], scr_sq.rearrange("p (g d) -> p g d", d=64), AX.X, ALU.add), reads=[k1], writes=[('sm', 0)])
        tk.op('act', lambda e: e.activation(sm[0:R, H:2 * H], sm[0:R, 0:H], AF.Ln, bias=self.epsb[0:R, :], scale=1.0 / 64),
              reads=[('sm', 0), ('epsb',)], writes=[('sm', 1)])
        tk.op('act', lambda e: e.activation(sm[0:R, H:2 * H], sm[0:R, H:2 * H], AF.Exp, scale=-0.5), reads=[('sm', 1)], writes=[('sm', 1)])
        d0 = dst0.rearrange("p (g d) -> p g d", d=64)
        d1 = dst1.rearrange("p (g d) -> p g d", d=64)
        tk.op('dve', lambda e: e.tensor_tensor(d0, x3, bc(sm[0:R, H:2 * H], 2, 64), ALU.mult), reads=[ka, ('sm', 1)], writes=[k0])
        tk.op('dve', lambda e: e.scalar_tensor_tensor(d0, d0, scale, bc(self.qkg[0:R, gi, :], 1, H), ALU.mult, ALU.mult),
              reads=[k0, ('qkg',)], writes=[k0])
        x1, x2 = d0[:, :, 0:32], d0[:, :, 32:64]
        cosb = bc(self.cs[0:R, ti, 0:32], 1, H)
        sinb = bc(self.cs[0:R, ti, 32:64], 1, H)
        b1 = a1.rearrange("p (g d) -> p g d", d=32)
        b2 = a2.rearrange("p (g d) -> p g d", d=32)
        rd = [k0, ('cs',)]
        tk.op('pool', lambda e: e.tensor_tensor(b1, x1, cosb, ALU.mult), reads=rd, writes=[('ow', 0)])
        tk.op('pool', lambda e: e.tensor_tensor(b2, x2, sinb, ALU.mult), reads=rd, writes=[('ow', 1)])
        tk.op('pool', lambda e: e.tensor_tensor(d1[:, :, 0:32], b1, b2, ALU.subtract), reads=[('ow',)], writes=[k1 + (0,)])
        tk.op('pool', lambda e: e.tensor_tensor(b1, x2, cosb, ALU.mult), reads=rd, writes=[('ow', 0)])
        tk.op('pool', lambda e: e.tensor_tensor(b2, x1, sinb, ALU.mult), reads=rd, writes=[('ow', 1)])
        tk.op('pool', lambda e: e.tensor_tensor(d1[:, :, 32:64], b1, b2, ALU.add), reads=[('ow',)], writes=[k1 + (1,)])

    def attention(self, gp, nk=NT):
        tk = self.tk
        P = self.P
        ps, xn, resid = self.ps, self.xn, self.resid
        cv = self.carve
        Wq = cv(0, [128, 8, 512], BF16)
        Wo = cv(8192, [128, 4, 1024], BF16)
        stg = cv(16384, [128, 2, 512], F32)
        qz = cv(20480, [128, 2, 2, 512], BF16)
        qsz = cv(24576, [128, 2, 2, 16], BF16)
        mbzs = cv(24832, [128, 4, 16], BF16)
        Es4 = cv(25088, [128, 2, 64], BF16)
        rmask = self.rmask
        KT = cv(32768, [128, 8192], BF16)
        V = cv(49152, [128, 64, 2, 65], BF16)
        kwT = cv(65792, [128, 8, 128], BF16)
        vw = cv(67840, [128, 8, 2, 65], BF16)
        qf = cv(69920, [128, 2, 512], F32)
        qT = cv(74016, [128, 2, 4, 128], BF16)
        cE = cv(76064, [128, 4, 128], F32)
        mk = cv(78112, [128, 4, 128], F32)
        mbz = cv(69920, [128, 4, 512], BF16)
        E = cv(25344, [128, 3, 512], BF16)
        PT = cv(83232, [128, 4, 128], BF16)
        oacc = cv(84256, [128, 512], F32)
        ow = cv(86304, [128, 2, 256], F32)
        oT = cv(88352, [128, 4, 128], BF16)
        sm = cv(89376, [128, 32], F32)
        osw = cv(28672, [128, 2, 260], F32)
        g11 = self.gain(1, 1)
        nwq, nwo, kts, vss, ktw, vws = [P.ins[k] for k in ('nwq', 'nwo', 'kts', 'vss', 'ktw', 'vws')]
        arow, tabs, dm, wm, selc, gat, kcT, vcb = self.arow, self.tabs, self.dm, self.wm, self.selc, self.gat, self.kcT, self.vcb
        for h in range(8):
            q = self.ns()
            tk.op('sp', lambda e: e.dma_start(out=stg[:, q, :], in_=nwq[gp, :, h, :]), writes=[('stg', q)], dma=True)
            tk.op('dve', lambda e: e.tensor_scalar(Wq[:, h, :], stg[:, q, :], g11[:, h:h + 1], None, ALU.mult),
                  reads=[('stg', q), ('ngs',)], writes=[('Wq', h)])
            q = self.ns()
            tk.op('sp', lambda e: e.dma_start(out=stg[:, q, :], in_=nwo[gp, :, h // 2, (h % 2) * 512:(h % 2 + 1) * 512]), writes=[('stg', q)], dma=True)
            tk.op('pool', lambda e: e.tensor_copy(Wo[:, h // 2, (h % 2) * 512:(h % 2 + 1) * 512], stg[:, q, :]), reads=[('stg', q)], writes=[('Wo', h)])
        for c8 in range(16):
            q = self.ns()
            tk.op('sp', lambda e: e.dma_start(out=stg[:, q, :], in_=kts[gp, :, c8 * 512:(c8 + 1) * 512]), writes=[('stg', q)], dma=True)
            tk.op('dve', lambda e: e.tensor_copy(KT[:, c8 * 512:(c8 + 1) * 512], stg[:, q, :]), reads=[('stg', q)], writes=[('KT', c8)])
            q = self.ns()
            tk.op('sp', lambda e: e.dma_start(out=stg[:, q, :].rearrange("p (t c) -> p t c", c=128), in_=vss[:, c8 * 4:(c8 + 1) * 4, gp * 128:(gp + 1) * 128]),
                  writes=[('stg', q)], dma=True)
            tk.op('pool', lambda e: e.tensor_copy(V[:, c8 * 4:(c8 + 1) * 4, :, 0:64], stg[:, q, :].rearrange("p (t g d) -> p t g d", g=2, d=64)),
                  reads=[('stg', q)], writes=[('V', c8)])
        tk.op('sp', lambda e: e.dma_start(out=kcT[:, :, :].rearrange("p a b -> p (a b)"), in_=self.d_kc[0, :, :]), reads=[('d_kc', 0)], writes=[('kcT',)], dma=True)
        tk.op('sp', lambda e: e.dma_start(out=vcb[:, :], in_=self.d_vc[0, :, :]), reads=[('d_vc', 0)], writes=[('vcb',)], dma=True)
        tk.op('pool', lambda e: e.memset(V[:, :, :, 64:65], 1.0), writes=[('V1',)])
        tk.op('pool', lambda e: e.memset(vw[:, :, :, 64:65], 1.0), writes=[('vw1',)])
        for k in range(nk):
            c0 = k * TW + 2
            ci = k // 3
            m0 = 0 if k > 0 else 4
            j0 = 4 * k - 4
            for mh in range(m0 // 4, 2):
                ma = 4 * mh
                q = self.ns()
                tk.op('sp', lambda e: e.dma_start(out=stg[:, q, :], in_=ktw[gp, :, (j0 + ma) * 128:(j0 + ma + 4) * 128]), writes=[('stg', q)], dma=True)
                tk.op('dve', lambda e: e.tensor_copy(kwT[:, ma:ma + 4, :].rearrange("p a b -> p (a b)"), stg[:, q, :]), reads=[('stg', q)], writes=[('kwT', mh)])
                q = self.ns()
                tk.op('sp', lambda e: e.dma_start(out=stg[:, q, :].rearrange("p (t c) -> p t c", c=128),
                                                  in_=vws[:, j0 + ma:j0 + ma + 4, gp * 128:(gp + 1) * 128]), writes=[('stg', q)], dma=True)
                tk.op('pool', lambda e: e.tensor_copy(vw[:, ma:ma + 4, :, 0:64], stg[:, q, :].rearrange("p (t g d) -> p t g d", g=2, d=64)),
                      reads=[('stg', q)], writes=[('vw', mh)])
            for c in range(KC):
                tk.op('pe', lambda e: e.matmul(ps[0][:, 0:512], xn[:, c, c0:c0 + 128], Wq[:, c, :], start=(c == 0), stop=(c == KC - 1)),
                      reads=[('xn', ci), ('Wq',)], writes=[('ps', 0)])
            self.qk_norm_rope(ps[0][:, 0:512], 128, 8, 0, k, qf[:, 0, :], qf[:, 1, :], qf[:, 1, :], sm, ow[:, 0, :], ow[:, 1, :], 0.125,
                              ('qf', 0), ('qf', 1), ('ps', 0))
            for v in range(2):
                bkv = 0 if v == 0 else 2
                for t in range(4):
                    tk.op('pe', lambda e: e.transpose(ps[bkv][:, t * 128:(t + 1) * 128], qf[:, v, t * 128:(t + 1) * 128], self.ident[:, :]),
                          reads=[('qf', v), ('ident',)], writes=[('ps', bkv)])
                tk.op('act', lambda e: e.activation(qT[:, v, :, :].rearrange("p a b -> p (a b)"), ps[bkv][:, 0:512], AF.Copy),
                      reads=[('ps', bkv)], writes=[('qT', v)])
            for v in range(2):
                for gg in range(2):
                    tk.op('pool', lambda e: e.tensor_scalar(qz[:, v, gg, :], qT[:, v, :, :].rearrange("p a b -> p (a b)"), rmask[:, gg:gg + 1], None, ALU.mult),
                          reads=[('qT', v), ('rmask',)], writes=[('qz', v, gg)])
            for gg in range(2):
                g = 2 * gp + gg
                h0, h1 = 64 * gg, 64 * gg + 64
                for j in range(4):
                    tk.op('pe', lambda e: e.matmul(ps[1][:, j * 128:(j + 1) * 128], qz[:, 0, gg, j * 128:(j + 1) * 128], kcT[:, gp, :], start=True, stop=True),
                          reads=[('qz', 0, gg), ('kcT',)], writes=[('ps', 1)])
                cEf = cE[:, :, :].rearrange("p a b -> p (a b)")
                tk.op('act', lambda e: e.activation(cEf, ps[1][:, 0:512], AF.Exp), reads=[('ps', 1)], writes=[('cE',)])
                tk.op('dve', lambda e: e.tensor_scalar(mk[:, 0, :], arow[:, 0, :], tabs[:, 0, k:k + 1], None, ALU.is_le),
                      reads=[('arow',), ('tabs',)], writes=[('mk', 0)])
                tk.op('dve', lambda e: e.tensor_tensor(cE[:, :, :], cE[:, :, :], bc(mk[:, 0, :], 1, 4), ALU.mult), reads=[('cE',), ('mk', 0)], writes=[('cE',)])
                tk.op('dve', lambda e: e.tensor_reduce(sm[:, 16:20], cE[:, :, :], AX.X, ALU.add), reads=[('cE',)], writes=[('sm', 2)])
                tk.op('dve', lambda e: e.tensor_scalar(sm[:, 16:20], sm[:, 16:20], 1e-30, None, ALU.max), reads=[('sm', 2)], writes=[('sm', 2)])
                tk.op('dve', lambda e: e.reciprocal(sm[:, 20:24], sm[:, 16:20]), reads=[('sm', 2)], writes=[('sm', 3)])
                tk.op('dve', lambda e: e.tensor_tensor(cE[:, :, :], cE[:, :, :], bc(sm[:, 20:24], 2, 128), ALU.mult), reads=[('cE',), ('sm', 3)], writes=[('cE',)])
                tk.op('dve', lambda e: e.tensor_reduce(mk[:, 3, :], cE[:, :, :].rearrange("p j n -> p n j"), AX.X, ALU.add), reads=[('cE',)], writes=[('mk', 3)])
                for j in range(4):
                    tk.op('pe', lambda e: e.transpose(ps[0][:, j * 128:(j + 1) * 128], cE[:, j, :], self.ident[:, :]),
                          reads=[('cE',), ('ident',)], writes=[('ps', 0)])
                tk.op('act', lambda e: e.activation(PT[:, :, :].rearrange("p a b -> p (a b)"), ps[0][:, 0:512], AF.Copy), reads=[('ps', 0)], writes=[('PT',)])
                for j in range(4):
                    tk.op('pe', lambda e: e.matmul(ps[3][:, j * 64:(j + 1) * 64], PT[:, j, :], vcb[:, g * 64:(g + 1) * 64], start=True, stop=True),
                          reads=[('PT',), ('vcb',)], writes=[('ps', 3)])
                tk.op('dve', lambda e: e.tensor_scalar(mk[:, 0, :], arow[:, 1, :], tabs[:, 1, k:k + 1], None, ALU.is_le), reads=[('arow',), ('tabs',)], writes=[('mk', 0)])
                tk.op('dve', lambda e: e.tensor_scalar(mk[:, 1, :], arow[:, 1, :], tabs[:, 2, k:k + 1], None, ALU.is_ge), reads=[('arow',), ('tabs',)], writes=[('mk', 1)])
                tk.op('dve', lambda e: e.tensor_tensor(mk[:, 1, :], mk[:, 1, :], arow[:, 2, :], ALU.max), reads=[('mk', 1), ('arow',)], writes=[('mk', 1)])
                tk.op('dve', lambda e: e.scalar_tensor_tensor(mk[:, 1, :], mk[:, 1, :], 1e4, mk[:, 3, :], ALU.mult, ALU.add), reads=[('mk', 1), ('mk', 3)], writes=[('mk', 1)])
                tk.op('dve', lambda e: e.scalar_tensor_tensor(mk[:, 1, :], mk[:, 1, :], 1.0, mk[:, 0, :], ALU.add, ALU.mult), reads=[('mk', 1), ('mk', 0)], writes=[('mk', 1)])
                tk.op('dve', lambda e: e.max(sm[:, 0:8], mk[:, 1, :]), reads=[('mk', 1)], writes=[('sm', 0)])
                tk.op('dve', lambda e: e.match_replace(mk[:, 2, :], sm[:, 0:8], mk[:, 1, :], -1.0), reads=[('mk', 1), ('sm', 0)], writes=[('mk', 2)])
                tk.op('dve', lambda e: e.max(sm[:, 8:16], mk[:, 2, :]), reads=[('mk', 2)], writes=[('sm', 1)])
                tk.op('dve', lambda e: e.scalar_tensor_tensor(mk[:, 2, :], mk[:, 1, :], sm[:, 15:16], mk[:, 0, :], ALU.is_ge, ALU.mult),
                      reads=[('mk', 1), ('sm', 1), ('mk', 0)], writes=[('mk', 2)])
                tk.op('dve', lambda e: e.tensor_scalar(mk[:, 2, :], mk[:, 2, :], -1.0, 30000.0, ALU.add, ALU.mult), reads=[('mk', 2)], writes=[('mk', 2)])
                tk.op('pe', lambda e: e.transpose(ps[2][:, 0:128], mk[:, 2, :], self.ident[:, :]), reads=[('mk', 2), ('ident',)], writes=[('ps', 2)])
                for a4 in range(4):
                    tk.op('act', lambda e: e.activation(mbz[:, a4, :].rearrange("p (a b) -> p a b", b=128), bc(ps[2][:, 0:128], 1, 4), AF.Copy,
                                                        scale=rmask[:, 2 + a4:3 + a4]),
                          reads=[('ps', 2), ('rmask',), ('qf',)], writes=[('qf', 9, a4)])
                qr = qz[:, 1, gg, :]
                ntile = 4 * k + 4

                def sel_qk(j):
                    b = (6, 7, 0, 1)[j % 4]
                    es = j % 3
                    a_, kk = j // 16, j % 16
                    tk.op('pe', lambda e: e.matmul(ps[b][:, 0:512], KT[:, j * 128:(j + 1) * 128], qr, start=True, stop=False),
                          reads=[('KT',), ('qz', 1, gg)], writes=[('ps', b)])
                    tk.op('pe', lambda e: e.matmul(ps[b][:, 0:512], selc[:, kk, :], mbz[:, a_, :], start=False, stop=True),
                          reads=[('selc',), ('qf', 9, a_)], writes=[('ps', b)])
                    tk.op('act', lambda e: e.activation(E[:, es, :], ps[b][:, 0:512], AF.Exp), reads=[('ps', b)], writes=[('E', es)])
                    if j >= 4 * k:
                        tk.op('pool', lambda e: e.tensor_tensor(E[:, es, :].rearrange("p (a b) -> p a b", b=128), E[:, es, :].rearrange("p (a b) -> p a b", b=128),
                                                                bc(dm[:, j - 4 * k, :], 1, 4), ALU.mult),
                              reads=[('E', es), ('dm',)], writes=[('E', es)])

                def sel_pv(j):
                    es = j % 3
                    bpv = (4, 5, 2)[j % 3]
                    for jj in range(4):
                        tk.op('pe', lambda e: e.matmul(ps[bpv][:, jj * 65:(jj + 1) * 65], E[:, es, jj * 128:(jj + 1) * 128], V[:, j, gg, :],
                                                       start=True, stop=True),
                              reads=[('E', es), ('V',), ('V1',)], writes=[('ps', bpv)])
                    if j == 0:
                        tk.op('dve', lambda e: e.tensor_copy(osw[:, 0, :], ps[bpv][:, 0:260]), reads=[('ps', bpv)], writes=[('osw', 0)])
                    else:
                        tk.op('dve', lambda e: e.tensor_tensor(osw[:, 0, :], osw[:, 0, :], ps[bpv][:, 0:260], ALU.add), reads=[('ps', bpv), ('osw', 0)], writes=[('osw', 0)])
                LA = 2
                for j in range(min(LA, ntile)):
                    sel_qk(j)
                for j in range(ntile):
                    if j + LA < ntile:
                        sel_qk(j + LA)
                    sel_pv(j)
                def win_qk(m):
                    b = (6, 7, 0, 1)[m % 4]
                    es = m % 3
                    tk.op('pe', lambda e: e.matmul(ps[b][:, 0:512], kwT[:, m, :], qr, start=True, stop=True),
                          reads=[('kwT',), ('qz', 1, gg)], writes=[('ps', b)])
                    tk.op('act', lambda e: e.activation(E[:, es, :], ps[b][:, 0:512], AF.Exp), reads=[('ps', b)], writes=[('E', es)])
                    tk.op('pool', lambda e: e.tensor_tensor(E[:, es, :].rearrange("p (a b) -> p a b", b=128), E[:, es, :].rearrange("p (a b) -> p a b", b=128),
                                                            bc(wm[:, m, :], 1, 4), ALU.mult),
                          reads=[('E', es), ('wm',)], writes=[('E', es)])

                def win_pv(m):
                    es = m % 3
                    bpv = (4, 5, 2)[m % 3]
                    for jj in range(4):
                        tk.op('pe', lambda e: e.matmul(ps[bpv][:, jj * 65:(jj + 1) * 65], E[:, es, jj * 128:(jj + 1) * 128], vw[:, m, gg, :],
                                                       start=True, stop=True),
                              reads=[('E', es), ('vw',), ('vw1',)], writes=[('ps', bpv)])
                    if m == m0:
                        tk.op('dve', lambda e: e.tensor_copy(osw[:, 1, :], ps[bpv][:, 0:260]), reads=[('ps', bpv)], writes=[('osw', 1)])
                    else:
                        tk.op('dve', lambda e: e.tensor_tensor(osw[:, 1, :], osw[:, 1, :], ps[bpv][:, 0:260], ALU.add), reads=[('ps', bpv), ('osw', 1)], writes=[('osw', 1)])
                for m in range(m0, min(m0 + LA, 8)):
                    win_qk(m)
                for m in range(m0, 8):
                    if m + LA < 8:
                        win_qk(m + LA)
                    win_pv(m)
                if 'd_o' in P.outs and k == 0 and gg == 0 and gp == 0:
                    dbgt = cv(0, [128, 1024], F32)
                    tk.op('act', lambda e: e.activation(dbgt[:, 0:260], ps[3][:, 0:260], AF.Copy), reads=[('ps', 3)], writes=[('dbgt', 0)])
                    for bi in (1, 2):
                        tk.op('act', lambda e: e.activation(dbgt[:, bi * 260:(bi + 1) * 260], osw[:, bi - 1, :], AF.Copy), reads=[('osw', bi - 1)], writes=[('dbgt', bi)])
                    tk.op('act', lambda e: e.activation(dbgt[:, 780:908], mk[:, 2, :], AF.Copy), reads=[('mk', 2)], writes=[('dbgt', 3)])
                    tk.op('act', lambda e: e.activation(dbgt[:, 908:1024], mk[:, 3, 0:116], AF.Copy), reads=[('mk', 3)], writes=[('dbgt', 4)])
                    tk.op('pool', lambda e: e.dma_start(out=P.outs['d_o'][:, :], in_=dbgt[:, :]), reads=[('dbgt',)], writes=[('o_do',)], dma=True)
                gv = gat[:, k, g * 12:(g + 1) * 12].rearrange("p (j b) -> p j b", b=3)
                for (bk_, o_) in [(4, 24), (5, 28)]:
                    p3 = osw[:, bk_ - 4, :].rearrange("p (j d) -> p j d", d=65)
                    tk.op('dve', lambda e: e.tensor_scalar(sm[:, o_:o_ + 4], p3[:, :, 64], 1e-30, None, ALU.max), reads=[('osw', bk_ - 4)], writes=[('sm', o_)])
                    tk.op('dve', lambda e: e.reciprocal(sm[:, o_:o_ + 4], sm[:, o_:o_ + 4]), reads=[('sm', o_)], writes=[('sm', o_)])
                    tk.op('dve', lambda e: e.tensor_tensor(sm[:, o_:o_ + 4], sm[:, o_:o_ + 4], gv[:, :, 1 if bk_ == 4 else 2], ALU.mult),
                          reads=[('sm', o_), ('gat',)], writes=[('sm', o_)])
                oa = oacc[:, gg * 256:(gg + 1) * 256].rearrange("p (j d) -> p j d", d=64)
                tk.op('dve', lambda e: e.tensor_tensor(oa, ps[3][:, 0:256].rearrange("p (j d) -> p j d", d=64), bc(gv[:, :, 0], 2, 64), ALU.mult),
                      reads=[('ps', 3), ('gat',)], writes=[('oacc', gg)])
                for (bk_, o_, wi) in [(4, 24, 0), (5, 28, 1)]:
                    p3 = osw[:, wi, :].rearrange("p (j d) -> p j d", d=65)
                    w3 = ow[:, wi, :].rearrange("p (j d) -> p j d", d=64)
                    tk.op('dve', lambda e: e.tensor_tensor(w3, p3[:, :, 0:64], bc(sm[:, o_:o_ + 4], 2, 64), ALU.mult),
                          reads=[('osw', wi), ('sm', o_)], writes=[('ow', wi)])
                    tk.op('dve', lambda e: e.tensor_tensor(oa, oa, w3, ALU.add), reads=[('oacc', gg), ('ow', wi)], writes=[('oacc', gg)])
            for t in range(4):
                tk.op('pe', lambda e: e.transpose(ps[0][:, t * 128:(t + 1) * 128], oacc[:, t * 128:(t + 1) * 128], self.ident[:, :]),
                      reads=[('oacc',), ('ident',)], writes=[('ps', 0)])
            tk.op('act', lambda e: e.activation(oT[:, :, :].rearrange("p a b -> p (a b)"), ps[0][:, 0:512], AF.Copy), reads=[('ps', 0)], writes=[('oT',)])
            for half in range(2):
                bko = 1 if half == 0 else 2
                for mm in range(4):
                    m = half * 4 + mm
                    for t in range(4):
                        tk.op('pe', lambda e: e.matmul(ps[bko][:, mm * 128:(mm + 1) * 128], Wo[:, t, m * 128:(m + 1) * 128], oT[:, t, :],
                                                       start=(t == 0), stop=(t == 3)),
                              reads=[('Wo',), ('oT',)], writes=[('ps', bko)])
                tk.op('dve', lambda e: e.tensor_tensor(resid[:, 4 * half:4 * half + 4, c0:c0 + 128], resid[:, 4 * half:4 * half + 4, c0:c0 + 128],
                                                       ps[bko][:, 0:512].rearrange("p (a b) -> p a b", b=128), ALU.add),
                      reads=[('ps', bko), ('resid', ci)], writes=[('resid', ci)])


        if not self.do_sample:
            return
        tk.barrier()
        R = NSEQ * SW
        c0 = PCOL
        sg = stg[:, :, :].rearrange("p a (b c) -> p (a b) c", c=256)
        sgn = [0]

        def nsg():
            q = sgn[0]
            sgn[0] = (q + 1) % 4
            return q
        qs = cv(30752, [128, 2, 16], BF16)
        Es = cv(30816, [128, 2, 16], BF16)
        PTs = cv(30880, [128, 16], BF16)
        mbTs = cv(30912, [128, 16], BF16)
        oTs = cv(30944, [128, 4, 16], BF16)
        vnb = cv(31072, [128, 2, 2, 65], BF16)
        ktnb = cv(31592, [128, 2, 24], BF16)
        gts = cv(31688, [128, 4, 48], F32)
        nm, wm0, idx = self.nm, self.wm0, self.idx
        ktn, vn, ckwT, cvwt, gats = [P.ins[k_] for k_ in ('ktn', 'vn', 'ckwT', 'cvwt', 'gats')]
        pool_ks, pool_vs = P.ins['pool_ks'], P.ins['pool_vs']
        tk.op('sp', lambda e: e.dma_start(out=gts[0:4, :, :], in_=gats[:, :, :]), writes=[('gts',)], dma=True)
        tk.op('pool', lambda e: e.memset(vnb[:, :, :, 64:65], 1.0), writes=[('vnb1',)])
        for kind in range(2):
            q = nsg()
            tk.op('sp', lambda e: e.dma_start(out=sg[:, q, 0:24], in_=ktn[kind, gp, :, :]), writes=[('sg', q)], dma=True)
            tk.op('dve', lambda e: e.tensor_copy(ktnb[:, kind, :], sg[:, q, 0:24]), reads=[('sg', q)], writes=[('ktnb', kind)])
        for c in range(KC):
            tk.op('pe', lambda e: e.matmul(ps[0][0:R, 0:512], xn[:, c, c0:c0 + R], Wq[:, c, :], start=(c == 0), stop=(c == KC - 1)),
                  reads=[('xn',), ('Wq',)], writes=[('ps', 0)])
        self.qk_norm_rope(ps[0][0:R, 0:512], R, 8, 0, NT, qf[0:R, 0, :], qf[0:R, 1, :], qf[0:R, 1, :], sm, ow[0:R, 0, :], ow[0:R, 1, :], 0.125,
                          ('qf', 0), ('qf', 1), ('ps', 0))
        for v in range(2):
            bkv = 0 if v == 0 else 2
            for t in range(4):
                tk.op('pe', lambda e: e.transpose(ps[bkv][:, t * 32:t * 32 + R], qf[0:R, v, t * 128:(t + 1) * 128], self.ident[0:R, 0:R]),
                      reads=[('qf', v), ('ident',)], writes=[('ps', bkv)])
            tk.op('act', lambda e: e.activation(qT[:, v, :, 0:R], ps[bkv][:, 0:128].rearrange("p (a b) -> p a b", b=32)[:, :, 0:R], AF.Copy),
                  reads=[('ps', bkv)], writes=[('qT', v)])
        for sq_ in range(NSEQ):
            r0 = sq_ * SW + 2
            tk.op('sp', lambda e: e.dma_start(out=kcT[:, :, :].rearrange("p a b -> p (a b)"), in_=self.d_kc[1 + sq_, :, :]),
                  reads=[('d_kc', 1 + sq_)], writes=[('kcT',)], dma=True)
            tk.op('sp', lambda e: e.dma_start(out=vcb[:, :], in_=self.d_vc[1 + sq_, :, :]), reads=[('d_vc', 1 + sq_)], writes=[('vcb',)], dma=True)
            q = nsg()
            tk.op('sp', lambda e: e.dma_start(out=sg[0:4, q, :].rearrange("p (k c) -> p k c", c=128), in_=vn[:, sq_, :, gp * 128:(gp + 1) * 128]),
                  writes=[('sg', q)], dma=True)
            tk.op('dve', lambda e: e.tensor_copy(vnb[0:4, :, :, 0:64], sg[0:4, q, :].rearrange("p (k g d) -> p k g d", g=2, d=64)),
                  reads=[('sg', q)], writes=[('vnb',)])
            for page in range(64):
                col = sq_ * 64 + page
                q = nsg()
                tk.op('pool', lambda e: e.indirect_dma_start(out=sg[:, q, :], out_offset=None, in_=pool_ks[:, :],
                                                             in_offset=bass.IndirectOffsetOnAxis(ap=idx[:, col:col + 1], axis=0)),
                      reads=[('idx',)], writes=[('sg', q)], dma=True)
                b = 6 + (page % 2)
                tk.op('pe', lambda e: e.transpose(ps[b][:, 0:128], sg[:, q, gp * 128:(gp + 1) * 128], self.ident[:, :]),
                      reads=[('sg', q), ('ident',)], writes=[('ps', b)])
                tk.op('act', lambda e: e.activation(KT[:, page * 128:(page + 1) * 128], ps[b][:, 0:128], AF.Copy), reads=[('ps', b)], writes=[('KT', page)])
                q = nsg()
                tk.op('pool', lambda e: e.indirect_dma_start(out=sg[:, q, :], out_offset=None, in_=pool_vs[:, :],
                                                             in_offset=bass.IndirectOffsetOnAxis(ap=idx[:, col:col + 1], axis=0)),
                      reads=[('idx',)], writes=[('sg', q)], dma=True)
                tk.op('dve', lambda e: e.tensor_copy(V[:, page, :, 0:64], sg[:, q, gp * 128:(gp + 1) * 128].rearrange("p (g d) -> p g d", d=64)),
                      reads=[('sg', q)], writes=[('V', page)])
            for h in range(2):
                q = nsg()
                tk.op('sp', lambda e: e.dma_start(out=sg[:, q, :], in_=ckwT[gp, :, sq_ * 512 + h * 256:sq_ * 512 + (h + 1) * 256]), writes=[('sg', q)], dma=True)
                tk.op('dve', lambda e: e.tensor_copy(kwT[:, 2 * h:2 * h + 2, :].rearrange("p a b -> p (a b)"), sg[:, q, :]), reads=[('sg', q)], writes=[('kwT', h)])
                q = nsg()
                tk.op('sp', lambda e: e.dma_start(out=sg[:, q, :].rearrange("p (t c) -> p t c", c=128),
                                                  in_=cvwt[:, sq_ * 4 + 2 * h:sq_ * 4 + 2 * h + 2, gp * 128:(gp + 1) * 128]), writes=[('sg', q)], dma=True)
                tk.op('dve', lambda e: e.tensor_copy(vw[:, 2 * h:2 * h + 2, :, 0:64], sg[:, q, :].rearrange("p (t g d) -> p t g d", g=2, d=64)),
                      reads=[('sg', q)], writes=[('vw', h)])
            for v in range(2):
                tk.op('act', lambda e: e.activation(qs[:, v, :].rearrange("p (a b) -> p a b", b=4), qT[:, v, :, r0:r0 + 4], AF.Copy),
                      reads=[('qT', v)], writes=[('qs', v)])
            for v in range(2):
                for gg in range(2):
                    tk.op('pool', lambda e: e.tensor_scalar(qsz[:, v, gg, :], qs[:, v, :], rmask[:, gg:gg + 1], None, ALU.mult),
                          reads=[('qs', v), ('rmask',)], writes=[('qsz', v, gg)])
            for gg in range(2):
                g = 2 * gp + gg
                h0, h1 = 64 * gg, 64 * gg + 64
                for j in range(4):
                    tk.op('pe', lambda e: e.matmul(ps[1][0:4, j * 128:(j + 1) * 128], qsz[:, 0, gg, j * 4:(j + 1) * 4], kcT[:, gp, :], start=True, stop=True),
                          reads=[('qsz', 0, gg), ('kcT',)], writes=[('ps', 1)])
                tk.op('act', lambda e: e.activation(cE[0:4, :, :].rearrange("p a b -> p (a b)"), ps[1][0:4, 0:512], AF.Exp), reads=[('ps', 1)], writes=[('cE',)])
                tk.op('dve', lambda e: e.tensor_reduce(sm[0:4, 16:20], cE[0:4, :, :], AX.X, ALU.add), reads=[('cE',)], writes=[('sm', 2)])
                tk.op('dve', lambda e: e.tensor_scalar(sm[0:4, 16:20], sm[0:4, 16:20], 1e-30, None, ALU.max), reads=[('sm', 2)], writes=[('sm', 2)])
                tk.op('dve', lambda e: e.reciprocal(sm[0:4, 20:24], sm[0:4, 16:20]), reads=[('sm', 2)], writes=[('sm', 3)])
                tk.op('dve', lambda e: e.tensor_tensor(cE[0:4, :, :], cE[0:4, :, :], bc(sm[0:4, 20:24], 2, 128), ALU.mult), reads=[('cE',), ('sm', 3)], writes=[('cE',)])
                tk.op('dve', lambda e: e.tensor_reduce(mk[0:4, 3, :], cE[0:4, :, :].rearrange("p j n -> p n j"), AX.X, ALU.add), reads=[('cE',)], writes=[('mk', 3)])
                for j in range(4):
                    tk.op('pe', lambda e: e.transpose(ps[0][:, j * 4:(j + 1) * 4], cE[0:4, j, :], self.ident[0:4, 0:4]),
                          reads=[('cE',), ('ident',)], writes=[('ps', 0)])
                tk.op('act', lambda e: e.activation(PTs[:, :], ps[0][:, 0:16], AF.Copy), reads=[('ps', 0)], writes=[('PTs',)])
                for j in range(4):
                    tk.op('pe', lambda e: e.matmul(ps[3][0:4, j * 64:(j + 1) * 64], PTs[:, j * 4:(j + 1) * 4], vcb[:, g * 64:(g + 1) * 64], start=True, stop=True),
                          reads=[('PTs',), ('vcb',)], writes=[('ps', 3)])
                kq = NT
                tk.op('dve', lambda e: e.tensor_scalar(mk[0:4, 1, :], arow[0:4, 1, :], tabs[0:4, 2, kq:kq + 1], None, ALU.is_ge), reads=[('arow',), ('tabs',)], writes=[('mk', 1)])
                tk.op('dve', lambda e: e.tensor_tensor(mk[0:4, 1, :], mk[0:4, 1, :], arow[0:4, 2, :], ALU.max), reads=[('mk', 1), ('arow',)], writes=[('mk', 1)])
                tk.op('dve', lambda e: e.scalar_tensor_tensor(mk[0:4, 1, :], mk[0:4, 1, :], 1e4, mk[0:4, 3, :], ALU.mult, ALU.add), reads=[('mk', 1), ('mk', 3)], writes=[('mk', 1)])
                tk.op('dve', lambda e: e.max(sm[0:4, 0:8], mk[0:4, 1, :]), reads=[('mk', 1)], writes=[('sm', 0)])
                tk.op('dve', lambda e: e.match_replace(mk[0:4, 2, :], sm[0:4, 0:8], mk[0:4, 1, :], -1.0), reads=[('mk', 1), ('sm', 0)], writes=[('mk', 2)])
                tk.op('dve', lambda e: e.max(sm[0:4, 8:16], mk[0:4, 2, :]), reads=[('mk', 2)], writes=[('sm', 1)])
                tk.op('dve', lambda e: e.tensor_scalar(mk[0:4, 2, :], mk[0:4, 1, :], sm[0:4, 14:15], None, ALU.is_ge), reads=[('mk', 1), ('sm', 1)], writes=[('mk', 2)])
                tk.op('dve', lambda e: e.tensor_scalar(mk[0:4, 2, :], mk[0:4, 2, :], -1.0, 30000.0, ALU.add, ALU.mult), reads=[('mk', 2)], writes=[('mk', 2)])
                tk.op('pe', lambda e: e.transpose(ps[2][:, 0:4], mk[0:4, 2, :], self.ident[0:4, 0:4]), reads=[('mk', 2), ('ident',)], writes=[('ps', 2)])
                for a4 in range(4):
                    tk.op('act', lambda e: e.activation(mbzs[:, a4, :].rearrange("p (a b) -> p a b", b=4), bc(ps[2][:, 0:4], 1, 4), AF.Copy,
                                                        scale=rmask[:, 2 + a4:3 + a4]),
                          reads=[('ps', 2), ('rmask',)], writes=[('mbzs', a4)])
                qr = qsz[:, 1, gg, :]
                for grp in range(16):
                    b = 6 + (grp % 2)
                    for p4 in range(4):
                        page = 4 * grp + p4
                        a_, kk = page // 16, page % 16
                        tk.op('pe', lambda e: e.matmul(ps[b][:, p4 * 16:(p4 + 1) * 16], KT[:, page * 128:(page + 1) * 128], qr, start=True, stop=False),
                              reads=[('KT',), ('qsz', 1, gg)], writes=[('ps', b)])
                        tk.op('pe', lambda e: e.matmul(ps[b][:, p4 * 16:(p4 + 1) * 16], selc[:, kk, :], mbzs[:, a_, :], start=False, stop=True),
                              reads=[('selc',), ('mbzs', a_)], writes=[('ps', b)])
                    tk.op('act', lambda e: e.activation(Es4[:, grp % 2, :], ps[b][:, 0:64], AF.Exp), reads=[('ps', b)], writes=[('Es4', grp % 2)])
                    bpv = 4 + (grp % 2)
                    for jj in range(4):
                        for p4 in range(4):
                            page = 4 * grp + p4
                            tk.op('pe', lambda e: e.matmul(ps[bpv][0:4, jj * 65:(jj + 1) * 65], Es4[:, grp % 2, p4 * 16 + jj * 4:p4 * 16 + (jj + 1) * 4], V[:, page, gg, :],
                                                           start=(p4 == 0), stop=(p4 == 3)),
                                  reads=[('Es4', grp % 2), ('V',), ('V1',)], writes=[('ps', bpv)])
                    if grp == 0:
                        tk.op('dve', lambda e: e.tensor_copy(osw[0:4, 0, :], ps[bpv][0:4, 0:260]), reads=[('ps', bpv)], writes=[('osw', 0)])
                    else:
                        tk.op('dve', lambda e: e.tensor_tensor(osw[0:4, 0, :], osw[0:4, 0, :], ps[bpv][0:4, 0:260], ALU.add), reads=[('ps', bpv), ('osw', 0)], writes=[('osw', 0)])
                b = 6
                for m in range(4):
                    tk.op('pe', lambda e: e.matmul(ps[b][:, m * 16:(m + 1) * 16], kwT[:, m, :], qr, start=True, stop=True),
                          reads=[('kwT',), ('qsz', 1, gg)], writes=[('ps', b)])
                tk.op('act', lambda e: e.activation(Es4[:, 0, :], ps[b][:, 0:64], AF.Exp), reads=[('ps', b)], writes=[('Es4', 0)])
                tk.op('pool', lambda e: e.tensor_tensor(Es4[:, 0, 0:16], Es4[:, 0, 0:16], wm0[:, :], ALU.mult), reads=[('Es4', 0), ('wm0',)], writes=[('Es4', 0)])
                bpv = 4
                for jj in range(4):
                    for m in range(4):
                        tk.op('pe', lambda e: e.matmul(ps[bpv][0:4, jj * 65:(jj + 1) * 65], Es4[:, 0, m * 16 + jj * 4:m * 16 + (jj + 1) * 4], vw[:, m, gg, :],
                                                       start=(m == 0), stop=(m == 3)),
                              reads=[('Es4', 0), ('vw',), ('vw1',)], writes=[('ps', bpv)])
                tk.op('dve', lambda e: e.tensor_copy(osw[0:4, 1, :], ps[bpv][0:4, 0:260]), reads=[('ps', bpv)], writes=[('osw', 1)])
                for kind in range(2):
                    b = 6 + kind
                    tk.op('pe', lambda e: e.matmul(ps[b][0:4, 0:16], ktnb[:, kind, r0:r0 + 4], qr, start=True, stop=True),
                          reads=[('ktnb',), ('qsz', 1, gg)], writes=[('ps', b)])
                    tk.op('act', lambda e: e.activation(Es[0:4, kind, :], ps[b][0:4, 0:16], AF.Exp), reads=[('ps', b)], writes=[('Es', kind)])
                    tk.op('pool', lambda e: e.tensor_tensor(Es[0:4, kind, :], Es[0:4, kind, :], nm[0:4, :], ALU.mult), reads=[('Es', kind), ('nm',)], writes=[('Es', kind)])
                    bpv = 4 + kind
                    for jj in range(4):
                        tk.op('pe', lambda e: e.matmul(ps[bpv][0:4, jj * 65:(jj + 1) * 65], Es[0:4, kind, jj * 4:(jj + 1) * 4], vnb[0:4, kind, gg, :], start=True, stop=True),
                              reads=[('Es', kind), ('vnb',), ('vnb1',)], writes=[('ps', bpv)], pg=(0, 4))
                    tk.op('dve', lambda e: e.tensor_tensor(osw[0:4, kind, :], osw[0:4, kind, :], ps[bpv][0:4, 0:260], ALU.add), reads=[('ps', bpv), ('osw', kind)], writes=[('osw', kind)])
                gv = gts[0:4, sq_, g * 12:(g + 1) * 12].rearrange("p (j b) -> p j b", b=3)
                for (wi, o_) in [(0, 24), (1, 28)]:
                    p3 = osw[0:4, wi, :].rearrange("p (j d) -> p j d", d=65)
                    tk.op('dve', lambda e: e.tensor_scalar(sm[0:4, o_:o_ + 4], p3[:, :, 64], 1e-30, None, ALU.max), reads=[('osw', wi)], writes=[('sm', o_)])
                    tk.op('dve', lambda e: e.reciprocal(sm[0:4, o_:o_ + 4], sm[0:4, o_:o_ + 4]), reads=[('sm', o_)], writes=[('sm', o_)])
                    tk.op('dve', lambda e: e.tensor_tensor(sm[0:4, o_:o_ + 4], sm[0:4, o_:o_ + 4], gv[:, :, 1 + wi], ALU.mult), reads=[('sm', o_), ('gts',)], writes=[('sm', o_)])
                oa = oacc[0:4, gg * 256:(gg + 1) * 256].rearrange("p (j d) -> p j d", d=64)
                tk.op('dve', lambda e: e.tensor_tensor(oa, ps[3][0:4, 0:256].rearrange("p (j d) -> p j d", d=64), bc(gv[:, :, 0], 2, 64), ALU.mult),
                      reads=[('ps', 3), ('gts',)], writes=[('oacc', gg)])
                for (wi, o_) in [(0, 24), (1, 28)]:
                    p3 = osw[0:4, wi, :].rearrange("p (j d) -> p j d", d=65)
                    w3 = ow[0:4, wi, :].rearrange("p (j d) -> p j d", d=64)
                    tk.op('dve', lambda e: e.tensor_tensor(w3, p3[:, :, 0:64], bc(sm[0:4, o_:o_ + 4], 2, 64), ALU.mult), reads=[('osw', wi), ('sm', o_)], writes=[('ow', wi)])
                    tk.op('dve', lambda e: e.tensor_tensor(oa, oa, w3, ALU.add), reads=[('oacc', gg), ('ow', wi)], writes=[('oacc', gg)])
            for t in range(4):
                tk.op('pe', lambda e: e.transpose(ps[0][:, 32 + t * 4:32 + (t + 1) * 4], oacc[0:4, t * 128:(t + 1) * 128], self.ident[0:4, 0:4]),
                      reads=[('oacc',), ('ident',)], writes=[('ps', 0)])
            tk.op('act', lambda e: e.activation(oTs[:, :, sq_ * 4:(sq_ + 1) * 4], ps[0][:, 32:48].rearrange("p (a b) -> p a b", b=4), AF.Copy),
                  reads=[('ps', 0)], writes=[('oTs', sq_)])
        for half in range(2):
            bko = 1 if half == 0 else 2
            for mm in range(4):
                m = half * 4 + mm
                for t in range(4):
                    tk.op('pe', lambda e: e.matmul(ps[bko][:, mm * 16:(mm + 1) * 16], Wo[:, t, m * 128:(m + 1) * 128], oTs[:, t, :], start=(t == 0), stop=(t == 3)),
                          reads=[('Wo',), ('oTs',)], writes=[('ps', bko)])
            rv = resid[:, 4 * half:4 * half + 4, PCOL:NCOL].rearrange("p m (s w) -> p m s w", w=SW)[:, :, :, 2:6]
            tk.op('dve', lambda e: e.tensor_tensor(rv, rv, ps[bko][:, 0:64].rearrange("p (m s q) -> p m s q", s=4, q=4), ALU.add),
                  reads=[('ps', bko), ('resid', 5)], writes=[('resid', 5)])


def build(stage=99):
    from contextlib import ExitStack
    P = Prog()
    nc = P.nc
    xT = P.din('xT', [128, KC, NCOL])
    P.din('pT', [2, 128, 2, NCOL])
    ng = P.din('ng', [128, 2 * 4 * KC])
    P.din('wgu', [2, 2, 22, 128, KC * 256])
    P.din('wdn', [2, 2, 22, 128, 1024])
    P.din('wpg', [2, 8, 128, 1280])
    P.din('cwin', [24, 128, KC * 128])
    cw = P.din('cw', [128, 3 * KC])
    P.din('cwout', [8, 128, KC * 128])
    hm = P.din('hm', [128, NT])
    stT = P.din('stT', [128, KC, NSEQ, 2])
    P.din('nwkv', [128, KC, 1584])
    qkg = P.din('qkg', [128, 4 * 64])
    csd = P.din('cs', [128, NT + 1, 64])
    ckw = P.din('ckw', [NSEQ, 512, 256])
    cvw = P.din('cvw', [NSEQ, 512, 256])
    yT = P.dout('yT', [128, KC, NCOL])
    cvoT = P.dout('cvoT', [128, KC, 2 + 2 * NSEQ])
    P.dout('o_kvp', [NT, 128, 1536])
    P.dout('o_kvs', [NSEQ * SW, 1536])
    o_kws = P.dout('o_kws', [NSEQ, 512, 256])
    o_vws = P.dout('o_vws', [NSEQ, 512, 256])
    P.dout('gato', [128, NT + 1, 48])
    with ExitStack() as stack:
        B = Builder(P, stack)
        tk = B.tk
        sb = B.sb
        B.cws = sb("cws", [128, 3 * KC], F32)
        B.hms = sb("hms", [128, NT], F32)
        B.sts = sb("sts", [128, KC, NSEQ, 2], F32)
        B.cvo = sb("cvo", [128, KC, 2 + 2 * NSEQ], F32)
        B.qkg = sb("qkg", [128, 4, 64], F32)
        B.cs = sb("cs", [128, NT + 1, 64], F32)
        B.gat = sb("gat", [128, NT + 1, 48], F32)
        B.st4 = sb("st4", [128, 8], F32)
        tk.op('pool', lambda e: e.memset(B.onesb[:, :], 1.0 / D), writes=[('onesb',)])
        tk.op('pool', lambda e: e.memset(B.epsb[:, :], EPS), writes=[('epsb',)])
        tk.op('sp', lambda e: e.dma_start(out=B.ngs[:, :], in_=ng[:, :]), writes=[('ngs',)], dma=True)
        tk.op('sp', lambda e: e.dma_start(out=B.cws[:, :], in_=cw[:, :]), writes=[('cws',)], dma=True)
        tk.op('sp', lambda e: e.dma_start(out=B.hms[:, :], in_=hm[:, :]), writes=[('hms',)], dma=True)
        tk.op('sp', lambda e: e.dma_start(out=B.sts[:, :, :, :], in_=stT[:, :, :, :]), writes=[('sts',)], dma=True)
        tk.op('sp', lambda e: e.dma_start(out=B.qkg[:, :, :].rearrange("p a d -> p (a d)"), in_=qkg[:, :]), writes=[('qkg',)], dma=True)
        tk.op('sp', lambda e: e.dma_start(out=B.cs[:, :, :], in_=csd[:, :, :]), writes=[('cs',)], dma=True)
        for sq_ in range(NSEQ):
            tk.op('pool', lambda e: e.dma_start(out=o_kws[sq_, 0:508, :], in_=ckw[sq_, 4:512, :]), writes=[('o_kws0', sq_)], dma=True)
            tk.op('pool', lambda e: e.dma_start(out=o_vws[sq_, 0:508, :], in_=cvw[sq_, 4:512, :]), writes=[('o_vws0', sq_)], dma=True)
        for ci, (a, b) in enumerate(CTS):
            tk.op('sp', lambda e: e.dma_start(out=B.resid[:, :, a:b], in_=xT[:, :, a:b]), writes=[('resid', ci)], dma=True)
        B.ffn(0, 0)
        tk.barrier()
        tk.op('pool', lambda e: e.memset(B.yb[:, 0:2], 0.0), writes=[('yb',)])
        B.conv()
        tk.barrier()
        B.ffn(0, 1)
        B.ple(0)
        tk.op('pool', lambda e: e.dma_start(out=cvoT[:, :, :], in_=B.cvo[:, :, :]), reads=[('cvo',)], writes=[('o_cvo',)], dma=True)
        if stage >= 3:
            B.ffn(1, 0)
            tk.barrier()
            B.nsa_proj()
            tk.barrier()
        tk.op('pool', lambda e: e.dma_start(out=P.outs['gato'][:, :, :], in_=B.gat[:, :, :]), reads=[('gat',)], writes=[('o_gat',)], dma=True)
        for ci, (a, b) in enumerate(CTS):
            tk.op('pool', lambda e: e.dma_start(out=yT[:, :, a:b], in_=B.resid[:, :, a:b]), reads=[('resid', ci)], writes=[('o_y', ci)], dma=True)
        tk.wait_all('pool')
    return P


def build2(phase=3, nk=NT, ngp=2, cstop=9):
    from contextlib import ExitStack
    P = Prog()
    nc = P.nc
    xT = P.din('resid2', [128, KC, NCOL])
    ng = P.din('ng', [128, 2 * 4 * KC])
    if phase >= 3:
        P.din('pT', [2, 128, 2, NCOL])
        P.din('wgu', [2, 2, 22, 128, KC * 256])
        P.din('wdn', [2, 2, 22, 128, 1024])
        P.din('wpg', [2, 8, 128, 1280])
    qkg = P.din('qkg', [128, 4 * 64])
    csd = P.din('cs', [128, NT + 1, 64])
    gat2 = P.din('gat2', [128, NT + 1, 48])
    P.din('nwq', [2, 128, KC, 512])
    P.din('nwo', [2, 128, 4, 1024])
    P.din('kts', [2, 128, 8192])
    P.din('vss', [128, 64, 256])
    P.din('ktw', [2, 128, 8192])
    P.din('vws', [128, 64, 256])
    P.din('kcr', [2, 128, 64, 256])
    P.din('w1r', [2, 128, 64 * 128])
    P.din('w2s', [128, 128])
    posr = P.din('posr', [128, 128])
    tabs = P.din('tabs', [128, 51])
    arow = P.din('arow', [128, 384])
    dm = P.din('dm', [128, 4 * 128], BF16)
    wm = P.din('wm', [128, 8 * 128], BF16)
    selc = P.din('selc', [128, 16 * 128], BF16)
    ident = P.din('ident', [128, 128])
    rmd = P.din('rmask', [128, 6])
    do_sample = phase >= 3 or phase == -1
    if do_sample:
        for nm_ in ('pool_kc', 'pool_vc', 'pool_ks', 'pool_vs'):
            P.din(nm_, [2560 * 128, 256])
        ptab = P.din('ptab', [128, 256], I32)
        pcol = P.din('pcol', [128, 1])
        P.din('ktn', [2, 2, 128, 24])
        P.din('vn', [4, NSEQ, 2, 256])
        P.din('ckwT', [2, 128, NSEQ * 512])
        P.din('cvwt', [128, NSEQ * 4, 256])
        P.din('gats', [4, NSEQ, 48])
        nmd = P.din('nm', [128, 16], BF16)
        wm0d = P.din('wm0', [128, 16], BF16)
    yT = P.dout('yT2', [128, KC, NCOL])
    with ExitStack() as stack:
        B = Builder(P, stack)
        tk = B.tk
        sb = B.sb
        B.qkg = sb("qkg", [128, 4, 64], F32)
        B.cs = sb("cs", [128, NT + 1, 64], F32)
        B.gat = sb("gat", [128, NT + 1, 48], F32)
        B.st4 = sb("st4", [128, 8], F32)
        B.posr = sb("posr", [128, 2, 64], F32)
        B.tabs = sb("tabs", [128, 3, 17], F32)
        B.arow = sb("arow", [128, 3, 128], F32)
        B.dm = sb("dm", [128, 4, 128], BF16)
        B.wm = sb("wm", [128, 8, 128], BF16)
        B.selc = sb("selc", [128, 16, 128], BF16)
        B.ident = sb("ident", [128, 128], F32)
        B.rmask = sb("rmask", [128, 6], F32)
        B.kcT = sb("kcT", [128, 2, 128], BF16)
        B.vcb = sb("vcb", [128, 256], BF16)
        B.do_sample = do_sample
        B.d_kc = nc.dram_tensor("d_kc", [5, 128, 256], BF16, kind="Internal")
        B.d_vc = nc.dram_tensor("d_vc", [5, 128, 256], BF16, kind="Internal")
        if do_sample:
            B.idx = sb("idx", [128, 256], I32)
            B.pcol = sb("pcol", [128, 1], F32)
            B.nm = sb("nm", [128, 16], BF16)
            B.wm0 = sb("wm0", [128, 16], BF16)
        if phase == 2:
            P.dout('d_o', [128, 1024])
        tk.op('pool', lambda e: e.memset(B.onesb[:, :], 1.0 / D), writes=[('onesb',)])
        tk.op('pool', lambda e: e.memset(B.epsb[:, :], EPS), writes=[('epsb',)])
        flat = lambda t: t[:, :, :].rearrange("p a b -> p (a b)")
        for (dst, src, key) in [(B.ngs[:, :], ng[:, :], 'ngs'), (flat(B.qkg), qkg[:, :], 'qkg'), (B.cs[:, :, :], csd[:, :, :], 'cs'),
                                (B.gat[:, :, :], gat2[:, :, :], 'gat'), (flat(B.posr), posr[:, :], 'posr'), (flat(B.tabs), tabs[:, :], 'tabs'),
                                (flat(B.arow), arow[:, :], 'arow'), (flat(B.dm), dm[:, :], 'dm'), (flat(B.wm), wm[:, :], 'wm'),
                                (flat(B.selc), selc[:, :], 'selc'), (B.ident[:, :], ident[:, :], 'ident'), (B.rmask[:, :], rmd[:, :], 'rmask')]:
            tk.op('sp', lambda e: e.dma_start(out=dst, in_=src), writes=[(key,)], dma=True)
        for ci, (a, b) in enumerate(CTS):
            tk.op('sp', lambda e: e.dma_start(out=B.resid[:, :, a:b], in_=xT[:, :, a:b]), writes=[('resid', ci)], dma=True)
        if do_sample:
            tk.op('sp', lambda e: e.dma_start(out=B.idx[:, :], in_=ptab[:, :]), writes=[('idx',)], dma=True)
            tk.op('sp', lambda e: e.dma_start(out=B.pcol[:, :], in_=pcol[:, :]), writes=[('pcol',)], dma=True)
            tk.op('sp', lambda e: e.dma_start(out=B.nm[:, :], in_=nmd[:, :]), writes=[('nm',)], dma=True)
            tk.op('sp', lambda e: e.dma_start(out=B.wm0[:, :], in_=wm0d[:, :]), writes=[('wm0',)], dma=True)
            idf = B.carve(0, [128, 256], F32)
            tk.op('dve', lambda e: e.tensor_copy(idf[:, :], B.idx[:, :]), reads=[('idx',)], writes=[('idf',)])
            tk.op('dve', lambda e: e.tensor_scalar(idf[:, :], idf[:, :], 128.0, B.pcol[:, 0:1], ALU.mult, ALU.add), reads=[('idf',), ('pcol',)], writes=[('idf',)])
            tk.op('dve', lambda e: e.tensor_copy(B.idx[:, :], idf[:, :]), reads=[('idf',)], writes=[('idx',)])
            tk.barrier()
        B.norm()
        tk.barrier()
        if phase >= 1 or do_sample:
            B.compress(cstop)
            if do_sample:
                for sq_ in range(NSEQ):
                    tk.barrier()
                    B.compress(cstop, 1 + sq_, sq_)
        if phase >= 2 or do_sample:
            for gp in range(ngp):
                tk.barrier()
                B.attention(gp, nk)
        tk.barrier()
        if phase == 1:
            dk = P.dout('d_kcT', [128, 256], BF16)
            dv = P.dout('d_vcb', [128, 256], BF16)
            tk.op('pool', lambda e: e.dma_start(out=dk[:, :], in_=B.kcT[:, :, :].rearrange("p a b -> p (a b)")), reads=[('kcT',)], writes=[('o_dk',)], dma=True)
            tk.op('pool', lambda e: e.dma_start(out=dv[:, :], in_=B.vcb[:, :]), reads=[('vcb',)], writes=[('o_dv',)], dma=True)
        if phase >= 3:
            B.ffn(1, 1)
            B.ple(1)
        for ci, (a, b) in enumerate(CTS):
            tk.op('pool', lambda e: e.dma_start(out=yT[:, :, a:b], in_=B.resid[:, :, a:b]), reads=[('resid', ci)], writes=[('o_y', ci)], dma=True)
        tk.wait_all('pool')
    return P


def tile_w(W, cols):
    din = W.shape[0]
    sub = W[:, cols]
    return np.ascontiguousarray(sub.reshape(din // 128, 128, -1).transpose(1, 0, 2).reshape(128, -1))


def fm(a):
    ncol, F = a.shape
    return np.ascontiguousarray(a.T.reshape(F // 128, 128, ncol).transpose(1, 0, 2))


def core_cols(r, xp, xs):
    b, c = r // 4, r % 4
    F = xp.shape[-1]
    out = np.zeros((NCOL, F), np.float32)
    for k in range(NT):
        i = 4 * k + c
        lo = 128 * i - 2
        if lo < 0:
            out[k * TW + 2:(k + 1) * TW] = xp[b, 0:128]
        else:
            out[k * TW:(k + 1) * TW] = xp[b, lo:lo + TW]
    for s in range(NSEQ):
        out[PCOL + s * SW + 2:PCOL + (s + 1) * SW] = xs[4 * r + s]
    return out


def prep_shared(inp):
    f = lambda k: np.asarray(inp[k], np.float32)
    sh = {}
    ngv = f('norm_g')
    sh['ng'] = np.ascontiguousarray(ngv.reshape(2, 4, KC, 128).transpose(3, 0, 1, 2).reshape(128, 64))
    wgu = f('ffn_w_gu')
    a = np.zeros((2, 2, 22, 128, KC * 256), np.float32)
    for l in range(2):
        for ff in range(2):
            for j in range(22):
                cols = list(range(128 * j, 128 * j + 128)) + list(range(DFF + 128 * j, DFF + 128 * j + 128))
                a[l, ff, j] = tile_w(wgu[l, ff], cols)
    sh['wgu'] = a
    sh['wdn'] = np.ascontiguousarray(f('ffn_w_down').reshape(2, 2, 22, 128, 1024))
    wg, wp = f('ple_w_gate'), f('ple_w_proj')
    a = np.zeros((2, 8, 128, 1280), np.float32)
    for l in range(2):
        for m in range(8):
            cols = list(range(128 * m, 128 * m + 128))
            a[l, m, :, 0:1024] = tile_w(wg[l], cols)
            a[l, m, :, 1024:1280] = tile_w(wp[l], cols)
    sh['wpg'] = a
    cwi = f('conv_w_in')[0]
    a = np.zeros((24, 128, 1024), np.float32)
    for ff in range(8):
        for kind, base in enumerate([1024, 2048, 0]):
            a[3 * ff + kind] = tile_w(cwi, list(range(base + 128 * ff, base + 128 * ff + 128)))
    sh['cwin'] = a
    sh['cw'] = np.ascontiguousarray(f('conv_w')[0].reshape(3, KC, 128).transpose(2, 0, 1).reshape(128, 24))
    cwo = f('conv_w_out')[0]
    sh['cwout'] = np.stack([tile_w(cwo, list(range(128 * m, 128 * m + 128))) for m in range(8)])
    return sh


def prep_core(r, inp, sh):
    f = lambda k: np.asarray(inp[k], np.float32)
    m = dict(sh)
    m['xT'] = fm(core_cols(r, f('x_prompt'), f('x_sample')))
    pp, psm = f('p_prompt'), f('p_sample')
    m['pT'] = np.stack([fm(core_cols(r, pp[l], psm[l])) for l in range(2)])
    hmv = np.ones((128, NT), np.float32)
    if r % 4 == 0:
        hmv[:, 0] = 0.0
    m['hm'] = hmv
    st = f('state_conv')[0][4 * r:4 * r + 4]
    m['stT'] = np.ascontiguousarray(st.reshape(NSEQ, 2, KC, 128).transpose(3, 2, 0, 1))
    return m


def rope_tab(pos):
    half = 32
    inv = (np.float32(10000.0) ** (-(np.arange(half, dtype=np.float32) / np.float32(half)))).astype(np.float32)
    ang = (pos.astype(np.float32)[:, None] * inv[None, :]).astype(np.float32)
    return np.concatenate([np.cos(ang), np.sin(ang)], axis=1).astype(np.float32)


def prep_shared2(inp, sh):
    f = lambda k: np.asarray(inp[k], np.float32)
    win = f('nsa_w_in')[0]
    sh['nwkv'] = np.ascontiguousarray(win[:, 1024:2608].reshape(KC, 128, 1584).transpose(1, 0, 2))
    sh['qkg'] = np.ascontiguousarray(np.broadcast_to(f('nsa_qk_g')[0].reshape(1, 256), (128, 256)))
    return sh


def prep_core2(r, inp, m):
    f = lambda k: np.asarray(inp[k], np.float32)
    c = r % 4
    cs = np.zeros((128, NT + 1, 64), np.float32)
    for k in range(NT):
        cs[:, k, :] = rope_tab(128 * (4 * k + c) + np.arange(128))
    spos = np.zeros(128, np.int64)
    for s in range(NSEQ):
        spos[s * SW + 2:s * SW + 6] = 8192 + np.arange(4)
    cs[:, NT, :] = rope_tab(spos)
    m['cs'] = cs
    m['ckw'] = np.ascontiguousarray(f('cache_k_win')[0, 4 * r:4 * r + 4].reshape(NSEQ, 512, 256))
    m['cvw'] = np.ascontiguousarray(f('cache_v_win')[0, 4 * r:4 * r + 4].reshape(NSEQ, 512, 256))
    return m


_CACHE = {}


def prep2_shared(inp, sh):
    f = lambda k: np.asarray(inp[k], np.float32)
    o = {k: sh[k] for k in ('ng', 'wgu', 'wdn', 'wpg', 'qkg')}
    win = f('nsa_w_in')[0]
    nwq = np.zeros((2, 128, KC, 512), np.float32)
    for gp in range(2):
        cols = []
        for j in range(4):
            for gg in range(2):
                h = 4 * (2 * gp + gg) + j
                cols += list(range(64 * h, 64 * h + 64))
        nwq[gp] = tile_w(win, cols).reshape(128, KC, 512)
    o['nwq'] = nwq
    wo = f('nsa_w_out')[0]
    o['nwo'] = np.ascontiguousarray(wo.reshape(2, 4, 128, 1024).transpose(0, 2, 1, 3))
    w1 = f('nsa_cmp_w1')[0]
    w1r = w1.reshape(2, 1, 64, 64 * 128)
    o['w1r'] = np.ascontiguousarray(np.broadcast_to(w1r, (2, 2, 64, 64 * 128)).reshape(2, 128, 64 * 128))
    w2 = f('nsa_cmp_w2')[0]
    o['w2s'] = np.ascontiguousarray(w2.transpose(1, 0, 2).reshape(128, 128))
    pos = f('nsa_cmp_pos')[0]
    pr = pos.transpose(1, 0, 2).reshape(1, 64, 128)
    o['posr'] = np.ascontiguousarray(np.broadcast_to(pr, (2, 64, 128)).reshape(128, 128))
    n = np.arange(128, dtype=np.float32)
    ar = np.stack([64 * n + 63, n, (n == 0).astype(np.float32)], 0).reshape(1, 384)
    o['arow'] = np.ascontiguousarray(np.broadcast_to(ar, (128, 384))).astype(np.float32)
    sel = np.zeros((128, 16, 128), np.float32)
    for a in range(4):
        for m in range(32):
            for kk in range(16):
                if m // 2 == kk:
                    e = m % 2
                    sel[32 * a + m, kk, 64 * e:64 * e + 64] = 1.0
    o['selc'] = sel.reshape(128, 2048).astype(ml_dtypes.bfloat16)
    pp = np.arange(128)
    o['rmask'] = np.stack([(pp < 64), (pp >= 64), (pp // 32 == 0), (pp // 32 == 1), (pp // 32 == 2), (pp // 32 == 3)], 1).astype(np.float32)
    o['ident'] = np.eye(128, dtype=np.float32)
    o['_inp'] = inp
    o['_pools'] = {nm_: np.ascontiguousarray(np.asarray(inp[key_], np.float32)[0].reshape(2560 * 128, 256))
                   for nm_, key_ in (('pool_kc', 'cache_k_cmp'), ('pool_vc', 'cache_v_cmp'), ('pool_ks', 'cache_k_sel'), ('pool_vs', 'cache_v_sel'))}
    return o


def prep2_core(r, o, m1, res1, full):
    b, c = r // 4, r % 4
    m = dict(o)
    m['resid2'] = np.asarray(res1[r]['yT'])
    m['gat2'] = np.asarray(res1[r]['gato'])
    m['pT'] = m1['pT']
    m['cs'] = m1['cs']
    m.update(full[b])
    rr = np.arange(128, dtype=np.float32)
    tabs = np.zeros((128, 3, 17), np.float32)
    for k in range(NT):
        qp = 128 * (4 * k + c) + rr
        tabs[:, 0, k] = qp
        tabs[:, 1, k] = np.floor(qp / 64)
        tabs[:, 2, k] = np.floor(qp / 64) - 1
    tabs[:, 0, 16] = 8192 + rr
    tabs[:, 1, 16] = 128
    tabs[:, 2, 16] = 127
    m['tabs'] = tabs.reshape(128, 51)
    key = np.arange(128)[:, None]
    q = np.arange(128)[None, :]
    dmv = np.zeros((128, 4, 128), np.float32)
    for rp in range(4):
        if rp < c:
            dmv[:, rp, :] = 1.0
        elif rp == c:
            dmv[:, rp, :] = (key <= q)
    m['dm'] = dmv.reshape(128, 512).astype(ml_dtypes.bfloat16)
    wmv = np.zeros((128, 8, 128), np.float32)
    for mm in range(8):
        diff = 128 * (c + 4 - mm) + (q - key)
        wmv[:, mm, :] = (diff >= 0) & (diff < 512)
    m['wm'] = wmv.reshape(128, 1024).astype(ml_dtypes.bfloat16)
    inp = o['_inp']
    for nm_, key_ in (('pool_kc', 'cache_k_cmp'), ('pool_vc', 'cache_v_cmp'), ('pool_ks', 'cache_k_sel'), ('pool_vs', 'cache_v_sel')):
        m[nm_] = o['_pools'][nm_]
    pt = np.asarray(inp['page_table'], np.int32)[4 * r:4 * r + 4].reshape(1, 256)
    m['ptab'] = np.ascontiguousarray(np.broadcast_to(pt, (128, 256))).astype(np.int32)
    m['pcol'] = np.arange(128, dtype=np.float32).reshape(128, 1)
    oks = np.asarray(res1[r]['o_kvs'])
    ktn = np.zeros((2, 2, 128, 24), np.float32)
    vnv = np.zeros((4, NSEQ, 2, 256), np.float32)
    for kind, (kc0, vc0) in enumerate([(512, 768), (1024, 1280)]):
        kk_ = oks[:, kc0:kc0 + 256]
        for gp in range(2):
            ktn[kind, gp] = kk_[:, gp * 128:(gp + 1) * 128].T
        for s_ in range(NSEQ):
            vnv[:, s_, kind, :] = oks[s_ * SW + 2:s_ * SW + 6, vc0:vc0 + 256]
    m['ktn'] = ktn
    m['vn'] = vnv
    ckw = np.asarray(inp['cache_k_win'], np.float32)[0, 4 * r:4 * r + 4].reshape(NSEQ, 512, 256)
    cvw = np.asarray(inp['cache_v_win'], np.float32)[0, 4 * r:4 * r + 4].reshape(NSEQ, 512, 256)
    m['ckwT'] = np.ascontiguousarray(ckw.transpose(2, 0, 1).reshape(2, 128, NSEQ * 512))
    m['cvwt'] = np.ascontiguousarray(cvw.reshape(NSEQ, 4, 128, 256).transpose(2, 0, 1, 3).reshape(128, NSEQ * 4, 256))
    g1 = np.asarray(res1[r]['gato'])[:, NT, :]
    gs_ = np.zeros((4, NSEQ, 48), np.float32)
    for s_ in range(NSEQ):
        gs_[:, s_, :] = g1[s_ * SW + 2:s_ * SW + 6]
    m['gats'] = gs_
    kq = np.arange(128)[:, None]
    jq = np.arange(16)[None, :] % 4
    m['nm'] = ((kq <= jq) & (kq < 4)).astype(np.float32).astype(ml_dtypes.bfloat16)
    m['wm0'] = (kq >= jq + 1).astype(np.float32).astype(ml_dtypes.bfloat16)
    return m


def kernel(**inp):
    if 'P' not in _CACHE:
        _CACHE['P'] = build()
        _CACHE['P2'] = build2()
    P, P2 = _CACHE['P'], _CACHE['P2']
    sh = prep_shared2(inp, prep_shared(inp))
    maps = []
    full_maps = []
    for r in range(NCORE):
        m = prep_core2(r, inp, prep_core(r, inp, sh))
        full_maps.append(m)
        maps.append({k: v for k, v in m.items() if k in P.ins})
    res = run_bass_kernel_spmd(P.nc, maps, core_ids=list(range(NCORE)))
    R = res.results
    B_, S_ = 2, 8192
    y_p = np.zeros((B_, S_, D), np.float32)
    y_s = np.zeros((32, 4, D), np.float32)
    conv_p = np.zeros((1, B_, 2, D), np.float32)
    conv_s = np.zeros((1, 32, 2, D), np.float32)
    kvp = [np.zeros((1, B_, S_, 4, 64), np.float32) for _ in range(6)]
    kwp = [np.zeros((1, B_, 512, 4, 64), np.float32) for _ in range(2)]
    kvs = [np.zeros((1, 32, 4, 4, 64), np.float32) for _ in range(4)]
    kws = [np.zeros((1, 32, 512, 4, 64), np.float32) for _ in range(2)]
    for r in range(NCORE):
        b, c = r // 4, r % 4
        cv = np.asarray(R[r]['cvoT']).transpose(2, 1, 0).reshape(2 + 2 * NSEQ, D)
        okp = np.asarray(R[r]['o_kvp'])
        oks = np.asarray(R[r]['o_kvs'])
        for k in range(NT):
            i = 4 * k + c
            for j in range(6):
                kvp[j][0, b, 128 * i:128 * i + 128] = okp[k, :, 256 * j:256 * j + 256].reshape(128, 4, 64)
        if c == 3:
            conv_p[0, b] = cv[0:2]
        for s in range(NSEQ):
            sg = 4 * r + s
            conv_s[0, sg] = cv[2 + 2 * s:4 + 2 * s]
            for j in range(4):
                kvs[j][0, sg] = oks[s * SW + 2:s * SW + 6, 256 * j:256 * j + 256].reshape(4, 4, 64)
            kws[0][0, sg] = np.asarray(R[r]['o_kws'])[s].reshape(512, 4, 64)
            kws[1][0, sg] = np.asarray(R[r]['o_vws'])[s].reshape(512, 4, 64)
    for j in range(2):
        kwp[j][0] = kvp[4 + j][0][:, S_ - 512:]
    full = []
    for b in range(B_):
        d = {}
        d['kcr'] = np.ascontiguousarray(np.stack([kvp[0][0, b].reshape(64, 128, 256), kvp[1][0, b].reshape(64, 128, 256)]).transpose(0, 2, 1, 3))
        ks = kvp[2][0, b].reshape(S_, 256)
        d['kts'] = np.ascontiguousarray(ks.T.reshape(2, 128, S_))
        d['vss'] = np.ascontiguousarray(kvp[3][0, b].reshape(64, 128, 256).transpose(1, 0, 2))
        kw = kvp[4][0, b].reshape(S_, 256)
        d['ktw'] = np.ascontiguousarray(kw.T.reshape(2, 128, S_))
        d['vws'] = np.ascontiguousarray(kvp[5][0, b].reshape(64, 128, 256).transpose(1, 0, 2))
        full.append(d)
    o2 = prep2_shared(inp, sh)
    maps2 = []
    for r in range(NCORE):
        m = prep2_core(r, o2, full_maps[r], R, full)
        maps2.append({k: v for k, v in m.items() if k in P2.ins})
    if _CACHE.get('hook') is not None:
        return _CACHE['hook'](maps2)
    res2 = run_bass_kernel_spmd(P2.nc, maps2, core_ids=list(range(NCORE)))
    R2 = res2.results
    for r in range(NCORE):
        b, c = r // 4, r % 4
        yt = np.asarray(R2[r]['yT2']).transpose(2, 1, 0).reshape(NCOL, D)
        for k in range(NT):
            i = 4 * k + c
            y_p[b, 128 * i:128 * i + 128] = yt[k * TW + 2:(k + 1) * TW]
        for s in range(NSEQ):
            y_s[4 * r + s] = yt[PCOL + s * SW + 2:PCOL + (s + 1) * SW]
    return (y_p, y_s, conv_p, conv_s, kvp[0], kvp[1], kvp[2], kvp[3], kwp[0], kwp[1],
            kvs[0], kvs[1], kvs[2], kvs[3], kws[0], kws[1])
```
